# Optimizing a Trainium2 kernel written in Bass

```python
import math
import jax, jax.numpy as jnp
from jax import lax
import numpy as np

D_MODEL = 2048
BATCH = 4
SEQ = 2048
DEPTH = 1
DEC_BATCH = 128
DEC_SEQ = 8
PAST_LEN = 2048
PAGE_SIZE = 128

N_MEM = 256
FOX_HEADS = 8
FOX_HEAD_DIM = 128
FOX_WIDTH = FOX_HEADS * FOX_HEAD_DIM
Q_BLOCK = 128
SSD_HEADS = 16
SSD_HEAD_DIM = 64
SSD_WIDTH = SSD_HEADS * SSD_HEAD_DIM
SSD_GROUPS = 2
SSD_HEADS_PER_GROUP = SSD_HEADS // SSD_GROUPS
SSD_STATE = 128
SSD_CHUNK = 128
CONV_WIDTH = 4
CONV_DIM = SSD_WIDTH + 2 * SSD_GROUPS * SSD_STATE
MIX_WIDTH = FOX_WIDTH + SSD_WIDTH
XATTN_HEADS = 4
XATTN_HEAD_DIM = 128
XATTN_WIDTH = XATTN_HEADS * XATTN_HEAD_DIM
D_FF = 5632
FFN_RESIDUAL = 0.5
EPS = 1e-6
IN_SPLITS = (FOX_WIDTH, 2 * FOX_WIDTH, 3 * FOX_WIDTH, 3 * FOX_WIDTH + FOX_HEADS,
             3 * FOX_WIDTH + FOX_HEADS + SSD_WIDTH,
             3 * FOX_WIDTH + FOX_HEADS + SSD_WIDTH + CONV_DIM)
IN_PROJ_WIDTH = IN_SPLITS[-1] + SSD_HEADS

kernel_name = "fox_ssd_parallel_heads_macaron_decode_step"


def rmsnorm(x, g):
    xf = x.astype(jnp.float32)
    xf = xf * lax.rsqrt(jnp.mean(xf * xf, axis=-1, keepdims=True) + EPS)
    return (xf * g.astype(jnp.float32)).astype(x.dtype)


def swiglu(x, w_gate, w_up, w_down):
    return (jax.nn.silu(x @ w_gate) * (x @ w_up)) @ w_down


def fox_block(q, c_q, pos_q, k, v, c_k, pos_k):
    logits = jnp.einsum("blhd,bshd->bhls", q, k, preferred_element_type=jnp.float32) * (FOX_HEAD_DIM ** -0.5)
    bias = jnp.swapaxes(c_q, 1, 2)[:, :, :, None] - jnp.swapaxes(c_k, 1, 2)[:, :, None, :]
    causal = pos_k[None, :] <= pos_q[:, None]
    probs = jax.nn.softmax(jnp.where(causal, logits + bias, -jnp.inf), axis=-1)
    return jnp.einsum("bhls,bshd->blhd", probs.astype(v.dtype), v)


def fox_attention(q, c_q, pos_q, k, v, c_k, pos_k):
    b, L, h, d = q.shape
    blk = Q_BLOCK if L % Q_BLOCK == 0 else L
    nb = L // blk
    qb = jnp.swapaxes(q.reshape(b, nb, blk, h, d), 0, 1)
    cb = jnp.swapaxes(c_q.reshape(b, nb, blk, h), 0, 1)
    pb = pos_q.reshape(nb, blk)
    out = lax.map(lambda a: fox_block(a[0], a[1], a[2], k, v, c_k, pos_k), (qb, cb, pb))
    return jnp.swapaxes(out, 0, 1).reshape(b, L, h, d)


def causal_conv(xbc, buf, w, bias):
    L = xbc.shape[1]
    xp = jnp.concatenate([buf.astype(xbc.dtype), xbc], axis=1)
    y = bias
    for j in range(CONV_WIDTH):
        y = y + xp[:, j:j + L] * w[j]
    return jax.nn.silu(y), xp[:, L:]


def ssd_scan(x, dt, a_head, bm, cm, h0):
    b, L = x.shape[:2]
    chunk = SSD_CHUNK if L % SSD_CHUNK == 0 else L
    nc = L // chunk
    G, E, P, N = SSD_GROUPS, SSD_HEADS_PER_GROUP, SSD_HEAD_DIM, SSD_STATE
    xdt = x.reshape(b, nc, chunk, G, E, P) * dt.reshape(b, nc, chunk, G, E, 1)
    bm = bm.reshape(b, nc, chunk, G, N)
    cm = cm.reshape(b, nc, chunk, G, N)
    a_cum = jnp.cumsum(dt.reshape(b, nc, chunk, G, E) * a_head.reshape(G, E), axis=2)
    seg = a_cum[:, :, :, None] - a_cum[:, :, None, :]
    causal = jnp.tril(jnp.ones((chunk, chunk), dtype=bool))[:, :, None, None]
    decay = jnp.exp(jnp.where(causal, seg, -jnp.inf))
    cb = jnp.einsum("bclgn,bcsgn->bclsg", cm, bm, preferred_element_type=jnp.float32)
    y_diag = jnp.einsum("bclsg,bclsge,bcsgep->bclgep", cb, decay, xdt)
    decay_to_end = jnp.exp(a_cum[:, :, -1:] - a_cum)
    chunk_states = jnp.einsum("bclgn,bclge,bclgep->bcgepn", bm, decay_to_end, xdt)
    chunk_decay = jnp.exp(a_cum[:, :, -1])

    def step(h, inp):
        s_c, d_c = inp
        return h * d_c[..., None, None] + s_c, h

    h_last, h_prev = lax.scan(step, h0.reshape(b, G, E, P, N).astype(jnp.float32),
                              (jnp.swapaxes(chunk_states, 0, 1), jnp.swapaxes(chunk_decay, 0, 1)))
    h_prev = jnp.swapaxes(h_prev, 0, 1)
    y_off = jnp.einsum("bclgn,bcgepn,bclge->bclgep", cm, h_prev, jnp.exp(a_cum))
    return (y_diag + y_off).reshape(b, L, SSD_HEADS, P), h_last.reshape(b, SSD_HEADS, P, N)


def memory_kv(mem, p):
    b, m, _ = mem.shape
    kv = rmsnorm(mem, p["mem_norm"]) @ p["xattn_w_kv"]
    k, v = jnp.split(kv, [XATTN_WIDTH], axis=-1)
    k = rmsnorm(k.reshape(b, m, XATTN_HEADS, XATTN_HEAD_DIM), p["xattn_k_norm"])
    return k, v.reshape(b, m, XATTN_HEADS, XATTN_HEAD_DIM)


def cross_attention(u, mem_k, mem_v, p):
    b, L, _ = u.shape
    q = rmsnorm((u @ p["xattn_w_q"]).reshape(b, L, XATTN_HEADS, XATTN_HEAD_DIM), p["xattn_q_norm"])
    logits = jnp.einsum("blhd,bmhd->bhlm", q, mem_k, preferred_element_type=jnp.float32) * (XATTN_HEAD_DIM ** -0.5)
    probs = jax.nn.softmax(logits, axis=-1)
    o = jnp.einsum("bhlm,bmhd->blhd", probs.astype(mem_v.dtype), mem_v).reshape(b, L, XATTN_WIDTH)
    return o @ p["xattn_w_o"]


def hybrid_layer(x, p, fox_past, ssm_h0, conv_buf, mem_k, mem_v):
    b, L, _ = x.shape
    x = x + FFN_RESIDUAL * swiglu(rmsnorm(x, p["ffn1_norm"]), p["ffn1_w_gate"], p["ffn1_w_up"], p["ffn1_w_down"])
    u = rmsnorm(x, p["mix_norm"])
    q, k, v, f_logit, z, xbc, dt_raw = jnp.split(u @ p["w_in"], IN_SPLITS, axis=-1)
    q = rmsnorm(q.reshape(b, L, FOX_HEADS, FOX_HEAD_DIM), p["fox_q_norm"])
    k = rmsnorm(k.reshape(b, L, FOX_HEADS, FOX_HEAD_DIM), p["fox_k_norm"])
    v = v.reshape(b, L, FOX_HEADS, FOX_HEAD_DIM)
    logf = jax.nn.log_sigmoid(f_logit.astype(jnp.float32) + p["fox_b_f"].astype(jnp.float32))
    if fox_past is None:
        past = 0
        k_all, v_all, logf_all = k, v, logf
    else:
        k_past, v_past, logf_past = fox_past
        past = k_past.shape[1]
        k_all = jnp.concatenate([k_past.astype(k.dtype), k], axis=1)
        v_all = jnp.concatenate([v_past.astype(v.dtype), v], axis=1)
        logf_all = jnp.concatenate([logf_past.astype(jnp.float32), logf], axis=1)
    c_all = jnp.cumsum(logf_all, axis=1)
    pos_k = jnp.arange(past + L)
    fox_out = fox_attention(q, c_all[:, past:], pos_k[past:], k_all, v_all, c_all, pos_k)
    xbc, conv_new = causal_conv(xbc, conv_buf, p["conv_w"], p["conv_b"])
    xs, bm, cm = jnp.split(xbc, [SSD_WIDTH, SSD_WIDTH + SSD_GROUPS * SSD_STATE], axis=-1)
    xs = xs.reshape(b, L, SSD_HEADS, SSD_HEAD_DIM)
    bm = bm.reshape(b, L, SSD_GROUPS, SSD_STATE)
    cm = cm.reshape(b, L, SSD_GROUPS, SSD_STATE)
    dt = jax.nn.softplus(dt_raw.astype(jnp.float32) + p["ssd_dt_bias"].astype(jnp.float32))
    a_head = -jnp.exp(p["ssd_A_log"].astype(jnp.float32))
    y_ssd, h_new = ssd_scan(xs, dt, a_head, bm, cm, ssm_h0)
    y_ssd = y_ssd + xs * p["ssd_D"][:, None]
    y_ssd = rmsnorm(y_ssd.reshape(b, L, SSD_WIDTH) * jax.nn.silu(z), p["ssd_out_norm"])
    mixed = jnp.concatenate([fox_out.reshape(b, L, FOX_WIDTH), y_ssd.astype(fox_out.dtype)], axis=-1) @ p["w_out"]
    x = x + mixed
    x = x + cross_attention(rmsnorm(x, p["xattn_norm"]), mem_k, mem_v, p)
    x = x + FFN_RESIDUAL * swiglu(rmsnorm(x, p["ffn2_norm"]), p["ffn2_w_gate"], p["ffn2_w_up"], p["ffn2_w_down"])
    return x, (k, v, logf, h_new, conv_new)


def setup_inputs(seed: int = 0) -> dict:
    key = jax.random.key(seed)
    ks = iter(jax.random.split(key, 64))
    f32 = jnp.float32

    def nrm(shape, scale=1.0):
        return scale * jax.random.normal(next(ks), shape, f32)

    def gain(shape):
        return 1.0 + 0.05 * jax.random.normal(next(ks), shape, f32)

    n_pages = PAST_LEN // PAGE_SIZE
    n_used = DEC_BATCH * n_pages
    n_pool = n_used + n_used // 4
    page_table = jax.random.permutation(next(ks), n_pool)[:n_used].reshape(DEC_BATCH, n_pages).astype(jnp.int32)
    dt_init = jnp.exp(jax.random.uniform(next(ks), (DEPTH, SSD_HEADS), f32, math.log(1e-3), math.log(1e-1)))
    a_init = jax.random.uniform(next(ks), (DEPTH, SSD_HEADS), f32, 1.0, 16.0)
    return {
        "x_prompt": nrm((BATCH, SEQ, D_MODEL)),
        "x_sample": nrm((DEC_BATCH, DEC_SEQ, D_MODEL)),
        "cache_fox_k": nrm((DEPTH, n_pool, PAGE_SIZE, FOX_HEADS, FOX_HEAD_DIM)),
        "cache_fox_v": nrm((DEPTH, n_pool, PAGE_SIZE, FOX_HEADS, FOX_HEAD_DIM)),
        "cache_fox_logf": jax.nn.log_sigmoid(3.0 + nrm((DEPTH, n_pool, PAGE_SIZE, FOX_HEADS), 0.5)),
        "cache_mem_k": nrm((DEPTH, DEC_BATCH, N_MEM, XATTN_HEADS, XATTN_HEAD_DIM)),
        "cache_mem_v": nrm((DEPTH, DEC_BATCH, N_MEM, XATTN_HEADS, XATTN_HEAD_DIM)),
        "state_ssm": nrm((DEPTH, DEC_BATCH, SSD_HEADS, SSD_HEAD_DIM, SSD_STATE), 0.1),
        "state_conv": nrm((DEPTH, DEC_BATCH, CONV_WIDTH - 1, CONV_DIM)),
        "page_table": page_table,
        "mem_prompt": nrm((BATCH, N_MEM, D_MODEL)),
        "ffn1_norm": gain((DEPTH, D_MODEL)),
        "ffn1_w_gate": nrm((DEPTH, D_MODEL, D_FF), D_MODEL ** -0.5),
        "ffn1_w_up": nrm((DEPTH, D_MODEL, D_FF), D_MODEL ** -0.5),
        "ffn1_w_down": nrm((DEPTH, D_FF, D_MODEL), D_FF ** -0.5),
        "mix_norm": gain((DEPTH, D_MODEL)),
        "w_in": nrm((DEPTH, D_MODEL, IN_PROJ_WIDTH), D_MODEL ** -0.5),
        "fox_b_f": 3.0 + nrm((DEPTH, FOX_HEADS), 0.5),
        "fox_q_norm": gain((DEPTH, FOX_HEAD_DIM)),
        "fox_k_norm": gain((DEPTH, FOX_HEAD_DIM)),
        "conv_w": nrm((DEPTH, CONV_WIDTH, CONV_DIM), CONV_WIDTH ** -0.5),
        "conv_b": nrm((DEPTH, CONV_DIM), 0.02),
        "ssd_dt_bias": dt_init + jnp.log(-jnp.expm1(-dt_init)),
        "ssd_A_log": jnp.log(a_init),
        "ssd_D": gain((DEPTH, SSD_HEADS)),
        "ssd_out_norm": gain((DEPTH, SSD_WIDTH)),
        "w_out": nrm((DEPTH, MIX_WIDTH, D_MODEL), MIX_WIDTH ** -0.5),
        "xattn_norm": gain((DEPTH, D_MODEL)),
        "mem_norm": gain((DEPTH, D_MODEL)),
        "xattn_w_q": nrm((DEPTH, D_MODEL, XATTN_WIDTH), D_MODEL ** -0.5),
        "xattn_w_kv": nrm((DEPTH, D_MODEL, 2 * XATTN_WIDTH), D_MODEL ** -0.5),
        "xattn_q_norm": gain((DEPTH, XATTN_HEAD_DIM)),
        "xattn_k_norm": gain((DEPTH, XATTN_HEAD_DIM)),
        "xattn_w_o": nrm((DEPTH, XATTN_WIDTH, D_MODEL), XATTN_WIDTH ** -0.5),
        "ffn2_norm": gain((DEPTH, D_MODEL)),
        "ffn2_w_gate": nrm((DEPTH, D_MODEL, D_FF), D_MODEL ** -0.5),
        "ffn2_w_up": nrm((DEPTH, D_MODEL, D_FF), D_MODEL ** -0.5),
        "ffn2_w_down": nrm((DEPTH, D_FF, D_MODEL), D_FF ** -0.5),
    }


def reference(x_prompt, x_sample, cache_fox_k, cache_fox_v, cache_fox_logf, cache_mem_k, cache_mem_v,
              state_ssm, state_conv, page_table, mem_prompt,
              ffn1_norm, ffn1_w_gate, ffn1_w_up, ffn1_w_down, mix_norm, w_in, fox_b_f, fox_q_norm, fox_k_norm,
              conv_w, conv_b, ssd_dt_bias, ssd_A_log, ssd_D, ssd_out_norm, w_out,
              xattn_norm, mem_norm, xattn_w_q, xattn_w_kv, xattn_q_norm, xattn_k_norm, xattn_w_o,
              ffn2_norm, ffn2_w_gate, ffn2_w_up, ffn2_w_down):
    bp = x_prompt.shape[0]
    bs = x_sample.shape[0]
    past_len = page_table.shape[1] * PAGE_SIZE
    hp, hs = x_prompt, x_sample
    p_list, s_list, pm_list = [], [], []
    for l in range(DEPTH):
        p = {
            "ffn1_norm": ffn1_norm[l], "ffn1_w_gate": ffn1_w_gate[l], "ffn1_w_up": ffn1_w_up[l],
            "ffn1_w_down": ffn1_w_down[l], "mix_norm": mix_norm[l], "w_in": w_in[l], "fox_b_f": fox_b_f[l],
            "fox_q_norm": fox_q_norm[l], "fox_k_norm": fox_k_norm[l], "conv_w": conv_w[l], "conv_b": conv_b[l],
            "ssd_dt_bias": ssd_dt_bias[l], "ssd_A_log": ssd_A_log[l], "ssd_D": ssd_D[l],
            "ssd_out_norm": ssd_out_norm[l], "w_out": w_out[l], "xattn_norm": xattn_norm[l],
            "mem_norm": mem_norm[l], "xattn_w_q": xattn_w_q[l], "xattn_w_kv": xattn_w_kv[l],
            "xattn_q_norm": xattn_q_norm[l], "xattn_k_norm": xattn_k_norm[l], "xattn_w_o": xattn_w_o[l],
            "ffn2_norm": ffn2_norm[l], "ffn2_w_gate": ffn2_w_gate[l], "ffn2_w_up": ffn2_w_up[l],
            "ffn2_w_down": ffn2_w_down[l],
        }
        mk, mv = memory_kv(mem_prompt, p)
        h0 = jnp.zeros((bp, SSD_HEADS, SSD_HEAD_DIM, SSD_STATE), jnp.float32)
        cb0 = jnp.zeros((bp, CONV_WIDTH - 1, CONV_DIM), x_prompt.dtype)
        hp, st_p = hybrid_layer(hp, p, None, h0, cb0, mk, mv)
        p_list.append(st_p)
        pm_list.append((mk, mv))
        k_past = cache_fox_k[l][page_table].reshape(bs, past_len, FOX_HEADS, FOX_HEAD_DIM)
        v_past = cache_fox_v[l][page_table].reshape(bs, past_len, FOX_HEADS, FOX_HEAD_DIM)
        lf_past = cache_fox_logf[l][page_table].reshape(bs, past_len, FOX_HEADS)
        hs, st_s = hybrid_layer(hs, p, (k_past, v_past, lf_past), state_ssm[l], state_conv[l],
                                cache_mem_k[l], cache_mem_v[l])
        s_list.append(st_s)
    p_fox_k = jnp.stack([s[0] for s in p_list])
    p_fox_v = jnp.stack([s[1] for s in p_list])
    p_fox_logf = jnp.stack([s[2] for s in p_list])
    p_ssm = jnp.stack([s[3] for s in p_list])
    p_conv = jnp.stack([s[4] for s in p_list])
    p_mem_k = jnp.stack([m[0] for m in pm_list])
    p_mem_v = jnp.stack([m[1] for m in pm_list])
    s_fox_k = jnp.stack([s[0] for s in s_list])
    s_fox_v = jnp.stack([s[1] for s in s_list])
    s_fox_logf = jnp.stack([s[2] for s in s_list])
    s_ssm = jnp.stack([s[3] for s in s_list])
    s_conv = jnp.stack([s[4] for s in s_list])
    return (hp, hs, p_fox_k, p_fox_v, p_fox_logf, p_ssm, p_conv, p_mem_k, p_mem_v,
            s_fox_k, s_fox_v, s_fox_logf, s_ssm, s_conv)
```

```python
import numpy as np
import concourse.bass as bass
import concourse.mybir as mybir
from concourse.bass_utils import run_bass_kernel_spmd

F32, BF16, I32 = mybir.dt.float32, mybir.dt.bfloat16, mybir.dt.int32
AF = mybir.ActivationFunctionType
ALU = mybir.AluOpType
AX = mybir.AxisListType

D = 2048
DFF = 5632
NF = DFF // 128
EPS = 1e-6
NCORES = 8
INW = 5656
NEG = -30000.0


class Buf:
    def __init__(self, t, psum=False):
        self.t = t
        self.psum = psum
        self.lw = None
        self.rd = {}
        self.sem = None
        self.dcnt = 0
        self.dwcnt = 0

    def __getitem__(self, k):
        return self.t[k]


class Sched:
    ENG = ("pe", "act", "dve", "pool", "sp")

    def __init__(self, nc):
        self.nc = nc
        self.e = {"pe": nc.tensor, "act": nc.scalar, "dve": nc.vector, "pool": nc.gpsimd, "sp": nc.sync}
        self.cnt = {e: 0 for e in self.ENG}
        self.sem = {e: nc.alloc_semaphore(name="sem_" + e) for e in self.ENG}
        self.known = {e: {f: 0 for f in self.ENG} for e in self.ENG}
        self.kdma = {e: {} for e in self.ENG}
        self.dbufs = []
        self.nsb = 0
        self.ddep = {}

    def sb(self, shape, dt=F32, name=None):
        self.nsb += 1
        return Buf(self.nc.alloc_sbuf_tensor(name or f"sb{self.nsb}", list(shape), dt))

    def _wait(self, e, f, idx, raw=False, force=False):
        if f == e and not force:
            if e in ("pe", "sp") or not raw:
                return
        if self.known[e][f] >= idx:
            return
        self.known[e][f] = idx
        self.e[e].wait_ge(self.sem[f], idx)

    def _wait_dma(self, e, b, cnt):
        if cnt == 0 or self.kdma[e].get(id(b), 0) >= cnt:
            return
        self.kdma[e][id(b)] = cnt
        self.e[e].wait_ge(b.sem, cnt)

    def op(self, e, fn, reads=(), writes=()):
        for b in reads:
            if b.lw:
                self._wait(e, b.lw[0], b.lw[1], raw=True)
            self._wait_dma(e, b, b.dwcnt)
            if b.psum:
                for f, i in b.rd.items():
                    self._wait(e, f, i)
        for b in writes:
            if b.lw:
                self._wait(e, b.lw[0], b.lw[1], raw=b.psum)
            for f, i in b.rd.items():
                self._wait(e, f, i)
            self._wait_dma(e, b, b.dcnt)
        self.cnt[e] += 1
        idx = self.cnt[e]
        fn(self.e[e]).then_inc(self.sem[e], 1)
        for b in reads:
            b.rd[e] = idx
        for b in writes:
            b.lw = (e, idx)
            b.rd = {}

    def dma(self, q, out, in_, b, load, after=None, mark=None, indirect=None, reads=(), disjoint=False):
        if b.sem is None:
            b.sem = self.nc.alloc_semaphore(name=f"dsem{len(self.dbufs)}")
            self.dbufs.append(b)
        if load:
            if b.lw:
                self._wait(q, b.lw[0], b.lw[1], force=True)
            for f, i in b.rd.items():
                self._wait(q, f, i, force=True)
            if not disjoint:
                self._wait_dma(q, b, b.dcnt)
        else:
            if b.lw:
                self._wait(q, b.lw[0], b.lw[1], force=True)
            self._wait_dma(q, b, b.dwcnt)
        for rb in reads:
            if rb.lw:
                self._wait(q, rb.lw[0], rb.lw[1], force=True)
            self._wait_dma(q, rb, rb.dwcnt)
        if after is not None and after in self.ddep:
            db, dc = self.ddep[after]
            self._wait_dma(q, db, dc)
        b.dcnt += 16
        if load:
            b.dwcnt = b.dcnt
            b.lw = None
            b.rd = {}
        if mark is not None:
            self.ddep[mark] = (b, b.dcnt)
        if indirect is not None:
            self.e[q].indirect_dma_start(out=out, out_offset=None, in_=in_, in_offset=indirect).then_inc(b.sem, 16)
        else:
            self.e[q].dma_start(out=out, in_=in_).then_inc(b.sem, 16)

    def finish(self):
        for b in self.dbufs:
            self._wait_dma("sp", b, b.dcnt)
        for f in self.ENG:
            if f != "sp" and self.cnt[f] > 0:
                self._wait("sp", f, self.cnt[f], force=True)


_WSHAPES = [("ffn1_w_gate", [D, DFF]), ("ffn1_w_up", [D, DFF]), ("ffn1_w_down", [DFF, D]),
            ("ffn2_w_gate", [D, DFF]), ("ffn2_w_up", [D, DFF]), ("ffn2_w_down", [DFF, D]),
            ("w_in", [D, INW]), ("w_out", [D, D]),
            ("ffn1_norm", [D]), ("mix_norm", [D]), ("xattn_norm", [D]), ("ffn2_norm", [D]),
            ("fox_q_norm", [128]), ("fox_k_norm", [128]), ("fox_b_f", [8]),
            ("mem_norm", [D]), ("xattn_w_kv", [D, 1024]), ("xattn_w_q", [D, 512]), ("xattn_w_o", [512, D]),
            ("xattn_q_norm", [128]), ("xattn_k_norm", [128]),
            ("ssd_dt_bias", [16]), ("ssd_A_log", [16]), ("ssd_D", [16]), ("ssd_out_norm", [1024])]
_WNAMES = [n for n, _ in _WSHAPES]


def build(cfg):
    nc = bass.Bass("TRN2", target_bir_lowering=False)
    S = Sched(nc)

    def din(name, shape, dt=F32):
        return nc.dram_tensor(name, list(shape), dt, kind="ExternalInput").ap()

    def dout(name, shape):
        return nc.dram_tensor(name, list(shape), F32, kind="ExternalOutput").ap()

    NPRE = cfg["npre"]
    NOWN = cfg["nown"]
    NPG = cfg["npages"]
    PR = cfg["pool_rows"]
    SAM = cfg.get("sam", True)
    NK = NPRE + NOWN
    x_pre = din("x_pre", [NPRE * 128, D])
    x_own = din("x_own", [NOWN * 128, D])
    x_sam = din("x_sam", [128, D])
    cst = din("cst", [128, 1536])
    cst2 = din("cst2", [128, 2048])
    flg = din("flg", [128, 4])
    cwb = din("cwb", [5, 1536])
    mem_in = din("mem_in", [256, D])
    pool_kv = din("pool_kv", [PR, 2048])
    pool_lf = din("pool_lf", [PR, 8])
    ptab = din("ptab", [16 * NPG], I32)
    cmem_k = din("cmem_k", [16, 256, 512])
    cmem_v = din("cmem_v", [16, 256, 512])
    st_ssm = din("st_ssm", [16, 1024, 128])
    st_conv = din("st_conv", [48, 1536])
    w = {n: din(n, shp) for n, shp in _WSHAPES}
    y_own = dout("y_own", [NOWN * 128, D])
    y_sam = dout("y_sam", [128, D])
    k_own = dout("k_own", [NOWN * 128, 1024])
    v_own = dout("v_own", [NOWN * 128, 1024])
    lf_own = dout("lf_own", [NOWN * 128, 8])
    k_sam = dout("k_sam", [128, 1024])
    v_sam = dout("v_sam", [128, 1024])
    lf_sam = dout("lf_sam", [128, 8])
    memk_o = dout("memk_o", [256, 512])
    memv_o = dout("memv_o", [256, 512])
    ssm_p = dout("ssm_p", [1024, 128])
    conv_p = dout("conv_p", [3, 1536])
    ssm_s = dout("ssm_s", [16, 1024, 128])
    conv_s = dout("conv_s", [48, 1536])
    kts = nc.dram_tensor("kts", [NK, 128, 1024], BF16, kind="Internal").ap()
    vas = nc.dram_tensor("vas", [NK, 128, 1040], BF16, kind="Internal").ap()

    CST = S.sb([128, 1536], F32, "CST")
    S.dma("sp", CST[:], cst[:, :], CST, True)
    IDF = CST[:, 0:128]
    TRI = CST[:, 128:256]
    TRIBD = CST[:, 256:384]
    ONES = CST[:, 384:512]
    SELP = CST[:, 768:896]
    SELS = CST[:, 896:1024]
    BDSEL = CST[:, 1024:1040]
    EPSC = CST[:, 1040:1041]
    ONEC = CST[:, 1041:1042]
    PIDX = CST[:, 1042:1043]
    IDB = S.sb([128, 128], BF16, "IDB")
    S.op("dve", lambda e: e.tensor_copy(out=IDB[:], in_=IDF), [CST], [IDB])
    MNEG = S.sb([128, 2, 128], BF16, "MNEG")
    S.op("dve", lambda e: e.tensor_copy(out=MNEG[:].rearrange("p a b -> p (a b)"), in_=CST[:, 512:768]), [CST], [MNEG])
    FLG = S.sb([128, 4], F32, "FLG")
    S.dma("sp", FLG[:], flg[:, :], FLG, True)

    PS = [Buf(nc.alloc_psum_tensor(f"ps{i}", [128, 512], F32), psum=True) for i in range(8)]
    psi = [0]

    def ps():
        b = PS[psi[0] % 4]
        psi[0] += 1
        return b

    TG = 256
    XG = [[S.sb([128, 512], F32, f"XG{t}_{c}") for c in range(4)] for t in range(4)]
    UT = S.sb([128, 16, TG], BF16, "UT")
    GB = S.sb([128, D], F32, "GB")
    JUNK = S.sb([128, 512], BF16, "JUNK")
    UN = S.sb([128, D], BF16, "UN")
    SS = [S.sb([128, 4], F32, f"SS{t}") for t in range(2)]
    WA = [S.sb([128, 16, 256], BF16, f"WA{i}") for i in range(4)]
    SCR = [S.sb([128, 1024], F32, f"SCR{i}") for i in range(13)]
    STG = SCR[0:2]
    STG2 = SCR[2]
    ZS = SCR[3:5]
    XS = SCR[5:7]
    KTG, QT = SCR[7], SCR[8]
    MIXT = SCR[9:11]
    XSD, YT = SCR[11], SCR[12]
    DEC = YT
    UT2 = [SCR[9], SCR[10]]
    UT2v = [u[:].bitcast(BF16).rearrange("p (k c) -> p k c", k=8) for u in UT2]
    HT = [SCR[7], SCR[8]]
    HTv = [h_[:].bitcast(BF16).rearrange("p (f c) -> p f c", f=4) for h_ in HT]
    SG = [SCR[11], SCR[12]]
    WBS = [SCR[i] for i in range(7)]
    WBv = [b_[:].bitcast(BF16) for b_ in WBS]
    wbi = [0]
    cur_tb = [0]
    KTGv = KTG[:].bitcast(BF16).rearrange("p (h c) -> p h c", h=8)
    QTv = QT[:].bitcast(BF16).rearrange("p (h c) -> p h c", h=8)
    MIXv = [m[:].bitcast(BF16) for m in MIXT]
    VAG = S.sb([128, 2, 8, 130], BF16, "VAG")
    KC = [S.sb([128, 8, 128], BF16, f"KC{i}") for i in range(2)]
    VC = [S.sb([128, 8, 130], BF16, f"VC{i}") for i in range(2)]
    PT = [S.sb([128, 4, 128], BF16, f"PT{i}") for i in range(4)]
    GQ = S.sb([128, 2, 128], F32, "GQ")
    GX = S.sb([128, 2, 128], F32, "GX")
    BFB = S.sb([128, 8], F32, "BFB")
    DTB = S.sb([128, 16], F32, "DTB")
    ANEG = S.sb([128, 16], F32, "ANEG")
    DBC = S.sb([128, 16], F32, "DBC")
    WFD = S.sb([128, 16, 24], BF16, "WFD")
    SM = [S.sb([128, 32], F32, f"SM{t}") for t in range(2)]
    SMF = [S.sb([128, 64], F32, f"SMF{t}") for t in range(2)]
    DTT = [S.sb([128, 16], F32, f"DTT{t}") for t in range(2)]
    CARRY = S.sb([128, 8], F32, "CARRY")
    CKT = S.sb([128, NK, 8], F32, "CKT")
    CREFS = S.sb([128, 2, 8], F32, "CREFS")
    BIAS = S.sb([128, NK, 8], F32, "BIAS")
    RD = S.sb([128, 8], F32, "RD")
    H = S.sb([128, 1024], F32, "H")
    HB = S.sb([128, 1024], BF16, "HB")
    HX = S.sb([128, 12, 3], F32, "HX")
    CW = S.sb([128, 12, 5], F32, "CW")
    CWL = GB
    XE = [S.sb([128, 16, 11], F32, f"XE{i}") for i in range(2)]
    AC = [S.sb([128, 128], F32, f"AC{i}") for i in range(2)]
    XCF = S.sb([128, 128], F32, "XCF")
    BCT = S.sb([128, 4, TG], BF16, "BCT")
    SA = S.sb([128, 128], F32, "SA")
    ALB = S.sb([128, 16], F32, "ALB")
    CD = S.sb([128, 16], F32, "CD")
    CBM = S.sb([128, 2, 128], F32, "CBM")
    MT = [S.sb([128, 4, 128], BF16, f"MT{i}") for i in range(2)]
    XDT = S.sb([128, 1024], BF16, "XDT")
    XDTE = S.sb([128, 1024], BF16, "XDTE")
    BTM = S.sb([128, 2, 128], BF16, "BTM")
    MKT = S.sb([128, 4, 256], BF16, "MKT")
    MVA = S.sb([128, 2, 4, 130], BF16, "MVA")
    wai = [0, 0]

    def bc8(ap128, H8=8):
        return ap128.unsqueeze(1).to_broadcast([128, H8, 128])

    S.dma("sp", GQ[:, 0, :], w["fox_q_norm"].partition_broadcast(128), GQ, True)
    S.dma("sp", GQ[:, 1, :], w["fox_k_norm"].partition_broadcast(128), GQ, True)
    S.dma("sp", GX[:, 0, :], w["xattn_q_norm"].partition_broadcast(128), GX, True)
    S.dma("sp", GX[:, 1, :], w["xattn_k_norm"].partition_broadcast(128), GX, True)
    S.dma("sp", BFB[:], w["fox_b_f"].partition_broadcast(128), BFB, True)
    S.dma("sp", DTB[:], w["ssd_dt_bias"].partition_broadcast(128), DTB, True)
    S.dma("sp", ANEG[:], w["ssd_A_log"].partition_broadcast(128), ANEG, True)
    S.dma("sp", DBC[:], w["ssd_D"].partition_broadcast(128), DBC, True)
    S.op("act", lambda e: e.activation(out=ANEG[:], in_=ANEG[:], func=AF.Exp), [ANEG], [ANEG])
    S.op("dve", lambda e: e.tensor_scalar(out=ANEG[:], in0=ANEG[:], scalar1=-1.0, scalar2=None, op0=ALU.mult), [ANEG], [ANEG])
    win_v = w["w_in"].rearrange("(kc p) f -> p kc f", p=128)
    wkv_v = w["xattn_w_kv"].rearrange("(kc p) f -> p kc f", p=128)
    wq_v = w["xattn_w_q"].rearrange("(kc p) f -> p kc f", p=128)
    wout_v = w["w_out"].rearrange("(kc p) f -> p kc f", p=128)
    wo_v = w["xattn_w_o"].rearrange("(kc p) f -> p kc f", p=128)
    S.dma("pool", WFD[:, :, 0:8], win_v[:, :, 3072:3080], WFD, True)
    S.dma("pool", WFD[:, :, 8:24], win_v[:, :, 5640:5656], WFD, True)
    S.dma("sp", CWL[0:5, 0:1536], cwb[:, :], CWL, True)
    pcw = ps()
    for cc in range(12):
        S.op("pe", lambda e, cc=cc: e.transpose(out=pcw[:, cc * 5:cc * 5 + 5], in_=CWL[0:5, cc * 128:(cc + 1) * 128], identity=IDF[0:5, 0:5]),
             [CWL, CST], [pcw])
    S.op("dve", lambda e: e.tensor_copy(out=CW[:].rearrange("p a b -> p (a b)"), in_=pcw[:, 0:60]), [pcw], [CW])
    for b_, v_ in ((CARRY, 0.0), (H, 0.0), (HX, 0.0)):
        S.op("dve", lambda e, b_=b_, v_=v_: e.memset(b_[:], v_), [], [b_])
    S.op("dve", lambda e: e.memset(HB[:], 0.0), [], [HB])
    for vb in [VAG] + VC + [MVA]:
        S.op("pool", lambda e, vb=vb: e.memset(vb[:], 1.0), [], [vb])

    class WT:
        def __init__(self, name, rearr):
            shp = list(w[name].shape)
            self.name = name
            self.f = w[name].rearrange(rearr, p=128)
            self.b = nc.dram_tensor(name + "_bf", shp, BF16, kind="Internal").ap().rearrange(rearr, p=128)
            self.done = set()

    def wload2(slot_buf, slot_ap, wt, idx, key):
        if key in wt.done:
            S.dma("pool", slot_ap, wt.b[idx], slot_buf, True, after=(wt.name, key))
        else:
            S.dma("pool", slot_ap, wt.f[idx], slot_buf, True)
            S.dma("sp", wt.b[idx], slot_ap, slot_buf, False, mark=(wt.name, key))
            wt.done.add(key)

    WTS = {}

    def wt_of(name, rearr):
        if name not in WTS:
            WTS[name] = WT(name, rearr)
        return WTS[name]

    def wload(slot, src):
        S.dma("pool", slot[:, 0:src.shape[1], 0:src.shape[2]], src, slot, True)

    def norm_T(nt, gain, tiles=None, pos0=0):
        if tiles is None:
            tiles = [cur_tb[0] + t for t in range(nt)]
        S.dma("sp", GB[:], gain.partition_broadcast(128), GB, True)
        for i, xt in enumerate(tiles):
            pos = pos0 + i
            ss = SS[i % 2]
            for c in range(4):
                S.op("act", lambda e, xt=xt, c=c: e.activation(out=JUNK[:], in_=XG[xt][c][:], func=AF.Square,
                                                                accum_out=ss[:, c:c + 1]), [XG[xt][c]], [JUNK, ss])
            S.op("dve", lambda e: e.tensor_reduce(out=ss[:, 0:1], in_=ss[:, 0:4], axis=AX.X, op=ALU.add), [ss], [ss])
            S.op("act", lambda e: e.activation(out=ss[:, 1:2], in_=ss[:, 0:1], func=AF.Sqrt, bias=EPSC, scale=1.0 / D), [ss, CST], [ss])
            S.op("dve", lambda e: e.reciprocal(out=ss[:, 2:3], in_=ss[:, 1:2]), [ss], [ss])
            for c in range(4):
                S.op("dve", lambda e, xt=xt, c=c: e.scalar_tensor_tensor(out=UN[:, c * 512:(c + 1) * 512], in0=XG[xt][c][:], scalar=ss[:, 2:3],
                                                                          in1=GB[:, c * 512:(c + 1) * 512], op0=ALU.mult, op1=ALU.mult),
                     [XG[xt][c], ss, GB], [UN])

            def evac(kc0, n, pv3, pos=pos):
                if pos < 2:
                    S.op("act", lambda e: e.activation(out=UT[:, kc0:kc0 + n, pos * 128:(pos + 1) * 128], in_=pv3, func=AF.Copy), [curp[0]], [UT])
                else:
                    u2 = UT2[kc0 // 8]
                    S.op("act", lambda e: e.activation(out=UT2v[kc0 // 8][:, 0:n, (pos - 2) * 128:(pos - 1) * 128], in_=pv3, func=AF.Copy), [curp[0]], [u2])
            tr_bf(UN[:], 16, evac, [UN])

    curp = [None]

    def tr_bf(src2d, nblk, evac, rbufs):
        for b0 in range(0, nblk, 8):
            n = min(8, nblk - b0)
            p = ps()
            curp[0] = p
            pv = p[:].bitcast(BF16)
            for j in range(n):
                S.op("pe", lambda e, j=j, b0=b0: e.transpose(out=pv[:, j * 128:(j + 1) * 128], in_=src2d[:, (b0 + j) * 128:(b0 + j + 1) * 128],
                                                              identity=IDB[:]), rbufs + [IDB], [p])
            evac(b0, n, pv[:, 0:n * 128].rearrange("p (j c) -> p j c", j=n))

    def tr_f32(src_fn, nblk, evac, rbufs, rows=128):
        for b0 in range(0, nblk, 4):
            n = min(4, nblk - b0)
            p = ps()
            curp[0] = p
            for j in range(n):
                S.op("pe", lambda e, j=j, b0=b0: e.transpose(out=p[:, j * rows:(j + 1) * rows], in_=src_fn(b0 + j), identity=IDF[0:rows, 0:rows]),
                     rbufs + [CST], [p])
            evac(b0, n, p[:, 0:n * rows])

    def ffn(tiles, wg, wu, wd):
        ntl = len(tiles)
        T = ntl * 128
        T0 = min(T, 256)
        wgt = wt_of(wg, "(kc p) f -> p kc f")
        wut = wt_of(wu, "(kc p) f -> p kc f")
        wdt = wt_of(wd, "(f p) c -> p f c")

        def gu(pb, slot, j):
            for kc in range(16):
                S.op("pe", lambda e, kc=kc: e.matmul(pb[:, 0:T0], lhsT=slot[:, kc, j * 128:(j + 1) * 128], rhs=UT[:, kc, 0:T0],
                                                       start=(kc == 0), stop=(kc == 15)), [slot, UT], [pb])
            if T > 256:
                for kc in range(16):
                    S.op("pe", lambda e, kc=kc: e.matmul(pb[:, 256:T], lhsT=slot[:, kc, j * 128:(j + 1) * 128], rhs=UT2v[kc // 8][:, kc % 8, 0:T - 256],
                                                           start=(kc == 0), stop=(kc == 15)), [slot, UT2[kc // 8]], [pb])

        for sl in range(NF // 4):
            ht, htv = HT[sl % 2], HTv[sl % 2]
            for bb in range(2):
                blk = sl * 2 + bb
                g_s = WA[wai[0] % 2]
                u_s = WA[2 + wai[0] % 2]
                wai[0] += 1
                cs = (slice(None), slice(None), slice(blk * 256, (blk + 1) * 256))
                wload2(g_s, g_s[:], wgt, cs, blk)
                wload2(u_s, u_s[:], wut, cs, blk)
                for j in range(2):
                    fi = bb * 2 + j
                    pg, pu = ps(), ps()
                    gu(pg, g_s, j)
                    gu(pu, u_s, j)
                    sg = SG[fi % 2]
                    S.op("act", lambda e: e.activation(out=sg[:, 0:T], in_=pg[:, 0:T], func=AF.Silu), [pg], [sg])
                    S.op("dve", lambda e, fi=fi: e.tensor_tensor(out=htv[:, fi, 0:T], in0=sg[:, 0:T], in1=pu[:, 0:T], op=ALU.mult), [sg, pu], [ht])
            wds = []
            for fi in range(4):
                k = wbi[0] % len(WBS)
                wbi[0] += 1
                wload2(WBS[k], WBv[k], wdt, (slice(None), sl * 4 + fi, slice(None)), sl * 4 + fi)
                wds.append(k)
            for i, xt in enumerate(tiles):
                for c in range(4):
                    po = ps()
                    for fi in range(4):
                        k = wds[fi]
                        S.op("pe", lambda e, fi=fi, i=i, c=c, k=k: e.matmul(po[:, :], lhsT=htv[:, fi, i * 128:(i + 1) * 128],
                                                                              rhs=WBv[k][:, c * 512:(c + 1) * 512],
                                                                              start=(fi == 0), stop=(fi == 3)), [ht, WBS[k]], [po])
                    S.op("dve", lambda e, xt=xt, c=c: e.scalar_tensor_tensor(out=XG[xt][c][:], in0=po[:, :], scalar=0.5, in1=XG[xt][c][:],
                                                                              op0=ALU.mult, op1=ALU.add), [po, XG[xt][c]], [XG[xt][c]])

    def proj_tm(nt, wv, col0, ncols, sink, nkc=16, src=None):
        src = UT if src is None else src
        for b0 in range(0, ncols, 256):
            nb = min(256, ncols - b0)
            slot = WA[wai[1] % 4]
            wai[1] += 1
            if isinstance(wv, str):
                wt = wt_of(wv, "(kc p) f -> p kc f")
                wload2(slot, slot[:, 0:nkc, 0:nb], wt, (slice(None), slice(None), slice(col0 + b0, col0 + b0 + nb)), col0 + b0)
            else:
                wload(slot, wv[:, :, col0 + b0:col0 + b0 + nb])
            for t in range(nt):
                p = ps()
                for kc in range(nkc):
                    S.op("pe", lambda e, kc=kc, t=t: e.matmul(p[:, 0:nb], lhsT=src[:, kc, t * 128:(t + 1) * 128], rhs=slot[:, kc, 0:nb],
                                                                start=(kc == 0), stop=(kc == nkc - 1)), [src, slot], [p])
                sink(t, b0, nb, p)

    def to_stg(t, b0, nb, p):
        S.op("act", lambda e: e.activation(out=STG[t][:, b0:b0 + nb], in_=p[:, 0:nb], func=AF.Copy), [p], [STG[t]])

    def add_resid(t, b0, nb, p):
        c = b0 // 512
        o = b0 % 512
        xt = cur_tb[0] + t
        S.op("dve", lambda e: e.tensor_tensor(out=XG[xt][c][:, o:o + nb], in0=p[:, 0:nb], in1=XG[xt][c][:, o:o + nb], op=ALU.add),
             [p, XG[xt][c]], [XG[xt][c]])

    def headnorm(t, which, H8=8, G=None, xb=None):
        G = GQ if G is None else G
        xb = STG[t] if xb is None else xb
        W = H8 * 128
        x3 = xb[:, 0:W].rearrange("p (h d) -> p h d", h=H8)
        s3 = STG2[:, 0:W].rearrange("p (h d) -> p h d", h=H8)
        sm = SM[t]
        S.op("dve", lambda e: e.tensor_tensor(out=STG2[:, 0:W], in0=xb[:, 0:W], in1=xb[:, 0:W], op=ALU.mult), [xb], [STG2])
        S.op("dve", lambda e: e.tensor_reduce(out=sm[:, 0:H8], in_=s3, axis=AX.X, op=ALU.add), [STG2], [sm])
        S.op("act", lambda e: e.activation(out=sm[:, 8:8 + H8], in_=sm[:, 0:H8], func=AF.Sqrt, bias=EPSC, scale=1.0 / 128), [sm, CST], [sm])
        S.op("dve", lambda e: e.reciprocal(out=sm[:, 16:16 + H8], in_=sm[:, 8:8 + H8]), [sm], [sm])
        S.op("dve", lambda e: e.tensor_tensor(out=x3, in0=x3, in1=sm[:, 16:16 + H8].unsqueeze(2).to_broadcast([128, H8, 128]), op=ALU.mult),
             [xb, sm], [xb])
        S.op("dve", lambda e: e.tensor_tensor(out=x3, in0=x3, in1=bc8(G[:, which, :], H8), op=ALU.mult), [xb, G], [xb])

    def stg_T(t, nh, dstv, dbuf, scale=1.0, xb=None):
        xb = STG[t] if xb is None else xb
        tr_f32(lambda j: xb[:, j * 128:(j + 1) * 128], nh,
               lambda b0, n, pv: S.op("act", lambda e: e.activation(out=dstv[:, b0:b0 + n, t * 128:(t + 1) * 128],
                                                                      in_=pv.rearrange("p (j c) -> p j c", j=n), func=AF.Copy, scale=scale),
                                      [curp[0]], [dbuf]), [xb])

    def to_buf(bufs):
        def sink(t, b0, nb, p):
            S.op("act", lambda e: e.activation(out=bufs[t][:, b0:b0 + nb], in_=p[:, 0:nb], func=AF.Copy), [p], [bufs[t]])
        return sink

    def ssd_tile(kind, t, want_y):
        sam = kind == "sam"
        tri = TRIBD if sam else TRI
        sel = SELS if sam else SELP
        c0 = t * 128
        dt = SA[:, 80:96]
        da, acol, altm, dte, eac = SA[:, 0:16], SA[:, 16:32], SA[:, 32:48], SA[:, 48:64], SA[:, 64:80]
        S.op("dve", lambda e: e.tensor_copy(out=dt, in_=DTT[t][:]), [DTT[t]], [SA])
        S.op("dve", lambda e: e.tensor_tensor(out=da, in0=dt, in1=ANEG[:], op=ALU.mult), [SA, ANEG], [SA])
        p1 = ps()
        S.op("pe", lambda e: e.matmul(p1[:, 0:16], lhsT=tri, rhs=da, start=True, stop=True), [CST, SA], [p1])
        S.op("dve", lambda e: e.tensor_copy(out=acol, in_=p1[:, 0:16]), [p1], [SA])
        S.op("pe", lambda e: e.matmul(p1[:, 16:32], lhsT=sel, rhs=acol, start=True, stop=True), [CST, SA], [p1])
        S.op("dve", lambda e: e.tensor_copy(out=altm, in_=p1[:, 16:32]), [p1], [SA])
        S.op("dve", lambda e: e.tensor_tensor(out=dte, in0=altm, in1=acol, op=ALU.subtract), [SA], [SA])
        S.op("act", lambda e: e.activation(out=dte, in_=dte, func=AF.Exp), [SA], [SA])
        S.op("act", lambda e: e.activation(out=eac, in_=acol, func=AF.Exp), [SA], [SA])
        xs3 = XS[t][:].rearrange("p (h d) -> p h d", h=16)
        S.op("dve", lambda e: e.tensor_tensor(out=XDT[:].rearrange("p (h d) -> p h d", h=16), in0=xs3,
                                              in1=dt.unsqueeze(2).to_broadcast([128, 16, 64]), op=ALU.mult), [XS[t], SA], [XDT])
        S.op("dve", lambda e: e.tensor_tensor(out=XDTE[:].rearrange("p (h d) -> p h d", h=16), in0=XDT[:].rearrange("p (h d) -> p h d", h=16),
                                              in1=dte.unsqueeze(2).to_broadcast([128, 16, 64]), op=ALU.mult), [XDT, SA], [XDTE])
        if want_y:
            S.op("dve", lambda e: e.tensor_tensor(out=XSD[:].rearrange("p (h d) -> p h d", h=16), in0=xs3,
                                                  in1=DBC[:].unsqueeze(2).to_broadcast([128, 16, 64]), op=ALU.mult), [XS[t], DBC], [XSD])
            pc = ps()
            for g in range(2):
                S.op("pe", lambda e, g=g: e.matmul(pc[:, g * 128:(g + 1) * 128], lhsT=BCT[:, g, c0:c0 + 128], rhs=BCT[:, 2 + g, c0:c0 + 128],
                                                    start=True, stop=True), [BCT], [pc])
            S.op("dve", lambda e: e.tensor_tensor(out=CBM[:], in0=pc[:, 0:256].rearrange("p (g c) -> p g c", g=2),
                                                  in1=tri.unsqueeze(1).to_broadcast([128, 2, 128]), op=ALU.mult), [pc, CST], [CBM])
            for hq in range(4):
                pa = ps()
                for j in range(4):
                    h = hq * 4 + j
                    S.op("pe", lambda e, j=j, h=h: e.matmul(pa[:, j * 128:(j + 1) * 128], lhsT=da[:, h:h + 1].to_broadcast([128, 128]), rhs=tri,
                                                              start=True, stop=True), [SA, CST], [pa])
                dec3 = DEC[:, 0:512].rearrange("p (j c) -> p j c", j=4)
                for j in range(4):
                    h = hq * 4 + j
                    S.op("dve", lambda e, j=j, h=h: e.tensor_scalar(out=dec3[:, j, :], in0=pa[:, j * 128:(j + 1) * 128], scalar1=acol[:, h:h + 1],
                                                                      scalar2=0.0, op0=ALU.subtract, op1=ALU.min), [pa, SA], [DEC])
                S.op("act", lambda e: e.activation(out=DEC[:, 0:512], in_=DEC[:, 0:512], func=AF.Exp), [DEC], [DEC])
                mt = MT[hq % 2]
                g = hq // 2
                S.op("dve", lambda e, g=g: e.tensor_tensor(out=mt[:], in0=dec3, in1=CBM[:, g, :].unsqueeze(1).to_broadcast([128, 4, 128]), op=ALU.mult),
                     [DEC, CBM], [mt])
                for j in range(4):
                    h = hq * 4 + j
                    pb = PS[4 + h // 8]
                    S.op("pe", lambda e, j=j, h=h, pb=pb: e.matmul(pb[:, (h % 8) * 64:(h % 8 + 1) * 64], lhsT=mt[:, j, :], rhs=XDT[:, h * 64:(h + 1) * 64],
                                                                     start=True, stop=True), [mt, XDT], [pb])
            if not sam:
                for g in range(2):
                    S.op("pe", lambda e, g=g: e.matmul(PS[6 + g][:, :], lhsT=BCT[:, 2 + g, c0:c0 + 128], rhs=HB[:, g * 512:(g + 1) * 512],
                                                        start=True, stop=True), [BCT, HB], [PS[6 + g]])
        pbt = ps()
        pbv = pbt[:].bitcast(BF16)
        for g in range(2):
            S.op("pe", lambda e, g=g: e.transpose(out=pbv[:, g * 128:(g + 1) * 128], in_=BCT[:, g, c0:c0 + 128], identity=IDB[:]), [BCT, IDB], [pbt])
        S.op("act", lambda e: e.activation(out=BTM[:].rearrange("p a b -> p (a b)"), in_=pbv[:, 0:256], func=AF.Copy), [pbt], [BTM])
        return da, acol, altm, dte, eac

    def ssd_finish_y(t, eac):
        y3 = YT[:].rearrange("p (h d) -> p h d", h=16)
        for g in range(2):
            S.op("dve", lambda e, g=g: e.tensor_tensor(out=y3[:, g * 8:(g + 1) * 8, :], in0=PS[6 + g][:, :].rearrange("p (h d) -> p h d", h=8),
                                                        in1=eac[:, g * 8:(g + 1) * 8].unsqueeze(2).to_broadcast([128, 8, 64]), op=ALU.mult),
                 [PS[6 + g], SA], [YT])
        S.op("dve", lambda e: e.tensor_tensor(out=YT[:], in0=YT[:], in1=XSD[:], op=ALU.add), [YT, XSD], [YT])
        for g in range(2):
            S.op("dve", lambda e, g=g: e.tensor_tensor(out=YT[:, g * 512:(g + 1) * 512], in0=PS[4 + g][:, :], in1=YT[:, g * 512:(g + 1) * 512], op=ALU.add),
                 [PS[4 + g], YT], [YT])
        S.op("dve", lambda e: e.tensor_tensor(out=YT[:], in0=YT[:], in1=ZS[t][:], op=ALU.mult), [YT, ZS[t]], [YT])
        ss = SS[t]
        S.dma("sp", GB[:, 0:1024], w["ssd_out_norm"].partition_broadcast(128), GB, True)
        S.op("act", lambda e: e.activation(out=STG2[:], in_=YT[:], func=AF.Square, accum_out=ss[:, 0:1]), [YT], [STG2, ss])
        S.op("act", lambda e: e.activation(out=ss[:, 1:2], in_=ss[:, 0:1], func=AF.Sqrt, bias=EPSC, scale=1.0 / 1024), [ss, CST], [ss])
        S.op("dve", lambda e: e.reciprocal(out=ss[:, 2:3], in_=ss[:, 1:2]), [ss], [ss])
        S.op("dve", lambda e: e.scalar_tensor_tensor(out=MIXv[t][:, 1024:2048], in0=YT[:], scalar=ss[:, 2:3], in1=GB[:, 0:1024], op0=ALU.mult, op1=ALU.mult),
             [YT, ss, GB], [MIXT[t]])

    def state_out(dst_view, hsrc, hbuf):
        ho = STG2
        tr_f32(lambda j: hsrc[:, j * 128:(j + 1) * 128], 8,
               lambda b0, n, pv: S.op("act", lambda e: e.activation(out=ho[:, b0 * 128:(b0 + n) * 128], in_=pv, func=AF.Copy), [curp[0]], [ho]), [hbuf])
        S.dma("sp", dst_view, ho[:].rearrange("p (c n) -> p c n", c=8), ho, False)

    def big_group(kind, bgi):
        sam = kind == "sam"
        full = kind != "pre"
        ntot = 1 if sam else (NPRE if kind == "pre" else NOWN)
        ntl = 1 if sam else min(4, ntot - bgi * 4)
        xsrc = {"pre": x_pre, "own": x_own, "sam": x_sam}[kind]
        rb = bgi * 512
        tiles = list(range(ntl))
        for t in tiles:
            for c in range(4):
                S.dma("sp", XG[t][c][:], xsrc[rb + t * 128:rb + (t + 1) * 128, c * 512:(c + 1) * 512], XG[t][c], True)
        norm_T(ntl, w["ffn1_norm"], tiles)
        ffn(tiles, "ffn1_w_gate", "ffn1_w_up", "ffn1_w_down")
        for sub in range((ntl + 1) // 2):
            cur_tb[0] = sub * 2
            mixer(kind, bgi * 2 + sub, min(2, ntl - sub * 2))
        cur_tb[0] = 0
        if not full:
            return
        norm_T(ntl, w["ffn2_norm"], tiles)
        ffn(tiles, "ffn2_w_gate", "ffn2_w_up", "ffn2_w_down")
        ydst = {"own": y_own, "sam": y_sam}[kind]
        for t in tiles:
            for c in range(4):
                S.dma("sp", ydst[rb + t * 128:rb + (t + 1) * 128, c * 512:(c + 1) * 512], XG[t][c][:], XG[t][c], False)

    def mixer(kind, gi, nt):
        sam = kind == "sam"
        T = nt * 128
        full = kind != "pre"
        r0 = gi * 256
        norm_T(nt, w["mix_norm"])
        kdst = {"own": k_own, "sam": k_sam}.get(kind)
        vdst = {"own": v_own, "sam": v_sam}.get(kind)
        ldst = {"own": lf_own, "sam": lf_sam}.get(kind)
        kt0 = (0 if kind == "pre" else NPRE) + gi * 2
        proj_tm(nt, "w_in", 1024, 1024, to_stg)
        proj_tm(nt, "w_in", 2048, 1024, to_buf(XS))
        if full:
            proj_tm(nt, "w_in", 0, 1024, to_buf(MIXT))
            proj_tm(nt, "w_in", 3080, 1024,
                    lambda t, b0, nb, p: S.op("act", lambda e: e.activation(out=ZS[t][:, b0:b0 + nb], in_=p[:, 0:nb], func=AF.Silu), [p], [ZS[t]]))
        for t in range(nt):
            headnorm(t, 1)
            if kdst is not None:
                S.dma("sp", kdst[r0 + t * 128:r0 + (t + 1) * 128, :], STG[t][:], STG[t], False)
            stg_T(t, 8, KTGv, KTG)
        if not sam:
            for t in range(nt):
                S.dma("sp", kts[kt0 + t].rearrange("p (h c) -> p h c", h=8), KTGv[:, :, t * 128:(t + 1) * 128], KTG, False, mark=("k", kt0 + t))
        for t in range(nt):
            if vdst is not None:
                S.dma("sp", vdst[r0 + t * 128:r0 + (t + 1) * 128, :], XS[t][:], XS[t], False)
            S.op("pool", lambda e, t=t: e.tensor_copy(out=VAG[:, t, :, 0:128], in_=XS[t][:].rearrange("p (h d) -> p h d", h=8)), [XS[t]], [VAG])
            if not sam:
                S.dma("sp", vas[kt0 + t], VAG[:, t, :, :].rearrange("p h c -> p (h c)"), VAG, False, mark=("v", kt0 + t))
        if full:
            for t in range(nt):
                headnorm(t, 0, xb=MIXT[t])
                stg_T(t, 8, QTv, QT, scale=128 ** -0.5, xb=MIXT[t])
        for t in range(nt):
            p = ps()
            sm = SMF[t]
            for kc in range(16):
                S.op("pe", lambda e, kc=kc, t=t: e.matmul(p[:, 0:24], lhsT=UT[:, kc, t * 128:(t + 1) * 128], rhs=WFD[:, kc, :],
                                                            start=(kc == 0), stop=(kc == 15)), [UT, WFD], [p])
            S.op("dve", lambda e: e.tensor_tensor(out=sm[:, 0:8], in0=p[:, 0:8], in1=BFB[:], op=ALU.add), [p, BFB], [sm])
            S.op("dve", lambda e: e.tensor_tensor(out=sm[:, 32:48], in0=p[:, 8:24], in1=DTB[:], op=ALU.add), [p, DTB], [sm])
            S.op("act", lambda e: e.activation(out=sm[:, 8:16], in_=sm[:, 0:8], func=AF.Exp, scale=-1.0), [sm], [sm])
            S.op("act", lambda e: e.activation(out=sm[:, 48:64], in_=sm[:, 32:48], func=AF.Exp), [sm], [sm])
            S.op("act", lambda e: e.activation(out=sm[:, 16:24], in_=sm[:, 8:16], func=AF.Ln, bias=ONEC, scale=1.0), [sm, CST], [sm])
            S.op("act", lambda e, t=t: e.activation(out=DTT[t][:], in_=sm[:, 48:64], func=AF.Ln, bias=ONEC, scale=1.0), [sm, CST], [DTT[t]])
            S.op("dve", lambda e: e.tensor_scalar(out=sm[:, 24:32], in0=sm[:, 16:24], scalar1=-1.0, scalar2=None, op0=ALU.mult), [sm], [sm])
            if ldst is not None:
                S.dma("sp", ldst[r0 + t * 128:r0 + (t + 1) * 128, :], sm[:, 24:32], sm, False)
        if not sam:
            for t in range(nt):
                kt = kt0 + t
                p = ps()
                lf = SMF[t][:, 24:32]
                S.op("pe", lambda e: e.matmul(p[:, 0:8], lhsT=TRI, rhs=lf, start=True, stop=True), [CST, SMF[t]], [p])
                S.op("pe", lambda e: e.matmul(p[:, 8:16], lhsT=ONES, rhs=lf, start=True, stop=True), [CST, SMF[t]], [p])
                S.op("dve", lambda e, kt=kt: e.tensor_tensor(out=CKT[:, kt, :], in0=p[:, 0:8], in1=CARRY[:], op=ALU.add), [p, CARRY], [CKT])
                S.op("dve", lambda e: e.tensor_tensor(out=CARRY[:], in0=p[:, 8:16], in1=CARRY[:], op=ALU.add), [p, CARRY], [CARRY])
                S.op("dve", lambda e, t=t: e.tensor_copy(out=CREFS[:, t, :], in_=CARRY[:]), [CARRY], [CREFS])
        if sam:
            S.dma("sp", STG2[0:48, :], st_conv[:, 0:1024], STG2, True)
            S.dma("sp", STG[0][0:48, 0:512], st_conv[:, 1024:1536], STG[0], True)
        def conv_post(cc, p):
            for t in range(nt):
                xe = XE[t]
                xef = xe[:].rearrange("p a b -> p (a b)")
                ac = AC[t]
                if sam:
                    ph = ps()
                    hsrc = STG2[0:48, cc * 128:(cc + 1) * 128] if cc < 8 else STG[0][0:48, (cc - 8) * 128:(cc - 7) * 128]
                    hb = STG2 if cc < 8 else STG[0]
                    S.op("pe", lambda e: e.transpose(out=ph[:, 0:48], in_=hsrc, identity=IDF[0:48, 0:48]), [hb, CST], [ph])
                    S.op("dve", lambda e: e.tensor_copy(out=xe[:, :, 0:3], in_=ph[:, 0:48].rearrange("p (b j) -> p b j", j=3)), [ph], [xe])
                    S.op("act", lambda e: e.activation(out=xe[:, :, 3:11], in_=p[:, 0:128].rearrange("p (b j) -> p b j", j=8), func=AF.Copy), [p], [xe])
                    xin = [xe[:, :, j2:j2 + 8] for j2 in range(4)]
                    aco = ac[:].rearrange("p (b j) -> p b j", j=8)
                    pre_cols = xe[:, :, 8:11]
                else:
                    if t == 0:
                        S.op("dve", lambda e, cc=cc: e.tensor_copy(out=xef[:, 0:3], in_=HX[:, cc, :]), [HX], [xe])
                    else:
                        xp = XE[0][:].rearrange("p a b -> p (a b)")
                        S.op("dve", lambda e: e.tensor_copy(out=xef[:, 0:3], in_=xp[:, 128:131]), [XE[0]], [xe])
                    S.op("act", lambda e, t=t: e.activation(out=xef[:, 3:131], in_=p[:, t * 128:(t + 1) * 128], func=AF.Copy), [p], [xe])
                    if t == nt - 1:
                        S.op("dve", lambda e, cc=cc: e.tensor_copy(out=HX[:, cc, :], in_=xef[:, 128:131]), [xe], [HX])
                    xin = [xef[:, j2:j2 + 128] for j2 in range(4)]
                    aco = ac[:]
                S.op("dve", lambda e, cc=cc: e.tensor_scalar(out=aco, in0=xin[0], scalar1=CW[:, cc, 0:1], scalar2=CW[:, cc, 4:5], op0=ALU.mult, op1=ALU.add),
                     [xe, CW], [ac])
                for j2 in range(1, 4):
                    S.op("dve", lambda e, cc=cc, j2=j2: e.scalar_tensor_tensor(out=aco, in0=xin[j2], scalar=CW[:, cc, j2:j2 + 1], in1=aco, op0=ALU.mult, op1=ALU.add),
                         [xe, CW, ac], [ac])
                if cc < 8:
                    S.op("act", lambda e: e.activation(out=XCF[:], in_=ac[:], func=AF.Silu), [ac], [XCF])
                    pt_ = ps()
                    S.op("pe", lambda e: e.transpose(out=pt_[:, 0:128], in_=XCF[:], identity=IDF), [XCF, CST], [pt_])
                    S.op("dve", lambda e, cc=cc, t=t: e.tensor_copy(out=XS[t][:, cc * 128:(cc + 1) * 128], in_=pt_[:, 0:128]), [pt_], [XS[t]])
                else:
                    S.op("act", lambda e, cc=cc, t=t: e.activation(out=BCT[:, cc - 8, t * 128:(t + 1) * 128], in_=ac[:], func=AF.Silu), [ac], [BCT])
                last_prompt = (kind == "own" and gi == NOWN // 2 - 1 and t == nt - 1)
                if last_prompt or sam:
                    pt2 = ps()
                    if sam:
                        S.op("act", lambda e: e.activation(out=XCF[:].rearrange("p (b j) -> p b j", j=8), in_=xe[:, :, 3:11], func=AF.Copy), [xe], [XCF])
                    else:
                        S.op("act", lambda e: e.activation(out=XCF[:], in_=xef[:, 3:131], func=AF.Copy), [xe], [XCF])
                    S.op("pe", lambda e: e.transpose(out=pt2[:, 0:128], in_=XCF[:], identity=IDF), [XCF, CST], [pt2])
                    cvb = DEC if cc < 8 else XSD
                    S.op("dve", lambda e, cc=cc: e.tensor_copy(out=cvb[:, (cc % 8) * 128:(cc % 8 + 1) * 128], in_=pt2[:, 0:128]), [pt2], [cvb])
        pend = None
        for cc in range(13):
            cur = None
            if cc < 12:
                if cc % 2 == 0:
                    slot = WA[wai[1] % 4]
                    wai[1] += 1
                    wload2(slot, slot[:], wt_of("w_in", "(kc p) f -> p kc f"), (slice(None), slice(None), slice(4104 + cc * 128, 4104 + cc * 128 + 256)), 4104 + cc * 128)
                j = cc % 2
                p = PS[4 + cc % 2]
                for kc in range(16):
                    S.op("pe", lambda e, kc=kc, j=j, slot=slot, p=p: e.matmul(p[:, 0:T], lhsT=slot[:, kc, j * 128:(j + 1) * 128], rhs=UT[:, kc, 0:T],
                                                                                start=(kc == 0), stop=(kc == 15)), [slot, UT], [p])
                cur = (cc, p)
            if pend is not None:
                conv_post(*pend)
            pend = cur
        if kind == "own" and gi == NOWN // 2 - 1:
            S.dma("sp", conv_p[:, 0:1024], DEC[125:128, :], DEC, False)
            S.dma("sp", conv_p[:, 1024:1536], XSD[125:128, 0:512], XSD, False)
        if sam:
            for b in range(16):
                S.dma("sp", conv_s[b * 3:b * 3 + 3, 0:1024], DEC[b * 8 + 5:b * 8 + 8, :], DEC, False)
                S.dma("sp", conv_s[b * 3:b * 3 + 3, 1024:1536], XSD[b * 8 + 5:b * 8 + 8, 0:512], XSD, False)
        for t in range(nt):
            da, acol, altm, dte, eac = ssd_tile(kind, t, full)
            if not sam:
                if full:
                    ssd_finish_y(t, eac)
                p1 = ps()
                S.op("pe", lambda e: e.matmul(p1[:, 16:32], lhsT=ONES, rhs=da, start=True, stop=True), [CST, SA], [p1])
                S.op("act", lambda e: e.activation(out=CD[:], in_=p1[:, 16:32], func=AF.Exp), [p1], [CD])
                for g in range(2):
                    S.op("pe", lambda e, g=g: e.matmul(PS[4 + g][:, :], lhsT=BTM[:, g, :], rhs=XDTE[:, g * 512:(g + 1) * 512], start=True, stop=True),
                         [BTM, XDTE], [PS[4 + g]])
                h3 = H[:].rearrange("p (h d) -> p h d", h=16)
                S.op("dve", lambda e: e.tensor_tensor(out=h3, in0=h3, in1=CD[:].unsqueeze(2).to_broadcast([128, 16, 64]), op=ALU.mult), [H, CD], [H])
                for g in range(2):
                    S.op("dve", lambda e, g=g: e.tensor_tensor(out=H[:, g * 512:(g + 1) * 512], in0=PS[4 + g][:, :], in1=H[:, g * 512:(g + 1) * 512], op=ALU.add),
                         [PS[4 + g], H], [H])
                S.op("act", lambda e: e.activation(out=HB[:], in_=H[:], func=AF.Copy), [H], [HB])
            else:
                sample_ssd(acol, eac)
        if kind == "pre" and gi == NPRE // 2 - 1:
            S.op("dve", lambda e: e.tensor_scalar(out=H[:], in0=H[:], scalar1=FLG[:, 0:1], scalar2=None, op0=ALU.mult), [H, FLG], [H])
            S.op("act", lambda e: e.activation(out=HB[:], in_=H[:], func=AF.Copy), [H], [HB])
        if kind == "own" and gi == NOWN // 2 - 1:
            state_out(ssm_p.rearrange("(c p) n -> p c n", p=128), H, H)
        if not full:
            return
        for t in range(nt):
            if sam:
                sample_attn()
            else:
                prompt_attn(gi, t)
        for t in range(nt):
            tr_bf(MIXv[t], 16, lambda kc0, n, pv3, t=t: S.op("act", lambda e: e.activation(
                out=UT[:, kc0:kc0 + n, t * 128:(t + 1) * 128], in_=pv3, func=AF.Copy), [curp[0]], [UT]), [MIXT[t]])
        proj_tm(nt, "w_out", 0, D, add_resid)
        norm_T(nt, w["xattn_norm"])
        proj_tm(nt, "xattn_w_q", 0, 512, to_stg)
        for t in range(nt):
            headnorm(t, 0, 4, GX)
            stg_T(t, 4, QTv, QT, scale=128 ** -0.5)
        for t in range(nt):
            xattn(sam, t)
        for t in range(nt):
            tr_bf(MIXv[t][:, 0:512], 4, lambda kc0, n, pv3, t=t: S.op("act", lambda e: e.activation(
                out=UT[:, kc0:kc0 + n, t * 128:(t + 1) * 128], in_=pv3, func=AF.Copy), [curp[0]], [UT]), [MIXT[t]])
        proj_tm(nt, "xattn_w_o", 0, D, add_resid, nkc=4)

    OB = [PS[4], PS[5], PS[6]]

    def o_region(h, n=129):
        return OB[h // 3][:, (h % 3) * 129:(h % 3) * 129 + n]

    def attn_finish(t, nheads, width=128):
        for h in range(nheads):
            S.op("dve", lambda e, h=h: e.reciprocal(out=RD[:, h:h + 1], in_=o_region(h)[:, 128:129]), [OB[h // 3]], [RD])
        for h in range(nheads):
            S.op("act", lambda e, h=h: e.activation(out=MIXv[t][:, h * 128:(h + 1) * 128], in_=o_region(h, 128), func=AF.Copy, scale=RD[:, h:h + 1]),
                 [OB[h // 3], RD], [MIXT[t]])

    kci = [0]

    def prompt_attn(gi, t):
        oi = gi * 2 + t
        nk = NPRE + oi + 1
        S.op("dve", lambda e: e.tensor_tensor(out=BIAS[:, 0:nk, :], in0=CREFS[:, t, :].unsqueeze(1).to_broadcast([128, nk, 8]), in1=CKT[:, 0:nk, :],
                                              op=ALU.subtract), [CREFS, CKT], [BIAS])
        if NPRE > 0:
            S.op("dve", lambda e: e.tensor_scalar(out=BIAS[:, 0:NPRE, :], in0=BIAS[:, 0:NPRE, :], scalar1=FLG[:, 1:2], scalar2=None, op0=ALU.add),
                 [BIAS, FLG], [BIAS])
        for ob in OB:
            S.op("dve", lambda e, ob=ob: e.memset(ob[:, :], 0.0), [], [ob])
        for kt in range(nk):
            kc_, vc_ = KC[kci[0] % 2], VC[kci[0] % 2]
            kci[0] += 1
            S.dma("sp", kc_[:], kts[kt].rearrange("p (h c) -> p h c", h=8), kc_, True, after=("k", kt))
            S.dma("sp", vc_[:].rearrange("p h c -> p (h c)"), vas[kt], vc_, True, after=("v", kt))
            diag = kt == nk - 1
            for hq in range(2):
                p = ps()
                pt = PT[(kci[0] * 2 + hq) % 4]
                for j in range(4):
                    h = hq * 4 + j
                    S.op("pe", lambda e, j=j, h=h: e.matmul(p[:, j * 128:(j + 1) * 128], lhsT=kc_[:, h, :], rhs=QTv[:, h, t * 128:(t + 1) * 128],
                                                              start=True, stop=not diag), [kc_, QT], [p])
                    if diag:
                        S.op("pe", lambda e, j=j: e.matmul(p[:, j * 128:(j + 1) * 128], lhsT=IDB[:], rhs=MNEG[:, 0, :], start=False, stop=True),
                             [IDB, MNEG], [p])
                for j in range(4):
                    h = hq * 4 + j
                    S.op("act", lambda e, j=j, h=h, kt=kt: e.activation(out=pt[:, j, :], in_=p[:, j * 128:(j + 1) * 128], func=AF.Exp,
                                                                          bias=BIAS[:, kt, h:h + 1], scale=1.0), [p, BIAS], [pt])
                for j in range(4):
                    h = hq * 4 + j
                    S.op("pe", lambda e, j=j, h=h: e.matmul(o_region(h), lhsT=pt[:, j, :], rhs=vc_[:, h, 0:129], start=False, stop=(kt == nk - 1),
                                                              skip_group_check=True), [pt, vc_], [OB[h // 3]])
        attn_finish(t, 8)

    def xattn(sam, t):
        for ob in OB[0:2]:
            S.op("dve", lambda e, ob=ob: e.memset(ob[:, :], 0.0), [], [ob])
        if not sam:
            for mt_ in range(2):
                p = ps()
                pt = PT[mt_ % 4]
                for h in range(4):
                    S.op("pe", lambda e, h=h: e.matmul(p[:, h * 128:(h + 1) * 128], lhsT=MKT[:, h, mt_ * 128:(mt_ + 1) * 128],
                                                        rhs=QTv[:, h, t * 128:(t + 1) * 128], start=True, stop=True), [MKT, QT], [p])
                S.op("act", lambda e: e.activation(out=pt[:].rearrange("p a b -> p (a b)"), in_=p[:, :], func=AF.Exp), [p], [pt])
                for h in range(4):
                    S.op("pe", lambda e, h=h: e.matmul(o_region(h), lhsT=pt[:, h, :], rhs=MVA[:, mt_, h, 0:129], start=False, stop=(mt_ == 1),
                                                        skip_group_check=True), [pt, MVA], [OB[h // 3]])
        else:
            sample_xattn()
        attn_finish(t, 4)

    IDX = S.sb([128, 16 * NPG], I32, "IDX")
    IDXF = STG2
    if SAM:
        S.dma("sp", IDX[:], ptab.partition_broadcast(128), IDX, True)
        S.op("dve", lambda e: e.tensor_copy(out=IDXF[:, 0:16 * NPG], in_=IDX[:]), [IDX], [IDXF])
        S.op("dve", lambda e: e.tensor_scalar(out=IDXF[:, 0:16 * NPG], in0=IDXF[:, 0:16 * NPG], scalar1=128.0, scalar2=PIDX, op0=ALU.mult, op1=ALU.add), [IDXF, CST], [IDXF])
        S.op("dve", lambda e: e.tensor_copy(out=IDX[:], in_=IDXF[:, 0:16 * NPG]), [IDXF], [IDX])

    U32 = mybir.dt.uint32
    TMPS = [S.sb([128, 64], F32, f"TMPS{i}") for i in range(2)]
    LFP = [S.sb([128, 16, 8], F32, f"LFP{i}") for i in range(2)]
    CPX = S.sb([128, 17, 8], F32, "CPX")
    TOTP = S.sb([128, 16, 8], F32, "TOTP")
    BIASPS = [S.sb([128, 16, 8], F32, f"BIASP{i}") for i in range(2)]
    NEWTOT = S.sb([128, 16, 8], F32, "NEWTOT")
    LFB = S.sb([128, 16, 16], F32, "LFB")
    CDS = S.sb([128, 16, 16], F32, "CDS")
    BIASN = S.sb([128, 24], F32, "BIASN")
    CTPB = [S.sb([128, 2, 128], BF16, f"CTPB{i}") for i in range(2)]
    BTMB = [S.sb([128, 2, 128], BF16, f"BTMB{i}") for i in range(2)]

    def sample_ssd(acol, eac):
        da = SA[:, 0:16]
        S.op("dve", lambda e: e.tensor_tensor(out=LFB[:], in0=da.unsqueeze(1).to_broadcast([128, 16, 16]),
                                              in1=BDSEL.unsqueeze(2).to_broadcast([128, 16, 16]), op=ALU.mult), [SA, CST], [LFB])
        pc = ps()
        S.op("pe", lambda e: e.matmul(pc[:, 0:256], lhsT=ONES, rhs=LFB[:].rearrange("p a b -> p (a b)"), start=True, stop=True), [CST, LFB], [pc])
        S.op("act", lambda e: e.activation(out=CDS[:].rearrange("p a b -> p (a b)"), in_=pc[:, 0:256], func=AF.Exp), [pc], [CDS])
        H0L = [SCR[1], SCR[4]]
        H0T = [SCR[6], SCR[10]]
        HBS = [(HB, HB[:]), (H, H[:].bitcast(BF16)[:, 0:1024])]
        for b in range(16):
            h0l, h0t = H0L[b % 2], H0T[b % 2]
            hbb, hbv = HBS[b % 2]
            S.dma("sp", h0l[:].rearrange("p (c n) -> p c n", c=8), st_ssm[b].rearrange("(c p) n -> p c n", p=128), h0l, True)
            tr_f32(lambda j: h0l[:, j * 128:(j + 1) * 128], 8,
                   lambda b0, n, pv: S.op("act", lambda e: e.activation(out=h0t[:, b0 * 128:(b0 + n) * 128], in_=pv, func=AF.Copy), [curp[0]], [h0t]), [h0l])
            S.op("dve", lambda e: e.tensor_copy(out=hbv, in_=h0t[:]), [h0t], [hbb])
            ctp, btm = CTPB[b % 2], BTMB[b % 2]
            S.op("pool", lambda e: e.memset(ctp[:], 0.0), [], [ctp])
            S.op("pool", lambda e, b=b: e.tensor_copy(out=ctp[:, :, b * 8:(b + 1) * 8], in_=BCT[:, 2:4, b * 8:(b + 1) * 8]), [BCT], [ctp])
            S.op("dve", lambda e, b=b: e.tensor_scalar(out=btm[:].rearrange("p a b -> p (a b)"), in0=BTM[:].rearrange("p a b -> p (a b)"),
                                                         scalar1=BDSEL[:, b:b + 1], scalar2=None, op0=ALU.mult), [BTM, CST], [btm])
            for g in range(2):
                S.op("pe", lambda e, g=g, b=b: e.matmul(PS[6 + g][:, :], lhsT=ctp[:, g, :], rhs=hbv[:, g * 512:(g + 1) * 512], start=(b == 0), stop=(b == 15)),
                     [ctp, hbb], [PS[6 + g]])
            pS = [ps(), ps()]
            for g in range(2):
                S.op("pe", lambda e, g=g: e.matmul(pS[g][:, :], lhsT=btm[:, g, :], rhs=XDTE[:, g * 512:(g + 1) * 512], start=True, stop=True), [btm, XDTE], [pS[g]])
            h3 = h0t[:].rearrange("p (h d) -> p h d", h=16)
            S.op("dve", lambda e, b=b: e.tensor_tensor(out=h3, in0=h3, in1=CDS[:, b, :].unsqueeze(2).to_broadcast([128, 16, 64]), op=ALU.mult), [h0t, CDS], [h0t])
            for g in range(2):
                S.op("dve", lambda e, g=g: e.tensor_tensor(out=h0t[:, g * 512:(g + 1) * 512], in0=pS[g][:, :], in1=h0t[:, g * 512:(g + 1) * 512], op=ALU.add),
                     [pS[g], h0t], [h0t])
            state_out(ssm_s[b].rearrange("(c p) n -> p c n", p=128), h0t, h0t)
        ssd_finish_y(0, eac)

    def sample_attn():
        lf = SMF[0][:, 24:32]
        for ob in OB:
            S.op("dve", lambda e, ob=ob: e.memset(ob[:, :], 0.0), [], [ob])
        S.op("dve", lambda e: e.tensor_tensor(out=LFB[:, :, 0:8], in0=lf.unsqueeze(1).to_broadcast([128, 16, 8]),
                                              in1=BDSEL.unsqueeze(2).to_broadcast([128, 16, 8]), op=ALU.mult), [SMF[0], CST], [LFB])
        p = ps()
        S.op("pe", lambda e: e.matmul(p[:, 0:128].rearrange("p (a b) -> p a b", a=16), lhsT=ONES, rhs=LFB[:, :, 0:8], start=True, stop=True), [CST, LFB], [p])
        S.op("pe", lambda e: e.matmul(p[:, 128:136], lhsT=TRIBD, rhs=lf, start=True, stop=True), [CST, SMF[0]], [p])
        S.op("dve", lambda e: e.tensor_copy(out=NEWTOT[:].rearrange("p a b -> p (a b)"), in_=p[:, 0:128]), [p], [NEWTOT])
        S.op("dve", lambda e: e.tensor_copy(out=BIASN[:, 0:8], in_=p[:, 128:136]), [p], [BIASN])
        S.op("pe", lambda e: e.matmul(p[:, 136:144], lhsT=SELS, rhs=BIASN[:, 0:8], start=True, stop=True), [CST, BIASN], [p])
        S.op("dve", lambda e: e.tensor_tensor(out=BIASN[:, 8:16], in0=p[:, 136:144], in1=BIASN[:, 0:8], op=ALU.subtract), [p, BIASN], [BIASN])
        for hq in range(2):
            p = ps()
            pt = PT[hq]
            for j in range(4):
                h = hq * 4 + j
                S.op("pe", lambda e, j=j, h=h: e.matmul(p[:, j * 128:(j + 1) * 128], lhsT=KTGv[:, h, 0:128], rhs=QTv[:, h, 0:128], start=True, stop=False), [KTG, QT], [p])
                S.op("pe", lambda e, j=j: e.matmul(p[:, j * 128:(j + 1) * 128], lhsT=IDB[:], rhs=MNEG[:, 1, :], start=False, stop=True), [IDB, MNEG], [p])
            for j in range(4):
                h = hq * 4 + j
                S.op("act", lambda e, j=j, h=h: e.activation(out=pt[:, j, :], in_=p[:, j * 128:(j + 1) * 128], func=AF.Exp, bias=BIASN[:, 8 + h:9 + h], scale=1.0),
                     [p, BIASN], [pt])
            for j in range(4):
                h = hq * 4 + j
                S.op("pe", lambda e, j=j, h=h: e.matmul(o_region(h), lhsT=pt[:, j, :], rhs=VAG[:, 0, h, 0:129], start=False, stop=False, skip_group_check=True),
                     [pt, VAG], [OB[h // 3]])
        NS = 4
        KVS = WA
        KVv = [k_[:].bitcast(F32).rearrange("p a b -> p (a b)") for k_ in KVS]
        KTP = [XS[0], XS[1], SCR[4], SCR[0]]
        VPB = [(VC[0], VC[0][:]), (VC[1], VC[1][:]), (MVA, MVA[:].rearrange("p a b c -> p (a b) c")),
               (SCR[1], SCR[1][:].bitcast(BF16)[:, 0:1040].rearrange("p (h c) -> p h c", h=8))]
        S.op("dve", lambda e: e.memset(SCR[1][:].bitcast(BF16)[:, 0:1040], 1.0), [], [SCR[1]])
        PTZ = [(PT[0], PT[1]), (PT[2], PT[3])]
        pages = [(b, j) for b in range(16) for j in range(NPG)]
        NP_ = len(pages)
        lastb = [None, None]
        ktvs = {}

        def seq_setup(b):
            lfp = LFP[b % 2]
            for j in range(NPG):
                col = b * NPG + j
                S.dma("pool", lfp[:, j, :], pool_lf[:, :], lfp, True, indirect=bass.IndirectOffsetOnAxis(ap=IDX[:, col:col + 1].bitcast(U32), axis=0), reads=[IDX],
                      disjoint=(j > 0))
            p = ps()
            lfp2 = lfp[:, 0:NPG, :]
            S.op("pe", lambda e: e.matmul(p[:, 0:NPG * 8].rearrange("p (a b) -> p a b", a=NPG), lhsT=ONES, rhs=lfp2, start=True, stop=True), [CST, lfp], [p])
            S.op("pe", lambda e: e.matmul(p[:, 128:128 + NPG * 8].rearrange("p (a b) -> p a b", a=NPG), lhsT=TRI, rhs=lfp2, start=True, stop=True), [CST, lfp], [p])
            S.op("dve", lambda e: e.tensor_copy(out=TOTP[:, 0:NPG, :].rearrange("p a b -> p (a b)"), in_=p[:, 0:NPG * 8]), [p], [TOTP])
            S.op("dve", lambda e: e.memset(CPX[:, 0, :], 0.0), [], [CPX])
            for j in range(NPG):
                S.op("dve", lambda e, j=j: e.tensor_tensor(out=CPX[:, j + 1, :], in0=CPX[:, j, :], in1=TOTP[:, j, :], op=ALU.add), [CPX, TOTP], [CPX])
            bp = BIASPS[b % 2]
            S.op("dve", lambda e, b=b: e.tensor_tensor(out=BIASN[:, 16:24], in0=CPX[:, NPG, :], in1=NEWTOT[:, b, :], op=ALU.add), [CPX, NEWTOT], [BIASN])
            S.op("dve", lambda e: e.tensor_tensor(out=bp[:, 0:NPG, :], in0=BIASN[:, 16:24].unsqueeze(1).to_broadcast([128, NPG, 8]), in1=CPX[:, 0:NPG, :],
                                                  op=ALU.subtract), [BIASN, CPX], [bp])
            S.op("dve", lambda e: e.tensor_tensor(out=bp[:, 0:NPG, :], in0=bp[:, 0:NPG, :], in1=p[:, 128:128 + NPG * 8].rearrange("p (a b) -> p a b", a=NPG),
                                                  op=ALU.subtract), [bp, p], [bp])

        def stage_T(i):
            b, j = pages[i]
            if j == 0:
                seq_setup(b)
            col = b * NPG + j
            kv, kvv, ktp = KVS[i % NS], KVv[i % NS], KTP[i % NS]
            vb, vv = VPB[i % NS]
            off = bass.IndirectOffsetOnAxis(ap=IDX[:, col:col + 1].bitcast(U32), axis=0)
            S.dma("pool", kvv, pool_kv[:, :], kv, True, indirect=off, reads=[IDX])
            ktv = ktp[:].bitcast(BF16)[:, 0:1024].rearrange("p (h c) -> p h c", h=8)
            ktvs[i] = ktv
            tr_f32(lambda jj: kvv[:, jj * 128:(jj + 1) * 128], 8,
                   lambda b0, n, pv: S.op("act", lambda e: e.activation(out=ktv[:, b0:b0 + n, :], in_=pv.rearrange("p (j c) -> p j c", j=n), func=AF.Copy),
                                          [curp[0]], [ktp]), [kv])
            S.op("dve", lambda e: e.tensor_copy(out=vv[:, :, 0:128], in_=kvv[:, 1024:2048].rearrange("p (h d) -> p h d", h=8)), [kv], [vb])

        def stage_Q(i):
            b, j = pages[i]
            ktp, ktv = KTP[i % NS], ktvs[i]
            ptz = PTZ[i % 2]
            tmp = TMPS[i % 2]
            if lastb[i % 2] is not None and lastb[i % 2] != b:
                ob = lastb[i % 2]
                for z_ in ptz:
                    S.op("dve", lambda e, z_=z_, ob=ob: e.memset(z_[:, :, ob * 8:(ob + 1) * 8], 0.0), [], [z_])
            lastb[i % 2] = b
            p = ps()
            for h in range(8):
                S.op("pe", lambda e, h=h, b=b: e.matmul(p[:, h * 8:(h + 1) * 8], lhsT=ktv[:, h, :], rhs=QTv[:, h, b * 8:(b + 1) * 8], start=True, stop=True),
                     [ktp, QT], [p])
            S.op("dve", lambda e, j=j, b=b: e.tensor_tensor(out=tmp[:].rearrange("p (h q) -> p h q", h=8), in0=p[:, 0:64].rearrange("p (h q) -> p h q", h=8),
                                                             in1=BIASPS[b % 2][:, j, :].unsqueeze(2).to_broadcast([128, 8, 8]), op=ALU.add), [p, BIASPS[b % 2]], [tmp])
            for hh in range(2):
                S.op("act", lambda e, hh=hh, b=b: e.activation(out=ptz[hh][:, :, b * 8:(b + 1) * 8], in_=tmp[:, hh * 32:(hh + 1) * 32].rearrange("p (h q) -> p h q", h=4),
                                                                 func=AF.Exp), [tmp], [ptz[hh]])

        def stage_P(i):
            ptz = PTZ[i % 2]
            vb, vv = VPB[i % NS]
            for h in range(8):
                S.op("pe", lambda e, h=h: e.matmul(o_region(h), lhsT=ptz[h // 4][:, h % 4, :], rhs=vv[:, h, 0:129], start=False, stop=False, skip_group_check=True),
                     [ptz[h // 4], vb], [OB[h // 3]])

        for pt in PT:
            S.op("dve", lambda e, pt=pt: e.memset(pt[:], 0.0), [], [pt])
        for i in range(NP_ + 3):
            if i < NP_:
                stage_T(i)
            if 0 <= i - 2 < NP_:
                stage_Q(i - 2)
            if 0 <= i - 3 < NP_:
                stage_P(i - 3)
        attn_finish(0, 8)

    def sample_xattn():
        MKL = [SCR[0], SCR[1]]
        MVL = [SCR[3], SCR[4]]
        MKTB = [XS[0], XS[1]]
        MVAB = [(MVA, MVA[:]), (VC[0], VC[0][:].rearrange("p (a b) c -> p a b c", a=2))]
        for b in range(16):
            mkl, mvl, mktb = MKL[b % 2], MVL[b % 2], MKTB[b % 2]
            mvb, mvv = MVAB[b % 2]
            ptz = (PT[0], PT[1]) if b % 2 == 0 else (PT[2], PT[3])
            S.dma("sp", mkl[:].rearrange("p (t c) -> p t c", t=2), cmem_k[b].rearrange("(t p) c -> p t c", p=128), mkl, True)
            S.dma("sp", mvl[:].rearrange("p (t c) -> p t c", t=2), cmem_v[b].rearrange("(t p) c -> p t c", p=128), mvl, True)
            mkv = mktb[:].bitcast(BF16)[:, 0:1024].rearrange("p (h c) -> p h c", h=4)
            for mt_ in range(2):
                tr_f32(lambda jj, mt_=mt_: mkl[:, mt_ * 512 + jj * 128:mt_ * 512 + (jj + 1) * 128], 4,
                       lambda b0, n, pv, mt_=mt_: S.op("act", lambda e: e.activation(out=mkv[:, b0:b0 + n, mt_ * 128:(mt_ + 1) * 128],
                                                                                      in_=pv.rearrange("p (j c) -> p j c", j=n), func=AF.Copy), [curp[0]], [mktb]), [mkl])
            S.op("pool", lambda e: e.tensor_copy(out=mvv[:, :, :, 0:128], in_=mvl[:].rearrange("p (t h d) -> p t h d", t=2, h=4)), [mvl], [mvb])
            for z_ in ptz:
                S.op("pool", lambda e, z_=z_: e.memset(z_[:], 0.0), [], [z_])
            p = ps()
            for mt_ in range(2):
                for h in range(4):
                    S.op("pe", lambda e, h=h, mt_=mt_, b=b: e.matmul(p[:, mt_ * 32 + h * 8:mt_ * 32 + (h + 1) * 8], lhsT=mkv[:, h, mt_ * 128:(mt_ + 1) * 128],
                                                                       rhs=QTv[:, h, b * 8:(b + 1) * 8], start=True, stop=True), [mktb, QT], [p])
            for mt_ in range(2):
                S.op("act", lambda e, mt_=mt_, b=b: e.activation(out=ptz[mt_][:, :, b * 8:(b + 1) * 8], in_=p[:, mt_ * 32:(mt_ + 1) * 32].rearrange("p (h q) -> p h q", h=4),
                                                                   func=AF.Exp), [p], [ptz[mt_]])
            for mt_ in range(2):
                for h in range(4):
                    S.op("pe", lambda e, h=h, mt_=mt_: e.matmul(o_region(h), lhsT=ptz[mt_][:, h, :], rhs=mvv[:, mt_, h, 0:129], start=False, stop=False,
                                                                  skip_group_check=True), [ptz[mt_], mvb], [OB[h // 3]])

    def memkv():
        for t in range(2):
            for c in range(4):
                S.dma("sp", XG[t][c][:], mem_in[t * 128:(t + 1) * 128, c * 512:(c + 1) * 512], XG[t][c], True)
        norm_T(2, w["mem_norm"])
        proj_tm(2, wkv_v, 0, 512, to_stg)
        for t in range(2):
            headnorm(t, 1, 4, GX)
            S.dma("sp", memk_o[t * 128:(t + 1) * 128, :], STG[t][:, 0:512], STG[t], False)
            stg_T(t, 4, MKT, MKT)
        proj_tm(2, wkv_v, 512, 512, to_stg)
        for t in range(2):
            S.dma("sp", memv_o[t * 128:(t + 1) * 128, :], STG[t][:, 0:512], STG[t], False)
            S.op("pool", lambda e, t=t: e.tensor_copy(out=MVA[:, t, :, 0:128], in_=STG[t][:, 0:512].rearrange("p (h d) -> p h d", h=4)), [STG[t]], [MVA])

    memkv()
    for bgi in range((NPRE + 3) // 4):
        big_group("pre", bgi)
    for bgi in range((NOWN + 3) // 4):
        big_group("own", bgi)
    if SAM:
        big_group("sam", 0)
    S.finish()
    return nc


def make_consts():
    c = np.zeros((128, 1536), np.float32)
    k = np.arange(128)
    le = k[:, None] <= k[None, :]
    same = (k[:, None] // 8) == (k[None, :] // 8)
    c[:, 0:128] = np.eye(128)
    c[:, 128:256] = le
    c[:, 256:384] = le & same
    c[:, 384:512] = 1.0
    c[:, 512:640] = np.where(le, 0.0, NEG)
    c[:, 640:768] = np.where(le & same, 0.0, NEG)
    c[:, 768:896] = (k[:, None] == 127)
    c[:, 896:1024] = (k[:, None] == (k[None, :] // 8) * 8 + 7)
    c[:, 1024:1040] = (k[:, None] // 8) == np.arange(16)[None, :]
    c[:, 1040] = EPS
    c[:, 1041] = 1.0
    c[:, 1042] = k
    c2 = np.zeros((128, 16, 128), np.float32)
    c2[:] = ((k[None, :] // 8) == np.arange(16)[:, None])[None]
    return c, c2.reshape(128, 2048)


def core_inputs(inp, c, cfg, cst, cst2):
    s, h = c // 2, c % 2
    xp = inp["x_prompt"]
    xs = inp["x_sample"]
    npre, nown = cfg["npre"] * 128, cfg["nown"] * 128
    f32 = lambda a: np.ascontiguousarray(np.asarray(a, np.float32))
    flg = np.zeros((128, 4), np.float32)
    flg[:, 0] = 1.0 if h == 1 else 0.0
    flg[:, 1] = 0.0 if h == 1 else NEG
    m = {"x_own": f32(xp[s, h * nown:(h + 1) * nown]),
         "x_pre": f32(xp[s, 0:npre]) if h == 1 else np.zeros((npre, D), np.float32),
         "x_sam": f32(xs[c * 16:(c + 1) * 16]).reshape(128, D),
         "mem_in": f32(inp["mem_prompt"][s]),
         "cst": cst, "cst2": cst2, "flg": flg,
         "cwb": np.ascontiguousarray(np.concatenate([np.asarray(inp["conv_w"], np.float32)[0], np.asarray(inp["conv_b"], np.float32)], axis=0)),
         "pool_kv": inp["_pool_kv"], "pool_lf": inp["_pool_lf"],
         "ptab": np.ascontiguousarray(np.asarray(inp["page_table"], np.int32)[c * 16:(c + 1) * 16]).reshape(-1),
         "cmem_k": f32(inp["cache_mem_k"][0, c * 16:(c + 1) * 16]).reshape(16, 256, 512),
         "cmem_v": f32(inp["cache_mem_v"][0, c * 16:(c + 1) * 16]).reshape(16, 256, 512),
         "st_ssm": f32(inp["state_ssm"][0, c * 16:(c + 1) * 16]).reshape(16, 1024, 128),
         "st_conv": f32(inp["state_conv"][0, c * 16:(c + 1) * 16]).reshape(48, 1536)}
    for n in _WNAMES:
        m[n] = f32(inp[n][0])
    return m


def kernel(**inp):
    inp = dict(inp)
    npool = inp["cache_fox_k"].shape[1]
    inp["_pool_kv"] = np.concatenate([np.asarray(inp["cache_fox_k"], np.float32)[0].reshape(npool * 128, 1024),
                                      np.asarray(inp["cache_fox_v"], np.float32)[0].reshape(npool * 128, 1024)], axis=1)
    inp["_pool_lf"] = np.ascontiguousarray(np.asarray(inp["cache_fox_logf"], np.float32)[0]).reshape(npool * 128, 8)
    cfg = {"npre": 8, "nown": 8, "npages": inp["page_table"].shape[1], "pool_rows": npool * 128}
    nc = build(cfg)
    cst, cst2 = make_consts()
    in_maps = [core_inputs(inp, c, cfg, cst, cst2) for c in range(NCORES)]
    res = run_bass_kernel_spmd(nc, in_maps, core_ids=list(range(NCORES))).results
    B, L = 4, 2048
    z = lambda *s: np.zeros(s, np.float32)
    y_p, pk, pv, plf = z(B, L, D), z(1, B, L, 8, 128), z(1, B, L, 8, 128), z(1, B, L, 8)
    pssm, pconv, pmk, pmv = z(1, B, 16, 64, 128), z(1, B, 3, 1536), z(1, B, 256, 4, 128), z(1, B, 256, 4, 128)
    y_s, sk, sv, slf = z(128, 8, D), z(1, 128, 8, 8, 128), z(1, 128, 8, 8, 128), z(1, 128, 8, 8)
    sssm, sconv = z(1, 128, 16, 64, 128), z(1, 128, 3, 1536)
    for c in range(NCORES):
        s, h = c // 2, c % 2
        r = res[c]
        sl = slice(h * 1024, (h + 1) * 1024)
        y_p[s, sl] = r["y_own"]
        pk[0, s, sl] = r["k_own"].reshape(1024, 8, 128)
        pv[0, s, sl] = r["v_own"].reshape(1024, 8, 128)
        plf[0, s, sl] = r["lf_own"]
        if h == 0:
            pmk[0, s] = r["memk_o"].reshape(256, 4, 128)
            pmv[0, s] = r["memv_o"].reshape(256, 4, 128)
        else:
            pssm[0, s] = r["ssm_p"].reshape(16, 64, 128)
            pconv[0, s] = r["conv_p"]
        cs = slice(c * 16, (c + 1) * 16)
        y_s[cs] = r["y_sam"].reshape(16, 8, D)
        sk[0, cs] = r["k_sam"].reshape(16, 8, 8, 128)
        sv[0, cs] = r["v_sam"].reshape(16, 8, 8, 128)
        slf[0, cs] = r["lf_sam"].reshape(16, 8, 8)
        sssm[0, cs] = r["ssm_s"].reshape(16, 16, 64, 128)
        sconv[0, cs] = r["conv_s"].reshape(16, 3, 1536)
    return (y_p, y_s, pk, pv, plf, pssm, pconv, pmk, pmv, sk, sv, slf, sssm, sconv)
```

```python
import numpy as np
import concourse.bass as bass
import concourse.mybir as mybir
from concourse.bass_utils import run_bass_kernel_spmd

F32, BF16, I32 = mybir.dt.float32, mybir.dt.bfloat16, mybir.dt.int32
AF = mybir.ActivationFunctionType
ALU = mybir.AluOpType
AX = mybir.AxisListType

D = 2048
DFF = 5632
NF = DFF // 128
EPS = 1e-6
NCORES = 8
INW = 5656
NEG = -30000.0


class Buf:
    def __init__(self, t, psum=False):
        self.t = t
        self.psum = psum
        self.lw = None
        self.rd = {}
        self.sem = None
        self.dcnt = 0
        self.dwcnt = 0

    def __getitem__(self, k):
        return self.t[k]


class Sched:
    ENG = ("pe", "act", "dve", "pool", "sp")

    def __init__(self, nc):
        self.nc = nc
        self.e = {"pe": nc.tensor, "act": nc.scalar, "dve": nc.vector, "pool": nc.gpsimd, "sp": nc.sync}
        self.cnt = {e: 0 for e in self.ENG}
        self.sem = {e: nc.alloc_semaphore(name="sem_" + e) for e in self.ENG}
        self.known = {e: {f: 0 for f in self.ENG} for e in self.ENG}
        self.kdma = {e: {} for e in self.ENG}
        self.dbufs = []
        self.nsb = 0
        self.ddep = {}

    def sb(self, shape, dt=F32, name=None):
        self.nsb += 1
        return Buf(self.nc.alloc_sbuf_tensor(name or f"sb{self.nsb}", list(shape), dt))

    def _wait(self, e, f, idx, raw=False, force=False):
        if f == e and not force:
            if e in ("pe", "sp") or not raw:
                return
        if self.known[e][f] >= idx:
            return
        self.known[e][f] = idx
        self.e[e].wait_ge(self.sem[f], idx)

    def _wait_dma(self, e, b, cnt):
        if cnt == 0 or self.kdma[e].get(id(b), 0) >= cnt:
            return
        self.kdma[e][id(b)] = cnt
        self.e[e].wait_ge(b.sem, cnt)

    def op(self, e, fn, reads=(), writes=()):
        for b in reads:
            if b.lw:
                self._wait(e, b.lw[0], b.lw[1], raw=True)
            self._wait_dma(e, b, b.dwcnt)
            if b.psum:
                for f, i in b.rd.items():
                    self._wait(e, f, i)
        for b in writes:
            if b.lw:
                self._wait(e, b.lw[0], b.lw[1], raw=b.psum)
            for f, i in b.rd.items():
                self._wait(e, f, i)
            self._wait_dma(e, b, b.dcnt)
        self.cnt[e] += 1
        idx = self.cnt[e]
        fn(self.e[e]).then_inc(self.sem[e], 1)
        for b in reads:
            b.rd[e] = idx
        for b in writes:
            b.lw = (e, idx)
            b.rd = {}

    def dma(self, q, out, in_, b, load, after=None, mark=None, indirect=None, reads=(), disjoint=False):
        if b.sem is None:
            b.sem = self.nc.alloc_semaphore(name=f"dsem{len(self.dbufs)}")
            self.dbufs.append(b)
        if load:
            if b.lw:
                self._wait(q, b.lw[0], b.lw[1], force=True)
            for f, i in b.rd.items():
                self._wait(q, f, i, force=True)
            if not disjoint:
                self._wait_dma(q, b, b.dcnt)
        else:
            if b.lw:
                self._wait(q, b.lw[0], b.lw[1], force=True)
            self._wait_dma(q, b, b.dwcnt)
        for rb in reads:
            if rb.lw:
                self._wait(q, rb.lw[0], rb.lw[1], force=True)
            self._wait_dma(q, rb, rb.dwcnt)
        if after is not None and after in self.ddep:
            db, dc = self.ddep[after]
            self._wait_dma(q, db, dc)
        b.dcnt += 16
        if load:
            b.dwcnt = b.dcnt
            b.lw = None
            b.rd = {}
        if mark is not None:
            self.ddep[mark] = (b, b.dcnt)
        if indirect is not None:
            self.e[q].indirect_dma_start(out=out, out_offset=None, in_=in_, in_offset=indirect).then_inc(b.sem, 16)
        else:
            self.e[q].dma_start(out=out, in_=in_).then_inc(b.sem, 16)

    def finish(self):
        for b in self.dbufs:
            self._wait_dma("sp", b, b.dcnt)
        for f in self.ENG:
            if f != "sp" and self.cnt[f] > 0:
                self._wait("sp", f, self.cnt[f], force=True)


_WSHAPES = [("ffn1_w_gate", [D, DFF]), ("ffn1_w_up", [D, DFF]), ("ffn1_w_down", [DFF, D]),
            ("ffn2_w_gate", [D, DFF]), ("ffn2_w_up", [D, DFF]), ("ffn2_w_down", [DFF, D]),
            ("w_in", [D, INW]), ("w_out", [D, D]),
            ("ffn1_norm", [D]), ("mix_norm", [D]), ("xattn_norm", [D]), ("ffn2_norm", [D]),
            ("fox_q_norm", [128]), ("fox_k_norm", [128]), ("fox_b_f", [8]),
            ("mem_norm", [D]), ("xattn_w_kv", [D, 1024]), ("xattn_w_q", [D, 512]), ("xattn_w_o", [512, D]),
            ("xattn_q_norm", [128]), ("xattn_k_norm", [128]),
            ("ssd_dt_bias", [16]), ("ssd_A_log", [16]), ("ssd_D", [16]), ("ssd_out_norm", [1024])]
_WNAMES = [n for n, _ in _WSHAPES]


def build(cfg):
    nc = bass.Bass("TRN2", target_bir_lowering=False)
    S = Sched(nc)

    def din(name, shape, dt=F32):
        return nc.dram_tensor(name, list(shape), dt, kind="ExternalInput").ap()

    def dout(name, shape):
        return nc.dram_tensor(name, list(shape), F32, kind="ExternalOutput").ap()

    NPRE = cfg["npre"]
    NOWN = cfg["nown"]
    NPG = cfg["npages"]
    PR = cfg["pool_rows"]
    SAM = cfg.get("sam", True)
    NK = NPRE + NOWN
    x_pre = din("x_pre", [NPRE * 128, D])
    x_own = din("x_own", [NOWN * 128, D])
    x_sam = din("x_sam", [128, D])
    cst = din("cst", [128, 1536])
    cst2 = din("cst2", [128, 2048])
    flg = din("flg", [128, 4])
    cwb = din("cwb", [5, 1536])
    mem_in = din("mem_in", [256, D])
    pool_kv = din("pool_kv", [PR, 2048])
    pool_lf = din("pool_lf", [PR, 8])
    ptab = din("ptab", [16 * NPG], I32)
    cmem_k = din("cmem_k", [16, 256, 512])
    cmem_v = din("cmem_v", [16, 256, 512])
    st_ssm = din("st_ssm", [16, 1024, 128])
    st_conv = din("st_conv", [48, 1536])
    w = {n: din(n, shp) for n, shp in _WSHAPES}
    y_own = dout("y_own", [NOWN * 128, D])
    y_sam = dout("y_sam", [128, D])
    k_own = dout("k_own", [NOWN * 128, 1024])
    v_own = dout("v_own", [NOWN * 128, 1024])
    lf_own = dout("lf_own", [NOWN * 128, 8])
    k_sam = dout("k_sam", [128, 1024])
    v_sam = dout("v_sam", [128, 1024])
    lf_sam = dout("lf_sam", [128, 8])
    memk_o = dout("memk_o", [256, 512])
    memv_o = dout("memv_o", [256, 512])
    ssm_p = dout("ssm_p", [1024, 128])
    conv_p = dout("conv_p", [3, 1536])
    ssm_s = dout("ssm_s", [16, 1024, 128])
    conv_s = dout("conv_s", [48, 1536])
    kts = nc.dram_tensor("kts", [NK, 128, 1024], BF16, kind="Internal").ap()
    vas = nc.dram_tensor("vas", [NK, 128, 1040], BF16, kind="Internal").ap()

    CST = S.sb([128, 1536], F32, "CST")
    S.dma("sp", CST[:], cst[:, :], CST, True)
    IDF = CST[:, 0:128]
    TRI = CST[:, 128:256]
    TRIBD = CST[:, 256:384]
    ONES = CST[:, 384:512]
    SELP = CST[:, 768:896]
    SELS = CST[:, 896:1024]
    BDSEL = CST[:, 1024:1040]
    EPSC = CST[:, 1040:1041]
    ONEC = CST[:, 1041:1042]
    PIDX = CST[:, 1042:1043]
    IDB = S.sb([128, 128], BF16, "IDB")
    S.op("dve", lambda e: e.tensor_copy(out=IDB[:], in_=IDF), [CST], [IDB])
    MNEG = S.sb([128, 2, 128], BF16, "MNEG")
    S.op("dve", lambda e: e.tensor_copy(out=MNEG[:].rearrange("p a b -> p (a b)"), in_=CST[:, 512:768]), [CST], [MNEG])
    FLG = S.sb([128, 4], F32, "FLG")
    S.dma("sp", FLG[:], flg[:, :], FLG, True)

    PS = [Buf(nc.alloc_psum_tensor(f"ps{i}", [128, 512], F32), psum=True) for i in range(8)]
    psi = [0]

    def ps():
        b = PS[psi[0] % 4]
        psi[0] += 1
        return b

    TG = 256
    XG = [[S.sb([128, 512], F32, f"XG{t}_{c}") for c in range(4)] for t in range(5)]
    UT = S.sb([128, 16, TG], BF16, "UT")
    GB = S.sb([128, D], F32, "GB")
    JUNK = S.sb([128, 512], BF16, "JUNK")
    UN = S.sb([128, D], BF16, "UN")
    SS = [S.sb([128, 4], F32, f"SS{t}") for t in range(2)]
    WA = [S.sb([128, 16, 256], BF16, f"WA{i}") for i in range(4)]
    SCR = [S.sb([128, 1024], F32, f"SCR{i}") for i in range(13)]
    STG = SCR[0:2]
    STG2 = SCR[2]
    ZS = SCR[3:5]
    XS = SCR[5:7]
    KTG, QT = SCR[7], SCR[8]
    MIXT = SCR[9:11]
    XSD, YT = SCR[11], SCR[12]
    DEC = YT
    UT2 = [SCR[9], SCR[10]]
    UT2v = [u[:].bitcast(BF16).rearrange("p (k c) -> p k c", k=8) for u in UT2]
    HT = [SCR[7], SCR[8]]
    HTv = [h_[:].bitcast(BF16).rearrange("p (f c) -> p f c", f=4) for h_ in HT]
    SG = [SCR[11], SCR[12]]
    WBS = [SCR[i] for i in range(5)]
    UT3 = SCR[5]
    UT3v = UT3[:].bitcast(BF16).rearrange("p (k c) -> p k c", k=16)
    HTB = SCR[6]
    HTBv = HTB[:].bitcast(BF16)[:, 0:512].rearrange("p (f c) -> p f c", f=4)
    psi8 = [0]

    def ps8():
        b = PS[psi8[0] % 8]
        psi8[0] += 1
        return b
    WBv = [b_[:].bitcast(BF16) for b_ in WBS]
    wbi = [0]
    cur_tb = [0]
    KTGv = KTG[:].bitcast(BF16).rearrange("p (h c) -> p h c", h=8)
    QTv = QT[:].bitcast(BF16).rearrange("p (h c) -> p h c", h=8)
    MIXv = [m[:].bitcast(BF16) for m in MIXT]
    VAG = S.sb([128, 2, 8, 130], BF16, "VAG")
    KC = [S.sb([128, 8, 128], BF16, f"KC{i}") for i in range(2)]
    VC = [S.sb([128, 8, 130], BF16, f"VC{i}") for i in range(2)]
    PT = [S.sb([128, 4, 128], BF16, f"PT{i}") for i in range(4)]
    GQ = S.sb([128, 2, 128], F32, "GQ")
    GX = S.sb([128, 2, 128], F32, "GX")
    BFB = S.sb([128, 8], F32, "BFB")
    DTB = S.sb([128, 16], F32, "DTB")
    ANEG = S.sb([128, 16], F32, "ANEG")
    DBC = S.sb([128, 16], F32, "DBC")
    WFD = S.sb([128, 16, 24], BF16, "WFD")
    SM = [S.sb([128, 32], F32, f"SM{t}") for t in range(2)]
    SMF = [S.sb([128, 64], F32, f"SMF{t}") for t in range(2)]
    DTT = [S.sb([128, 16], F32, f"DTT{t}") for t in range(2)]
    CARRY = S.sb([128, 8], F32, "CARRY")
    CKT = S.sb([128, NK, 8], F32, "CKT")
    CREFS = S.sb([128, 2, 8], F32, "CREFS")
    BIAS = S.sb([128, NK, 8], F32, "BIAS")
    RD = S.sb([128, 8], F32, "RD")
    H = S.sb([128, 1024], F32, "H")
    HB = S.sb([128, 1024], BF16, "HB")
    HX = S.sb([128, 12, 3], F32, "HX")
    CW = S.sb([128, 12, 5], F32, "CW")
    CWL = GB
    XE = [S.sb([128, 16, 11], F32, f"XE{i}") for i in range(2)]
    AC = [S.sb([128, 128], F32, f"AC{i}") for i in range(2)]
    XCF = S.sb([128, 128], F32, "XCF")
    BCT = S.sb([128, 4, TG], BF16, "BCT")
    SA = S.sb([128, 128], F32, "SA")
    ALB = S.sb([128, 16], F32, "ALB")
    CD = S.sb([128, 16], F32, "CD")
    CBM = S.sb([128, 2, 128], F32, "CBM")
    MT = [S.sb([128, 4, 128], BF16, f"MT{i}") for i in range(2)]
    XDT = S.sb([128, 1024], BF16, "XDT")
    XDTE = S.sb([128, 1024], BF16, "XDTE")
    BTM = S.sb([128, 2, 128], BF16, "BTM")
    MKT = S.sb([128, 4, 256], BF16, "MKT")
    MVA = S.sb([128, 2, 4, 130], BF16, "MVA")
    wai = [0, 0]

    def bc8(ap128, H8=8):
        return ap128.unsqueeze(1).to_broadcast([128, H8, 128])

    S.dma("sp", GQ[:, 0, :], w["fox_q_norm"].partition_broadcast(128), GQ, True)
    S.dma("sp", GQ[:, 1, :], w["fox_k_norm"].partition_broadcast(128), GQ, True)
    S.dma("sp", GX[:, 0, :], w["xattn_q_norm"].partition_broadcast(128), GX, True)
    S.dma("sp", GX[:, 1, :], w["xattn_k_norm"].partition_broadcast(128), GX, True)
    S.dma("sp", BFB[:], w["fox_b_f"].partition_broadcast(128), BFB, True)
    S.dma("sp", DTB[:], w["ssd_dt_bias"].partition_broadcast(128), DTB, True)
    S.dma("sp", ANEG[:], w["ssd_A_log"].partition_broadcast(128), ANEG, True)
    S.dma("sp", DBC[:], w["ssd_D"].partition_broadcast(128), DBC, True)
    S.op("act", lambda e: e.activation(out=ANEG[:], in_=ANEG[:], func=AF.Exp), [ANEG], [ANEG])
    S.op("dve", lambda e: e.tensor_scalar(out=ANEG[:], in0=ANEG[:], scalar1=-1.0, scalar2=None, op0=ALU.mult), [ANEG], [ANEG])
    win_v = w["w_in"].rearrange("(kc p) f -> p kc f", p=128)
    wkv_v = w["xattn_w_kv"].rearrange("(kc p) f -> p kc f", p=128)
    wq_v = w["xattn_w_q"].rearrange("(kc p) f -> p kc f", p=128)
    wout_v = w["w_out"].rearrange("(kc p) f -> p kc f", p=128)
    wo_v = w["xattn_w_o"].rearrange("(kc p) f -> p kc f", p=128)
    S.dma("pool", WFD[:, :, 0:8], win_v[:, :, 3072:3080], WFD, True)
    S.dma("pool", WFD[:, :, 8:24], win_v[:, :, 5640:5656], WFD, True)
    S.dma("sp", CWL[0:5, 0:1536], cwb[:, :], CWL, True)
    pcw = ps()
    for cc in range(12):
        S.op("pe", lambda e, cc=cc: e.transpose(out=pcw[:, cc * 5:cc * 5 + 5], in_=CWL[0:5, cc * 128:(cc + 1) * 128], identity=IDF[0:5, 0:5]),
             [CWL, CST], [pcw])
    S.op("dve", lambda e: e.tensor_copy(out=CW[:].rearrange("p a b -> p (a b)"), in_=pcw[:, 0:60]), [pcw], [CW])
    for b_, v_ in ((CARRY, 0.0), (H, 0.0), (HX, 0.0)):
        S.op("dve", lambda e, b_=b_, v_=v_: e.memset(b_[:], v_), [], [b_])
    S.op("dve", lambda e: e.memset(HB[:], 0.0), [], [HB])
    for vb in [VAG] + VC + [MVA]:
        S.op("pool", lambda e, vb=vb: e.memset(vb[:], 1.0), [], [vb])

    def wload(slot, src):
        S.dma("pool", slot[:, 0:src.shape[1], 0:src.shape[2]], src, slot, True)

    def norm_T(nt, gain, tiles=None, pos0=0):
        if tiles is None:
            tiles = [cur_tb[0] + t for t in range(nt)]
        S.dma("sp", GB[:], gain.partition_broadcast(128), GB, True)
        for i, xt in enumerate(tiles):
            pos = pos0 + i
            ss = SS[i % 2]
            for c in range(4):
                S.op("act", lambda e, xt=xt, c=c: e.activation(out=JUNK[:], in_=XG[xt][c][:], func=AF.Square,
                                                                accum_out=ss[:, c:c + 1]), [XG[xt][c]], [JUNK, ss])
            S.op("dve", lambda e: e.tensor_reduce(out=ss[:, 0:1], in_=ss[:, 0:4], axis=AX.X, op=ALU.add), [ss], [ss])
            S.op("act", lambda e: e.activation(out=ss[:, 1:2], in_=ss[:, 0:1], func=AF.Sqrt, bias=EPSC, scale=1.0 / D), [ss, CST], [ss])
            S.op("dve", lambda e: e.reciprocal(out=ss[:, 2:3], in_=ss[:, 1:2]), [ss], [ss])
            for c in range(4):
                S.op("dve", lambda e, xt=xt, c=c: e.scalar_tensor_tensor(out=UN[:, c * 512:(c + 1) * 512], in0=XG[xt][c][:], scalar=ss[:, 2:3],
                                                                          in1=GB[:, c * 512:(c + 1) * 512], op0=ALU.mult, op1=ALU.mult),
                     [XG[xt][c], ss, GB], [UN])

            def evac(kc0, n, pv3, pos=pos):
                if pos < 2:
                    S.op("act", lambda e: e.activation(out=UT[:, kc0:kc0 + n, pos * 128:(pos + 1) * 128], in_=pv3, func=AF.Copy), [curp[0]], [UT])
                elif pos == 4:
                    S.op("act", lambda e: e.activation(out=UT3v[:, kc0:kc0 + n, :], in_=pv3, func=AF.Copy), [curp[0]], [UT3])
                else:
                    u2 = UT2[kc0 // 8]
                    S.op("act", lambda e: e.activation(out=UT2v[kc0 // 8][:, 0:n, (pos - 2) * 128:(pos - 1) * 128], in_=pv3, func=AF.Copy), [curp[0]], [u2])
            tr_bf(UN[:], 16, evac, [UN])

    curp = [None]

    def tr_bf(src2d, nblk, evac, rbufs):
        for b0 in range(0, nblk, 8):
            n = min(8, nblk - b0)
            p = ps()
            curp[0] = p
            pv = p[:].bitcast(BF16)
            for j in range(n):
                S.op("pe", lambda e, j=j, b0=b0: e.transpose(out=pv[:, j * 128:(j + 1) * 128], in_=src2d[:, (b0 + j) * 128:(b0 + j + 1) * 128],
                                                              identity=IDB[:]), rbufs + [IDB], [p])
            evac(b0, n, pv[:, 0:n * 128].rearrange("p (j c) -> p j c", j=n))

    def tr_f32(src_fn, nblk, evac, rbufs, rows=128):
        for b0 in range(0, nblk, 4):
            n = min(4, nblk - b0)
            p = ps()
            curp[0] = p
            for j in range(n):
                S.op("pe", lambda e, j=j, b0=b0: e.transpose(out=p[:, j * rows:(j + 1) * rows], in_=src_fn(b0 + j), identity=IDF[0:rows, 0:rows]),
                     rbufs + [CST], [p])
            evac(b0, n, p[:, 0:n * rows])

    def ffn(tiles, wg, wu, wd):
        ntl = len(tiles)
        T = min(ntl, 4) * 128
        T0 = min(T, 256)
        five = ntl == 5
        wgv = wg.rearrange("(kc p) f -> p kc f", p=128)
        wuv = wu.rearrange("(kc p) f -> p kc f", p=128)
        wdv = wd.rearrange("(f p) c -> p f c", p=128)

        def gu(pb, pb2, slot, j):
            for kc in range(16):
                S.op("pe", lambda e, kc=kc: e.matmul(pb[:, 0:T0], lhsT=slot[:, kc, j * 128:(j + 1) * 128], rhs=UT[:, kc, 0:T0],
                                                       start=(kc == 0), stop=(kc == 15)), [slot, UT], [pb])
            if T > 256:
                for kc in range(16):
                    S.op("pe", lambda e, kc=kc: e.matmul(pb[:, 256:T], lhsT=slot[:, kc, j * 128:(j + 1) * 128], rhs=UT2v[kc // 8][:, kc % 8, 0:T - 256],
                                                           start=(kc == 0), stop=(kc == 15)), [slot, UT2[kc // 8]], [pb])
            if five:
                for kc in range(16):
                    S.op("pe", lambda e, kc=kc: e.matmul(pb2[:, 0:128], lhsT=slot[:, kc, j * 128:(j + 1) * 128], rhs=UT3v[:, kc, :],
                                                           start=(kc == 0), stop=(kc == 15)), [slot, UT3], [pb2])

        for sl in range(NF // 4):
            ht, htv = HT[sl % 2], HTv[sl % 2]
            for bb in range(2):
                blk = sl * 2 + bb
                g_s = WA[wai[0] % 2]
                u_s = WA[2 + wai[0] % 2]
                wai[0] += 1
                wload(g_s, wgv[:, :, blk * 256:(blk + 1) * 256])
                wload(u_s, wuv[:, :, blk * 256:(blk + 1) * 256])
                for j in range(2):
                    fi = bb * 2 + j
                    pg, pu = ps8(), ps8()
                    pg2, pu2 = (ps8(), ps8()) if five else (None, None)
                    gu(pg, pg2, g_s, j)
                    gu(pu, pu2, u_s, j)
                    sg = SG[fi % 2]
                    S.op("act", lambda e: e.activation(out=sg[:, 0:T], in_=pg[:, 0:T], func=AF.Silu), [pg], [sg])
                    S.op("dve", lambda e, fi=fi: e.tensor_tensor(out=htv[:, fi, 0:T], in0=sg[:, 0:T], in1=pu[:, 0:T], op=ALU.mult), [sg, pu], [ht])
                    if five:
                        S.op("act", lambda e: e.activation(out=sg[:, 512:640], in_=pg2[:, 0:128], func=AF.Silu), [pg2], [sg])
                        S.op("dve", lambda e, fi=fi: e.tensor_tensor(out=HTBv[:, fi, :], in0=sg[:, 512:640], in1=pu2[:, 0:128], op=ALU.mult), [sg, pu2], [HTB])
            wds = []
            for fi in range(4):
                k = wbi[0] % len(WBS)
                wbi[0] += 1
                S.dma("pool", WBv[k], wdv[:, sl * 4 + fi, :], WBS[k], True)
                wds.append(k)
            for i, xt in enumerate(tiles):
                for c in range(4):
                    po = ps8()
                    for fi in range(4):
                        k = wds[fi]
                        lh = htv[:, fi, i * 128:(i + 1) * 128] if i < 4 else HTBv[:, fi, :]
                        hb = ht if i < 4 else HTB
                        S.op("pe", lambda e, fi=fi, c=c, k=k, lh=lh: e.matmul(po[:, :], lhsT=lh, rhs=WBv[k][:, c * 512:(c + 1) * 512],
                                                                                start=(fi == 0), stop=(fi == 3)), [hb, WBS[k]], [po])
                    S.op("dve", lambda e, xt=xt, c=c: e.scalar_tensor_tensor(out=XG[xt][c][:], in0=po[:, :], scalar=0.5, in1=XG[xt][c][:],
                                                                              op0=ALU.mult, op1=ALU.add), [po, XG[xt][c]], [XG[xt][c]])

    def proj_tm(nt, wv, col0, ncols, sink, nkc=16, src=None):
        src = UT if src is None else src
        for b0 in range(0, ncols, 256):
            nb = min(256, ncols - b0)
            slot = WA[wai[1] % 4]
            wai[1] += 1
            wload(slot, wv[:, :, col0 + b0:col0 + b0 + nb])
            for t in range(nt):
                p = ps()
                for kc in range(nkc):
                    S.op("pe", lambda e, kc=kc, t=t: e.matmul(p[:, 0:nb], lhsT=src[:, kc, t * 128:(t + 1) * 128], rhs=slot[:, kc, 0:nb],
                                                                start=(kc == 0), stop=(kc == nkc - 1)), [src, slot], [p])
                sink(t, b0, nb, p)

    def to_stg(t, b0, nb, p):
        S.op("act", lambda e: e.activation(out=STG[t][:, b0:b0 + nb], in_=p[:, 0:nb], func=AF.Copy), [p], [STG[t]])

    def add_resid(t, b0, nb, p):
        c = b0 // 512
        o = b0 % 512
        xt = cur_tb[0] + t
        S.op("dve", lambda e: e.tensor_tensor(out=XG[xt][c][:, o:o + nb], in0=p[:, 0:nb], in1=XG[xt][c][:, o:o + nb], op=ALU.add),
             [p, XG[xt][c]], [XG[xt][c]])

    def headnorm(t, which, H8=8, G=None, xb=None):
        G = GQ if G is None else G
        xb = STG[t] if xb is None else xb
        W = H8 * 128
        x3 = xb[:, 0:W].rearrange("p (h d) -> p h d", h=H8)
        s3 = STG2[:, 0:W].rearrange("p (h d) -> p h d", h=H8)
        sm = SM[t]
        S.op("dve", lambda e: e.tensor_tensor(out=STG2[:, 0:W], in0=xb[:, 0:W], in1=xb[:, 0:W], op=ALU.mult), [xb], [STG2])
        S.op("dve", lambda e: e.tensor_reduce(out=sm[:, 0:H8], in_=s3, axis=AX.X, op=ALU.add), [STG2], [sm])
        S.op("act", lambda e: e.activation(out=sm[:, 8:8 + H8], in_=sm[:, 0:H8], func=AF.Sqrt, bias=EPSC, scale=1.0 / 128), [sm, CST], [sm])
        S.op("dve", lambda e: e.reciprocal(out=sm[:, 16:16 + H8], in_=sm[:, 8:8 + H8]), [sm], [sm])
        S.op("dve", lambda e: e.tensor_tensor(out=x3, in0=x3, in1=sm[:, 16:16 + H8].unsqueeze(2).to_broadcast([128, H8, 128]), op=ALU.mult),
             [xb, sm], [xb])
        S.op("dve", lambda e: e.tensor_tensor(out=x3, in0=x3, in1=bc8(G[:, which, :], H8), op=ALU.mult), [xb, G], [xb])

    def stg_T(t, nh, dstv, dbuf, scale=1.0, xb=None):
        xb = STG[t] if xb is None else xb
        tr_f32(lambda j: xb[:, j * 128:(j + 1) * 128], nh,
               lambda b0, n, pv: S.op("act", lambda e: e.activation(out=dstv[:, b0:b0 + n, t * 128:(t + 1) * 128],
                                                                      in_=pv.rearrange("p (j c) -> p j c", j=n), func=AF.Copy, scale=scale),
                                      [curp[0]], [dbuf]), [xb])

    def to_buf(bufs):
        def sink(t, b0, nb, p):
            S.op("act", lambda e: e.activation(out=bufs[t][:, b0:b0 + nb], in_=p[:, 0:nb], func=AF.Copy), [p], [bufs[t]])
        return sink

    def ssd_tile(kind, t, want_y):
        sam = kind == "sam"
        tri = TRIBD if sam else TRI
        sel = SELS if sam else SELP
        c0 = t * 128
        dt = SA[:, 80:96]
        da, acol, altm, dte, eac = SA[:, 0:16], SA[:, 16:32], SA[:, 32:48], SA[:, 48:64], SA[:, 64:80]
        S.op("dve", lambda e: e.tensor_copy(out=dt, in_=DTT[t][:]), [DTT[t]], [SA])
        S.op("dve", lambda e: e.tensor_tensor(out=da, in0=dt, in1=ANEG[:], op=ALU.mult), [SA, ANEG], [SA])
        p1 = ps()
        S.op("pe", lambda e: e.matmul(p1[:, 0:16], lhsT=tri, rhs=da, start=True, stop=True), [CST, SA], [p1])
        S.op("dve", lambda e: e.tensor_copy(out=acol, in_=p1[:, 0:16]), [p1], [SA])
        S.op("pe", lambda e: e.matmul(p1[:, 16:32], lhsT=sel, rhs=acol, start=True, stop=True), [CST, SA], [p1])
        S.op("dve", lambda e: e.tensor_copy(out=altm, in_=p1[:, 16:32]), [p1], [SA])
        S.op("dve", lambda e: e.tensor_tensor(out=dte, in0=altm, in1=acol, op=ALU.subtract), [SA], [SA])
        S.op("act", lambda e: e.activation(out=dte, in_=dte, func=AF.Exp), [SA], [SA])
        S.op("act", lambda e: e.activation(out=eac, in_=acol, func=AF.Exp), [SA], [SA])
        xs3 = XS[t][:].rearrange("p (h d) -> p h d", h=16)
        S.op("dve", lambda e: e.tensor_tensor(out=XDT[:].rearrange("p (h d) -> p h d", h=16), in0=xs3,
                                              in1=dt.unsqueeze(2).to_broadcast([128, 16, 64]), op=ALU.mult), [XS[t], SA], [XDT])
        S.op("dve", lambda e: e.tensor_tensor(out=XDTE[:].rearrange("p (h d) -> p h d", h=16), in0=XDT[:].rearrange("p (h d) -> p h d", h=16),
                                              in1=dte.unsqueeze(2).to_broadcast([128, 16, 64]), op=ALU.mult), [XDT, SA], [XDTE])
        if want_y:
            S.op("dve", lambda e: e.tensor_tensor(out=XSD[:].rearrange("p (h d) -> p h d", h=16), in0=xs3,
                                                  in1=DBC[:].unsqueeze(2).to_broadcast([128, 16, 64]), op=ALU.mult), [XS[t], DBC], [XSD])
            pc = ps()
            for g in range(2):
                S.op("pe", lambda e, g=g: e.matmul(pc[:, g * 128:(g + 1) * 128], lhsT=BCT[:, g, c0:c0 + 128], rhs=BCT[:, 2 + g, c0:c0 + 128],
                                                    start=True, stop=True), [BCT], [pc])
            S.op("dve", lambda e: e.tensor_tensor(out=CBM[:], in0=pc[:, 0:256].rearrange("p (g c) -> p g c", g=2),
                                                  in1=tri.unsqueeze(1).to_broadcast([128, 2, 128]), op=ALU.mult), [pc, CST], [CBM])
            for hq in range(4):
                pa = ps()
                for j in range(4):
                    h = hq * 4 + j
                    S.op("pe", lambda e, j=j, h=h: e.matmul(pa[:, j * 128:(j + 1) * 128], lhsT=da[:, h:h + 1].to_broadcast([128, 128]), rhs=tri,
                                                              start=True, stop=True), [SA, CST], [pa])
                dec3 = DEC[:, 0:512].rearrange("p (j c) -> p j c", j=4)
                for j in range(4):
                    h = hq * 4 + j
                    S.op("dve", lambda e, j=j, h=h: e.tensor_scalar(out=dec3[:, j, :], in0=pa[:, j * 128:(j + 1) * 128], scalar1=acol[:, h:h + 1],
                                                                      scalar2=0.0, op0=ALU.subtract, op1=ALU.min), [pa, SA], [DEC])
                S.op("act", lambda e: e.activation(out=DEC[:, 0:512], in_=DEC[:, 0:512], func=AF.Exp), [DEC], [DEC])
                mt = MT[hq % 2]
                g = hq // 2
                S.op("dve", lambda e, g=g: e.tensor_tensor(out=mt[:], in0=dec3, in1=CBM[:, g, :].unsqueeze(1).to_broadcast([128, 4, 128]), op=ALU.mult),
                     [DEC, CBM], [mt])
                for j in range(4):
                    h = hq * 4 + j
                    pb = PS[4 + h // 8]
                    S.op("pe", lambda e, j=j, h=h, pb=pb: e.matmul(pb[:, (h % 8) * 64:(h % 8 + 1) * 64], lhsT=mt[:, j, :], rhs=XDT[:, h * 64:(h + 1) * 64],
                                                                     start=True, stop=True), [mt, XDT], [pb])
            if not sam:
                for g in range(2):
                    S.op("pe", lambda e, g=g: e.matmul(PS[6 + g][:, :], lhsT=BCT[:, 2 + g, c0:c0 + 128], rhs=HB[:, g * 512:(g + 1) * 512],
                                                        start=True, stop=True), [BCT, HB], [PS[6 + g]])
        pbt = ps()
        pbv = pbt[:].bitcast(BF16)
        for g in range(2):
            S.op("pe", lambda e, g=g: e.transpose(out=pbv[:, g * 128:(g + 1) * 128], in_=BCT[:, g, c0:c0 + 128], identity=IDB[:]), [BCT, IDB], [pbt])
        S.op("act", lambda e: e.activation(out=BTM[:].rearrange("p a b -> p (a b)"), in_=pbv[:, 0:256], func=AF.Copy), [pbt], [BTM])
        return da, acol, altm, dte, eac

    def ssd_finish_y(t, eac):
        y3 = YT[:].rearrange("p (h d) -> p h d", h=16)
        for g in range(2):
            S.op("dve", lambda e, g=g: e.tensor_tensor(out=y3[:, g * 8:(g + 1) * 8, :], in0=PS[6 + g][:, :].rearrange("p (h d) -> p h d", h=8),
                                                        in1=eac[:, g * 8:(g + 1) * 8].unsqueeze(2).to_broadcast([128, 8, 64]), op=ALU.mult),
                 [PS[6 + g], SA], [YT])
        S.op("dve", lambda e: e.tensor_tensor(out=YT[:], in0=YT[:], in1=XSD[:], op=ALU.add), [YT, XSD], [YT])
        for g in range(2):
            S.op("dve", lambda e, g=g: e.tensor_tensor(out=YT[:, g * 512:(g + 1) * 512], in0=PS[4 + g][:, :], in1=YT[:, g * 512:(g + 1) * 512], op=ALU.add),
                 [PS[4 + g], YT], [YT])
        S.op("dve", lambda e: e.tensor_tensor(out=YT[:], in0=YT[:], in1=ZS[t][:], op=ALU.mult), [YT, ZS[t]], [YT])
        ss = SS[t]
        S.dma("sp", GB[:, 0:1024], w["ssd_out_norm"].partition_broadcast(128), GB, True)
        S.op("act", lambda e: e.activation(out=STG2[:], in_=YT[:], func=AF.Square, accum_out=ss[:, 0:1]), [YT], [STG2, ss])
        S.op("act", lambda e: e.activation(out=ss[:, 1:2], in_=ss[:, 0:1], func=AF.Sqrt, bias=EPSC, scale=1.0 / 1024), [ss, CST], [ss])
        S.op("dve", lambda e: e.reciprocal(out=ss[:, 2:3], in_=ss[:, 1:2]), [ss], [ss])
        S.op("dve", lambda e: e.scalar_tensor_tensor(out=MIXv[t][:, 1024:2048], in0=YT[:], scalar=ss[:, 2:3], in1=GB[:, 0:1024], op0=ALU.mult, op1=ALU.mult),
             [YT, ss, GB], [MIXT[t]])

    def state_out(dst_view, hsrc, hbuf):
        ho = STG2
        tr_f32(lambda j: hsrc[:, j * 128:(j + 1) * 128], 8,
               lambda b0, n, pv: S.op("act", lambda e: e.activation(out=ho[:, b0 * 128:(b0 + n) * 128], in_=pv, func=AF.Copy), [curp[0]], [ho]), [hbuf])
        S.dma("sp", dst_view, ho[:].rearrange("p (c n) -> p c n", c=8), ho, False)

    def big_group(kind, bgi, with_sam=False):
        sam = kind == "sam"
        full = kind != "pre"
        ntot = 1 if sam else (NPRE if kind == "pre" else NOWN)
        ntl = 1 if sam else min(4, ntot - bgi * 4)
        xsrc = {"pre": x_pre, "own": x_own, "sam": x_sam}[kind]
        rb = bgi * 512
        tiles = list(range(ntl))
        for t in tiles:
            for c in range(4):
                S.dma("sp", XG[t][c][:], xsrc[rb + t * 128:rb + (t + 1) * 128, c * 512:(c + 1) * 512], XG[t][c], True)
        if with_sam:
            for c in range(4):
                S.dma("sp", XG[4][c][:], x_sam[0:128, c * 512:(c + 1) * 512], XG[4][c], True)
            tiles = tiles + [4]
        poss = list(range(ntl)) + ([4] if with_sam else [])
        for xt, pos in zip(tiles, poss):
            norm_T(1, w["ffn1_norm"], [xt], pos)
        ffn(tiles, w["ffn1_w_gate"], w["ffn1_w_up"], w["ffn1_w_down"])
        for sub in range((ntl + 1) // 2):
            cur_tb[0] = sub * 2
            mixer(kind, bgi * 2 + sub, min(2, ntl - sub * 2))
        if with_sam:
            cur_tb[0] = 4
            mixer("sam", 0, 1)
        cur_tb[0] = 0
        if not full:
            return
        for xt, pos in zip(tiles, poss):
            norm_T(1, w["ffn2_norm"], [xt], pos)
        ffn(tiles, w["ffn2_w_gate"], w["ffn2_w_up"], w["ffn2_w_down"])
        ydst = {"own": y_own, "sam": y_sam}[kind]
        for t in range(ntl):
            for c in range(4):
                S.dma("sp", ydst[rb + t * 128:rb + (t + 1) * 128, c * 512:(c + 1) * 512], XG[t][c][:], XG[t][c], False)
        if with_sam:
            for c in range(4):
                S.dma("sp", y_sam[0:128, c * 512:(c + 1) * 512], XG[4][c][:], XG[4][c], False)

    def mixer(kind, gi, nt):
        sam = kind == "sam"
        T = nt * 128
        full = kind != "pre"
        r0 = gi * 256
        norm_T(nt, w["mix_norm"])
        kdst = {"own": k_own, "sam": k_sam}.get(kind)
        vdst = {"own": v_own, "sam": v_sam}.get(kind)
        ldst = {"own": lf_own, "sam": lf_sam}.get(kind)
        kt0 = (0 if kind == "pre" else NPRE) + gi * 2
        proj_tm(nt, win_v, 1024, 1024, to_stg)
        proj_tm(nt, win_v, 2048, 1024, to_buf(XS))
        if full:
            proj_tm(nt, win_v, 0, 1024, to_buf(MIXT))
            proj_tm(nt, win_v, 3080, 1024,
                    lambda t, b0, nb, p: S.op("act", lambda e: e.activation(out=ZS[t][:, b0:b0 + nb], in_=p[:, 0:nb], func=AF.Silu), [p], [ZS[t]]))
        for t in range(nt):
            headnorm(t, 1)
            if kdst is not None:
                S.dma("sp", kdst[r0 + t * 128:r0 + (t + 1) * 128, :], STG[t][:], STG[t], False)
            stg_T(t, 8, KTGv, KTG)
        if not sam:
            for t in range(nt):
                S.dma("sp", kts[kt0 + t].rearrange("p (h c) -> p h c", h=8), KTGv[:, :, t * 128:(t + 1) * 128], KTG, False, mark=("k", kt0 + t))
        for t in range(nt):
            if vdst is not None:
                S.dma("sp", vdst[r0 + t * 128:r0 + (t + 1) * 128, :], XS[t][:], XS[t], False)
            S.op("pool", lambda e, t=t: e.tensor_copy(out=VAG[:, t, :, 0:128], in_=XS[t][:].rearrange("p (h d) -> p h d", h=8)), [XS[t]], [VAG])
            if not sam:
                S.dma("sp", vas[kt0 + t], VAG[:, t, :, :].rearrange("p h c -> p (h c)"), VAG, False, mark=("v", kt0 + t))
        if full:
            for t in range(nt):
                headnorm(t, 0, xb=MIXT[t])
                stg_T(t, 8, QTv, QT, scale=128 ** -0.5, xb=MIXT[t])
        for t in range(nt):
            p = ps()
            sm = SMF[t]
            for kc in range(16):
                S.op("pe", lambda e, kc=kc, t=t: e.matmul(p[:, 0:24], lhsT=UT[:, kc, t * 128:(t + 1) * 128], rhs=WFD[:, kc, :],
                                                            start=(kc == 0), stop=(kc == 15)), [UT, WFD], [p])
            S.op("dve", lambda e: e.tensor_tensor(out=sm[:, 0:8], in0=p[:, 0:8], in1=BFB[:], op=ALU.add), [p, BFB], [sm])
            S.op("dve", lambda e: e.tensor_tensor(out=sm[:, 32:48], in0=p[:, 8:24], in1=DTB[:], op=ALU.add), [p, DTB], [sm])
            S.op("act", lambda e: e.activation(out=sm[:, 8:16], in_=sm[:, 0:8], func=AF.Exp, scale=-1.0), [sm], [sm])
            S.op("act", lambda e: e.activation(out=sm[:, 48:64], in_=sm[:, 32:48], func=AF.Exp), [sm], [sm])
            S.op("act", lambda e: e.activation(out=sm[:, 16:24], in_=sm[:, 8:16], func=AF.Ln, bias=ONEC, scale=1.0), [sm, CST], [sm])
            S.op("act", lambda e, t=t: e.activation(out=DTT[t][:], in_=sm[:, 48:64], func=AF.Ln, bias=ONEC, scale=1.0), [sm, CST], [DTT[t]])
            S.op("dve", lambda e: e.tensor_scalar(out=sm[:, 24:32], in0=sm[:, 16:24], scalar1=-1.0, scalar2=None, op0=ALU.mult), [sm], [sm])
            if ldst is not None:
                S.dma("sp", ldst[r0 + t * 128:r0 + (t + 1) * 128, :], sm[:, 24:32], sm, False)
        if not sam:
            for t in range(nt):
                kt = kt0 + t
                p = ps()
                lf = SMF[t][:, 24:32]
                S.op("pe", lambda e: e.matmul(p[:, 0:8], lhsT=TRI, rhs=lf, start=True, stop=True), [CST, SMF[t]], [p])
                S.op("pe", lambda e: e.matmul(p[:, 8:16], lhsT=ONES, rhs=lf, start=True, stop=True), [CST, SMF[t]], [p])
                S.op("dve", lambda e, kt=kt: e.tensor_tensor(out=CKT[:, kt, :], in0=p[:, 0:8], in1=CARRY[:], op=ALU.add), [p, CARRY], [CKT])
                S.op("dve", lambda e: e.tensor_tensor(out=CARRY[:], in0=p[:, 8:16], in1=CARRY[:], op=ALU.add), [p, CARRY], [CARRY])
                S.op("dve", lambda e, t=t: e.tensor_copy(out=CREFS[:, t, :], in_=CARRY[:]), [CARRY], [CREFS])
        if sam:
            S.dma("sp", STG2[0:48, :], st_conv[:, 0:1024], STG2, True)
            S.dma("sp", STG[0][0:48, 0:512], st_conv[:, 1024:1536], STG[0], True)
        def conv_post(cc, p):
            for t in range(nt):
                xe = XE[t]
                xef = xe[:].rearrange("p a b -> p (a b)")
                ac = AC[t]
                if sam:
                    ph = ps()
                    hsrc = STG2[0:48, cc * 128:(cc + 1) * 128] if cc < 8 else STG[0][0:48, (cc - 8) * 128:(cc - 7) * 128]
                    hb = STG2 if cc < 8 else STG[0]
                    S.op("pe", lambda e: e.transpose(out=ph[:, 0:48], in_=hsrc, identity=IDF[0:48, 0:48]), [hb, CST], [ph])
                    S.op("dve", lambda e: e.tensor_copy(out=xe[:, :, 0:3], in_=ph[:, 0:48].rearrange("p (b j) -> p b j", j=3)), [ph], [xe])
                    S.op("act", lambda e: e.activation(out=xe[:, :, 3:11], in_=p[:, 0:128].rearrange("p (b j) -> p b j", j=8), func=AF.Copy), [p], [xe])
                    xin = [xe[:, :, j2:j2 + 8] for j2 in range(4)]
                    aco = ac[:].rearrange("p (b j) -> p b j", j=8)
                    pre_cols = xe[:, :, 8:11]
                else:
                    if t == 0:
                        S.op("dve", lambda e, cc=cc: e.tensor_copy(out=xef[:, 0:3], in_=HX[:, cc, :]), [HX], [xe])
                    else:
                        xp = XE[0][:].rearrange("p a b -> p (a b)")
                        S.op("dve", lambda e: e.tensor_copy(out=xef[:, 0:3], in_=xp[:, 128:131]), [XE[0]], [xe])
                    S.op("act", lambda e, t=t: e.activation(out=xef[:, 3:131], in_=p[:, t * 128:(t + 1) * 128], func=AF.Copy), [p], [xe])
                    if t == nt - 1:
                        S.op("dve", lambda e, cc=cc: e.tensor_copy(out=HX[:, cc, :], in_=xef[:, 128:131]), [xe], [HX])
                    xin = [xef[:, j2:j2 + 128] for j2 in range(4)]
                    aco = ac[:]
                S.op("dve", lambda e, cc=cc: e.tensor_scalar(out=aco, in0=xin[0], scalar1=CW[:, cc, 0:1], scalar2=CW[:, cc, 4:5], op0=ALU.mult, op1=ALU.add),
                     [xe, CW], [ac])
                for j2 in range(1, 4):
                    S.op("dve", lambda e, cc=cc, j2=j2: e.scalar_tensor_tensor(out=aco, in0=xin[j2], scalar=CW[:, cc, j2:j2 + 1], in1=aco, op0=ALU.mult, op1=ALU.add),
                         [xe, CW, ac], [ac])
                if cc < 8:
                    S.op("act", lambda e: e.activation(out=XCF[:], in_=ac[:], func=AF.Silu), [ac], [XCF])
                    pt_ = ps()
                    S.op("pe", lambda e: e.transpose(out=pt_[:, 0:128], in_=XCF[:], identity=IDF), [XCF, CST], [pt_])
                    S.op("dve", lambda e, cc=cc, t=t: e.tensor_copy(out=XS[t][:, cc * 128:(cc + 1) * 128], in_=pt_[:, 0:128]), [pt_], [XS[t]])
                else:
                    S.op("act", lambda e, cc=cc, t=t: e.activation(out=BCT[:, cc - 8, t * 128:(t + 1) * 128], in_=ac[:], func=AF.Silu), [ac], [BCT])
                last_prompt = (kind == "own" and gi == NOWN // 2 - 1 and t == nt - 1)
                if last_prompt or sam:
                    pt2 = ps()
                    if sam:
                        S.op("act", lambda e: e.activation(out=XCF[:].rearrange("p (b j) -> p b j", j=8), in_=xe[:, :, 3:11], func=AF.Copy), [xe], [XCF])
                    else:
                        S.op("act", lambda e: e.activation(out=XCF[:], in_=xef[:, 3:131], func=AF.Copy), [xe], [XCF])
                    S.op("pe", lambda e: e.transpose(out=pt2[:, 0:128], in_=XCF[:], identity=IDF), [XCF, CST], [pt2])
                    cvb = DEC if cc < 8 else XSD
                    S.op("dve", lambda e, cc=cc: e.tensor_copy(out=cvb[:, (cc % 8) * 128:(cc % 8 + 1) * 128], in_=pt2[:, 0:128]), [pt2], [cvb])
        pend = None
        for cc in range(13):
            cur = None
            if cc < 12:
                if cc % 2 == 0:
                    slot = WA[wai[1] % 4]
                    wai[1] += 1
                    wload(slot, win_v[:, :, 4104 + cc * 128:4104 + cc * 128 + 256])
                j = cc % 2
                p = PS[4 + cc % 2]
                for kc in range(16):
                    S.op("pe", lambda e, kc=kc, j=j, slot=slot, p=p: e.matmul(p[:, 0:T], lhsT=slot[:, kc, j * 128:(j + 1) * 128], rhs=UT[:, kc, 0:T],
                                                                                start=(kc == 0), stop=(kc == 15)), [slot, UT], [p])
                cur = (cc, p)
            if pend is not None:
                conv_post(*pend)
            pend = cur
        if kind == "own" and gi == NOWN // 2 - 1:
            S.dma("sp", conv_p[:, 0:1024], DEC[125:128, :], DEC, False)
            S.dma("sp", conv_p[:, 1024:1536], XSD[125:128, 0:512], XSD, False)
        if sam:
            for b in range(16):
                S.dma("sp", conv_s[b * 3:b * 3 + 3, 0:1024], DEC[b * 8 + 5:b * 8 + 8, :], DEC, False)
                S.dma("sp", conv_s[b * 3:b * 3 + 3, 1024:1536], XSD[b * 8 + 5:b * 8 + 8, 0:512], XSD, False)
        for t in range(nt):
            da, acol, altm, dte, eac = ssd_tile(kind, t, full)
            if not sam:
                if full:
                    ssd_finish_y(t, eac)
                p1 = ps()
                S.op("pe", lambda e: e.matmul(p1[:, 16:32], lhsT=ONES, rhs=da, start=True, stop=True), [CST, SA], [p1])
                S.op("act", lambda e: e.activation(out=CD[:], in_=p1[:, 16:32], func=AF.Exp), [p1], [CD])
                for g in range(2):
                    S.op("pe", lambda e, g=g: e.matmul(PS[4 + g][:, :], lhsT=BTM[:, g, :], rhs=XDTE[:, g * 512:(g + 1) * 512], start=True, stop=True),
                         [BTM, XDTE], [PS[4 + g]])
                h3 = H[:].rearrange("p (h d) -> p h d", h=16)
                S.op("dve", lambda e: e.tensor_tensor(out=h3, in0=h3, in1=CD[:].unsqueeze(2).to_broadcast([128, 16, 64]), op=ALU.mult), [H, CD], [H])
                for g in range(2):
                    S.op("dve", lambda e, g=g: e.tensor_tensor(out=H[:, g * 512:(g + 1) * 512], in0=PS[4 + g][:, :], in1=H[:, g * 512:(g + 1) * 512], op=ALU.add),
                         [PS[4 + g], H], [H])
                S.op("act", lambda e: e.activation(out=HB[:], in_=H[:], func=AF.Copy), [H], [HB])
            else:
                sample_ssd(acol, eac)
        if kind == "pre" and gi == NPRE // 2 - 1:
            S.op("dve", lambda e: e.tensor_scalar(out=H[:], in0=H[:], scalar1=FLG[:, 0:1], scalar2=None, op0=ALU.mult), [H, FLG], [H])
            S.op("act", lambda e: e.activation(out=HB[:], in_=H[:], func=AF.Copy), [H], [HB])
        if kind == "own" and gi == NOWN // 2 - 1:
            state_out(ssm_p.rearrange("(c p) n -> p c n", p=128), H, H)
        if not full:
            return
        for t in range(nt):
            if sam:
                sample_attn()
            else:
                prompt_attn(gi, t)
        for t in range(nt):
            tr_bf(MIXv[t], 16, lambda kc0, n, pv3, t=t: S.op("act", lambda e: e.activation(
                out=UT[:, kc0:kc0 + n, t * 128:(t + 1) * 128], in_=pv3, func=AF.Copy), [curp[0]], [UT]), [MIXT[t]])
        proj_tm(nt, wout_v, 0, D, add_resid)
        norm_T(nt, w["xattn_norm"])
        proj_tm(nt, wq_v, 0, 512, to_stg)
        for t in range(nt):
            headnorm(t, 0, 4, GX)
            stg_T(t, 4, QTv, QT, scale=128 ** -0.5)
        for t in range(nt):
            xattn(sam, t)
        for t in range(nt):
            tr_bf(MIXv[t][:, 0:512], 4, lambda kc0, n, pv3, t=t: S.op("act", lambda e: e.activation(
                out=UT[:, kc0:kc0 + n, t * 128:(t + 1) * 128], in_=pv3, func=AF.Copy), [curp[0]], [UT]), [MIXT[t]])
        proj_tm(nt, wo_v, 0, D, add_resid, nkc=4)

    OB = [PS[4], PS[5], PS[6]]

    def o_region(h, n=129):
        return OB[h // 3][:, (h % 3) * 129:(h % 3) * 129 + n]

    def attn_finish(t, nheads, width=128):
        for h in range(nheads):
            S.op("dve", lambda e, h=h: e.reciprocal(out=RD[:, h:h + 1], in_=o_region(h)[:, 128:129]), [OB[h // 3]], [RD])
        for h in range(nheads):
            S.op("act", lambda e, h=h: e.activation(out=MIXv[t][:, h * 128:(h + 1) * 128], in_=o_region(h, 128), func=AF.Copy, scale=RD[:, h:h + 1]),
                 [OB[h // 3], RD], [MIXT[t]])

    kci = [0]

    def prompt_attn(gi, t):
        oi = gi * 2 + t
        nk = NPRE + oi + 1
        S.op("dve", lambda e: e.tensor_tensor(out=BIAS[:, 0:nk, :], in0=CREFS[:, t, :].unsqueeze(1).to_broadcast([128, nk, 8]), in1=CKT[:, 0:nk, :],
                                              op=ALU.subtract), [CREFS, CKT], [BIAS])
        if NPRE > 0:
            S.op("dve", lambda e: e.tensor_scalar(out=BIAS[:, 0:NPRE, :], in0=BIAS[:, 0:NPRE, :], scalar1=FLG[:, 1:2], scalar2=None, op0=ALU.add),
                 [BIAS, FLG], [BIAS])
        for ob in OB:
            S.op("dve", lambda e, ob=ob: e.memset(ob[:, :], 0.0), [], [ob])
        for kt in range(nk):
            kc_, vc_ = KC[kci[0] % 2], VC[kci[0] % 2]
            kci[0] += 1
            S.dma("sp", kc_[:], kts[kt].rearrange("p (h c) -> p h c", h=8), kc_, True, after=("k", kt))
            S.dma("sp", vc_[:].rearrange("p h c -> p (h c)"), vas[kt], vc_, True, after=("v", kt))
            diag = kt == nk - 1
            for hq in range(2):
                p = ps()
                pt = PT[(kci[0] * 2 + hq) % 4]
                for j in range(4):
                    h = hq * 4 + j
                    S.op("pe", lambda e, j=j, h=h: e.matmul(p[:, j * 128:(j + 1) * 128], lhsT=kc_[:, h, :], rhs=QTv[:, h, t * 128:(t + 1) * 128],
                                                              start=True, stop=not diag), [kc_, QT], [p])
                    if diag:
                        S.op("pe", lambda e, j=j: e.matmul(p[:, j * 128:(j + 1) * 128], lhsT=IDB[:], rhs=MNEG[:, 0, :], start=False, stop=True),
                             [IDB, MNEG], [p])
                for j in range(4):
                    h = hq * 4 + j
                    S.op("act", lambda e, j=j, h=h, kt=kt: e.activation(out=pt[:, j, :], in_=p[:, j * 128:(j + 1) * 128], func=AF.Exp,
                                                                          bias=BIAS[:, kt, h:h + 1], scale=1.0), [p, BIAS], [pt])
                for j in range(4):
                    h = hq * 4 + j
                    S.op("pe", lambda e, j=j, h=h: e.matmul(o_region(h), lhsT=pt[:, j, :], rhs=vc_[:, h, 0:129], start=False, stop=(kt == nk - 1),
                                                              skip_group_check=True), [pt, vc_], [OB[h // 3]])
        attn_finish(t, 8)

    def xattn(sam, t):
        for ob in OB[0:2]:
            S.op("dve", lambda e, ob=ob: e.memset(ob[:, :], 0.0), [], [ob])
        if not sam:
            for mt_ in range(2):
                p = ps()
                pt = PT[mt_ % 4]
                for h in range(4):
                    S.op("pe", lambda e, h=h: e.matmul(p[:, h * 128:(h + 1) * 128], lhsT=MKT[:, h, mt_ * 128:(mt_ + 1) * 128],
                                                        rhs=QTv[:, h, t * 128:(t + 1) * 128], start=True, stop=True), [MKT, QT], [p])
                S.op("act", lambda e: e.activation(out=pt[:].rearrange("p a b -> p (a b)"), in_=p[:, :], func=AF.Exp), [p], [pt])
                for h in range(4):
                    S.op("pe", lambda e, h=h: e.matmul(o_region(h), lhsT=pt[:, h, :], rhs=MVA[:, mt_, h, 0:129], start=False, stop=(mt_ == 1),
                                                        skip_group_check=True), [pt, MVA], [OB[h // 3]])
        else:
            sample_xattn()
        attn_finish(t, 4)

    IDX = S.sb([128, 16 * NPG], I32, "IDX")
    IDXF = STG2
    if SAM:
        S.dma("sp", IDX[:], ptab.partition_broadcast(128), IDX, True)
        S.op("dve", lambda e: e.tensor_copy(out=IDXF[:, 0:16 * NPG], in_=IDX[:]), [IDX], [IDXF])
        S.op("dve", lambda e: e.tensor_scalar(out=IDXF[:, 0:16 * NPG], in0=IDXF[:, 0:16 * NPG], scalar1=128.0, scalar2=PIDX, op0=ALU.mult, op1=ALU.add), [IDXF, CST], [IDXF])
        S.op("dve", lambda e: e.tensor_copy(out=IDX[:], in_=IDXF[:, 0:16 * NPG]), [IDXF], [IDX])

    U32 = mybir.dt.uint32
    TMPS = [S.sb([128, 64], F32, f"TMPS{i}") for i in range(2)]
    LFP = [S.sb([128, 16, 8], F32, f"LFP{i}") for i in range(2)]
    CPX = S.sb([128, 17, 8], F32, "CPX")
    TOTP = S.sb([128, 16, 8], F32, "TOTP")
    BIASPS = [S.sb([128, 16, 8], F32, f"BIASP{i}") for i in range(2)]
    NEWTOT = S.sb([128, 16, 8], F32, "NEWTOT")
    LFB = S.sb([128, 16, 16], F32, "LFB")
    CDS = S.sb([128, 16, 16], F32, "CDS")
    BIASN = S.sb([128, 24], F32, "BIASN")
    CTPB = [S.sb([128, 2, 128], BF16, f"CTPB{i}") for i in range(2)]
    BTMB = [S.sb([128, 2, 128], BF16, f"BTMB{i}") for i in range(2)]

    def sample_ssd(acol, eac):
        da = SA[:, 0:16]
        S.op("dve", lambda e: e.tensor_tensor(out=LFB[:], in0=da.unsqueeze(1).to_broadcast([128, 16, 16]),
                                              in1=BDSEL.unsqueeze(2).to_broadcast([128, 16, 16]), op=ALU.mult), [SA, CST], [LFB])
        pc = ps()
        S.op("pe", lambda e: e.matmul(pc[:, 0:256], lhsT=ONES, rhs=LFB[:].rearrange("p a b -> p (a b)"), start=True, stop=True), [CST, LFB], [pc])
        S.op("act", lambda e: e.activation(out=CDS[:].rearrange("p a b -> p (a b)"), in_=pc[:, 0:256], func=AF.Exp), [pc], [CDS])
        H0L = [SCR[1], SCR[4]]
        H0T = [SCR[6], SCR[10]]
        HBS = [(HB, HB[:]), (H, H[:].bitcast(BF16)[:, 0:1024])]
        for b in range(16):
            h0l, h0t = H0L[b % 2], H0T[b % 2]
            hbb, hbv = HBS[b % 2]
            S.dma("sp", h0l[:].rearrange("p (c n) -> p c n", c=8), st_ssm[b].rearrange("(c p) n -> p c n", p=128), h0l, True)
            tr_f32(lambda j: h0l[:, j * 128:(j + 1) * 128], 8,
                   lambda b0, n, pv: S.op("act", lambda e: e.activation(out=h0t[:, b0 * 128:(b0 + n) * 128], in_=pv, func=AF.Copy), [curp[0]], [h0t]), [h0l])
            S.op("dve", lambda e: e.tensor_copy(out=hbv, in_=h0t[:]), [h0t], [hbb])
            ctp, btm = CTPB[b % 2], BTMB[b % 2]
            S.op("pool", lambda e: e.memset(ctp[:], 0.0), [], [ctp])
            S.op("pool", lambda e, b=b: e.tensor_copy(out=ctp[:, :, b * 8:(b + 1) * 8], in_=BCT[:, 2:4, b * 8:(b + 1) * 8]), [BCT], [ctp])
            S.op("dve", lambda e, b=b: e.tensor_scalar(out=btm[:].rearrange("p a b -> p (a b)"), in0=BTM[:].rearrange("p a b -> p (a b)"),
                                                         scalar1=BDSEL[:, b:b + 1], scalar2=None, op0=ALU.mult), [BTM, CST], [btm])
            for g in range(2):
                S.op("pe", lambda e, g=g, b=b: e.matmul(PS[6 + g][:, :], lhsT=ctp[:, g, :], rhs=hbv[:, g * 512:(g + 1) * 512], start=(b == 0), stop=(b == 15)),
                     [ctp, hbb], [PS[6 + g]])
            pS = [ps(), ps()]
            for g in range(2):
                S.op("pe", lambda e, g=g: e.matmul(pS[g][:, :], lhsT=btm[:, g, :], rhs=XDTE[:, g * 512:(g + 1) * 512], start=True, stop=True), [btm, XDTE], [pS[g]])
            h3 = h0t[:].rearrange("p (h d) -> p h d", h=16)
            S.op("dve", lambda e, b=b: e.tensor_tensor(out=h3, in0=h3, in1=CDS[:, b, :].unsqueeze(2).to_broadcast([128, 16, 64]), op=ALU.mult), [h0t, CDS], [h0t])
            for g in range(2):
                S.op("dve", lambda e, g=g: e.tensor_tensor(out=h0t[:, g * 512:(g + 1) * 512], in0=pS[g][:, :], in1=h0t[:, g * 512:(g + 1) * 512], op=ALU.add),
                     [pS[g], h0t], [h0t])
            state_out(ssm_s[b].rearrange("(c p) n -> p c n", p=128), h0t, h0t)
        ssd_finish_y(0, eac)

    def sample_attn():
        lf = SMF[0][:, 24:32]
        for ob in OB:
            S.op("dve", lambda e, ob=ob: e.memset(ob[:, :], 0.0), [], [ob])
        S.op("dve", lambda e: e.tensor_tensor(out=LFB[:, :, 0:8], in0=lf.unsqueeze(1).to_broadcast([128, 16, 8]),
                                              in1=BDSEL.unsqueeze(2).to_broadcast([128, 16, 8]), op=ALU.mult), [SMF[0], CST], [LFB])
        p = ps()
        S.op("pe", lambda e: e.matmul(p[:, 0:128].rearrange("p (a b) -> p a b", a=16), lhsT=ONES, rhs=LFB[:, :, 0:8], start=True, stop=True), [CST, LFB], [p])
        S.op("pe", lambda e: e.matmul(p[:, 128:136], lhsT=TRIBD, rhs=lf, start=True, stop=True), [CST, SMF[0]], [p])
        S.op("dve", lambda e: e.tensor_copy(out=NEWTOT[:].rearrange("p a b -> p (a b)"), in_=p[:, 0:128]), [p], [NEWTOT])
        S.op("dve", lambda e: e.tensor_copy(out=BIASN[:, 0:8], in_=p[:, 128:136]), [p], [BIASN])
        S.op("pe", lambda e: e.matmul(p[:, 136:144], lhsT=SELS, rhs=BIASN[:, 0:8], start=True, stop=True), [CST, BIASN], [p])
        S.op("dve", lambda e: e.tensor_tensor(out=BIASN[:, 8:16], in0=p[:, 136:144], in1=BIASN[:, 0:8], op=ALU.subtract), [p, BIASN], [BIASN])
        for hq in range(2):
            p = ps()
            pt = PT[hq]
            for j in range(4):
                h = hq * 4 + j
                S.op("pe", lambda e, j=j, h=h: e.matmul(p[:, j * 128:(j + 1) * 128], lhsT=KTGv[:, h, 0:128], rhs=QTv[:, h, 0:128], start=True, stop=False), [KTG, QT], [p])
                S.op("pe", lambda e, j=j: e.matmul(p[:, j * 128:(j + 1) * 128], lhsT=IDB[:], rhs=MNEG[:, 1, :], start=False, stop=True), [IDB, MNEG], [p])
            for j in range(4):
                h = hq * 4 + j
                S.op("act", lambda e, j=j, h=h: e.activation(out=pt[:, j, :], in_=p[:, j * 128:(j + 1) * 128], func=AF.Exp, bias=BIASN[:, 8 + h:9 + h], scale=1.0),
                     [p, BIASN], [pt])
            for j in range(4):
                h = hq * 4 + j
                S.op("pe", lambda e, j=j, h=h: e.matmul(o_region(h), lhsT=pt[:, j, :], rhs=VAG[:, 0, h, 0:129], start=False, stop=False, skip_group_check=True),
                     [pt, VAG], [OB[h // 3]])
        NS = 4
        KVS = WA
        KVv = [k_[:].bitcast(F32).rearrange("p a b -> p (a b)") for k_ in KVS]
        KTP = [XS[0], XS[1], SCR[4], SCR[0]]
        VPB = [(VC[0], VC[0][:]), (VC[1], VC[1][:]), (MVA, MVA[:].rearrange("p a b c -> p (a b) c")),
               (SCR[1], SCR[1][:].bitcast(BF16)[:, 0:1040].rearrange("p (h c) -> p h c", h=8))]
        S.op("dve", lambda e: e.memset(SCR[1][:].bitcast(BF16)[:, 0:1040], 1.0), [], [SCR[1]])
        PTZ = [(PT[0], PT[1]), (PT[2], PT[3])]
        pages = [(b, j) for b in range(16) for j in range(NPG)]
        NP_ = len(pages)
        lastb = [None, None]
        ktvs = {}

        def seq_setup(b):
            lfp = LFP[b % 2]
            for j in range(NPG):
                col = b * NPG + j
                S.dma("pool", lfp[:, j, :], pool_lf[:, :], lfp, True, indirect=bass.IndirectOffsetOnAxis(ap=IDX[:, col:col + 1].bitcast(U32), axis=0), reads=[IDX],
                      disjoint=(j > 0))
            p = ps()
            lfp2 = lfp[:, 0:NPG, :]
            S.op("pe", lambda e: e.matmul(p[:, 0:NPG * 8].rearrange("p (a b) -> p a b", a=NPG), lhsT=ONES, rhs=lfp2, start=True, stop=True), [CST, lfp], [p])
            S.op("pe", lambda e: e.matmul(p[:, 128:128 + NPG * 8].rearrange("p (a b) -> p a b", a=NPG), lhsT=TRI, rhs=lfp2, start=True, stop=True), [CST, lfp], [p])
            S.op("dve", lambda e: e.tensor_copy(out=TOTP[:, 0:NPG, :].rearrange("p a b -> p (a b)"), in_=p[:, 0:NPG * 8]), [p], [TOTP])
            S.op("dve", lambda e: e.memset(CPX[:, 0, :], 0.0), [], [CPX])
            for j in range(NPG):
                S.op("dve", lambda e, j=j: e.tensor_tensor(out=CPX[:, j + 1, :], in0=CPX[:, j, :], in1=TOTP[:, j, :], op=ALU.add), [CPX, TOTP], [CPX])
            bp = BIASPS[b % 2]
            S.op("dve", lambda e, b=b: e.tensor_tensor(out=BIASN[:, 16:24], in0=CPX[:, NPG, :], in1=NEWTOT[:, b, :], op=ALU.add), [CPX, NEWTOT], [BIASN])
            S.op("dve", lambda e: e.tensor_tensor(out=bp[:, 0:NPG, :], in0=BIASN[:, 16:24].unsqueeze(1).to_broadcast([128, NPG, 8]), in1=CPX[:, 0:NPG, :],
                                                  op=ALU.subtract), [BIASN, CPX], [bp])
            S.op("dve", lambda e: e.tensor_tensor(out=bp[:, 0:NPG, :], in0=bp[:, 0:NPG, :], in1=p[:, 128:128 + NPG * 8].rearrange("p (a b) -> p a b", a=NPG),
                                                  op=ALU.subtract), [bp, p], [bp])

        def stage_T(i):
            b, j = pages[i]
            if j == 0:
                seq_setup(b)
            col = b * NPG + j
            kv, kvv, ktp = KVS[i % NS], KVv[i % NS], KTP[i % NS]
            vb, vv = VPB[i % NS]
            off = bass.IndirectOffsetOnAxis(ap=IDX[:, col:col + 1].bitcast(U32), axis=0)
            S.dma("pool", kvv, pool_kv[:, :], kv, True, indirect=off, reads=[IDX])
            ktv = ktp[:].bitcast(BF16)[:, 0:1024].rearrange("p (h c) -> p h c", h=8)
            ktvs[i] = ktv
            tr_f32(lambda jj: kvv[:, jj * 128:(jj + 1) * 128], 8,
                   lambda b0, n, pv: S.op("act", lambda e: e.activation(out=ktv[:, b0:b0 + n, :], in_=pv.rearrange("p (j c) -> p j c", j=n), func=AF.Copy),
                                          [curp[0]], [ktp]), [kv])
            S.op("dve", lambda e: e.tensor_copy(out=vv[:, :, 0:128], in_=kvv[:, 1024:2048].rearrange("p (h d) -> p h d", h=8)), [kv], [vb])

        def stage_Q(i):
            b, j = pages[i]
            ktp, ktv = KTP[i % NS], ktvs[i]
            ptz = PTZ[i % 2]
            tmp = TMPS[i % 2]
            if lastb[i % 2] is not None and lastb[i % 2] != b:
                ob = lastb[i % 2]
                for z_ in ptz:
                    S.op("dve", lambda e, z_=z_, ob=ob: e.memset(z_[:, :, ob * 8:(ob + 1) * 8], 0.0), [], [z_])
            lastb[i % 2] = b
            p = ps()
            for h in range(8):
                S.op("pe", lambda e, h=h, b=b: e.matmul(p[:, h * 8:(h + 1) * 8], lhsT=ktv[:, h, :], rhs=QTv[:, h, b * 8:(b + 1) * 8], start=True, stop=True),
                     [ktp, QT], [p])
            S.op("dve", lambda e, j=j, b=b: e.tensor_tensor(out=tmp[:].rearrange("p (h q) -> p h q", h=8), in0=p[:, 0:64].rearrange("p (h q) -> p h q", h=8),
                                                             in1=BIASPS[b % 2][:, j, :].unsqueeze(2).to_broadcast([128, 8, 8]), op=ALU.add), [p, BIASPS[b % 2]], [tmp])
            for hh in range(2):
                S.op("act", lambda e, hh=hh, b=b: e.activation(out=ptz[hh][:, :, b * 8:(b + 1) * 8], in_=tmp[:, hh * 32:(hh + 1) * 32].rearrange("p (h q) -> p h q", h=4),
                                                                 func=AF.Exp), [tmp], [ptz[hh]])

        def stage_P(i):
            ptz = PTZ[i % 2]
            vb, vv = VPB[i % NS]
            for h in range(8):
                S.op("pe", lambda e, h=h: e.matmul(o_region(h), lhsT=ptz[h // 4][:, h % 4, :], rhs=vv[:, h, 0:129], start=False, stop=False, skip_group_check=True),
                     [ptz[h // 4], vb], [OB[h // 3]])

        for pt in PT:
            S.op("dve", lambda e, pt=pt: e.memset(pt[:], 0.0), [], [pt])
        for i in range(NP_ + 3):
            if i < NP_:
                stage_T(i)
            if 0 <= i - 2 < NP_:
                stage_Q(i - 2)
            if 0 <= i - 3 < NP_:
                stage_P(i - 3)
        attn_finish(0, 8)

    def sample_xattn():
        MKL = [SCR[0], SCR[1]]
        MVL = [SCR[3], SCR[4]]
        MKTB = [XS[0], XS[1]]
        MVAB = [(MVA, MVA[:]), (VC[0], VC[0][:].rearrange("p (a b) c -> p a b c", a=2))]
        for b in range(16):
            mkl, mvl, mktb = MKL[b % 2], MVL[b % 2], MKTB[b % 2]
            mvb, mvv = MVAB[b % 2]
            ptz = (PT[0], PT[1]) if b % 2 == 0 else (PT[2], PT[3])
            S.dma("sp", mkl[:].rearrange("p (t c) -> p t c", t=2), cmem_k[b].rearrange("(t p) c -> p t c", p=128), mkl, True)
            S.dma("sp", mvl[:].rearrange("p (t c) -> p t c", t=2), cmem_v[b].rearrange("(t p) c -> p t c", p=128), mvl, True)
            mkv = mktb[:].bitcast(BF16)[:, 0:1024].rearrange("p (h c) -> p h c", h=4)
            for mt_ in range(2):
                tr_f32(lambda jj, mt_=mt_: mkl[:, mt_ * 512 + jj * 128:mt_ * 512 + (jj + 1) * 128], 4,
                       lambda b0, n, pv, mt_=mt_: S.op("act", lambda e: e.activation(out=mkv[:, b0:b0 + n, mt_ * 128:(mt_ + 1) * 128],
                                                                                      in_=pv.rearrange("p (j c) -> p j c", j=n), func=AF.Copy), [curp[0]], [mktb]), [mkl])
            S.op("pool", lambda e: e.tensor_copy(out=mvv[:, :, :, 0:128], in_=mvl[:].rearrange("p (t h d) -> p t h d", t=2, h=4)), [mvl], [mvb])
            for z_ in ptz:
                S.op("pool", lambda e, z_=z_: e.memset(z_[:], 0.0), [], [z_])
            p = ps()
            for mt_ in range(2):
                for h in range(4):
                    S.op("pe", lambda e, h=h, mt_=mt_, b=b: e.matmul(p[:, mt_ * 32 + h * 8:mt_ * 32 + (h + 1) * 8], lhsT=mkv[:, h, mt_ * 128:(mt_ + 1) * 128],
                                                                       rhs=QTv[:, h, b * 8:(b + 1) * 8], start=True, stop=True), [mktb, QT], [p])
            for mt_ in range(2):
                S.op("act", lambda e, mt_=mt_, b=b: e.activation(out=ptz[mt_][:, :, b * 8:(b + 1) * 8], in_=p[:, mt_ * 32:(mt_ + 1) * 32].rearrange("p (h q) -> p h q", h=4),
                                                                   func=AF.Exp), [p], [ptz[mt_]])
            for mt_ in range(2):
                for h in range(4):
                    S.op("pe", lambda e, h=h, mt_=mt_: e.matmul(o_region(h), lhsT=ptz[mt_][:, h, :], rhs=mvv[:, mt_, h, 0:129], start=False, stop=False,
                                                                  skip_group_check=True), [ptz[mt_], mvb], [OB[h // 3]])

    def memkv():
        for t in range(2):
            for c in range(4):
                S.dma("sp", XG[t][c][:], mem_in[t * 128:(t + 1) * 128, c * 512:(c + 1) * 512], XG[t][c], True)
        norm_T(2, w["mem_norm"])
        proj_tm(2, wkv_v, 0, 512, to_stg)
        for t in range(2):
            headnorm(t, 1, 4, GX)
            S.dma("sp", memk_o[t * 128:(t + 1) * 128, :], STG[t][:, 0:512], STG[t], False)
            stg_T(t, 4, MKT, MKT)
        proj_tm(2, wkv_v, 512, 512, to_stg)
        for t in range(2):
            S.dma("sp", memv_o[t * 128:(t + 1) * 128, :], STG[t][:, 0:512], STG[t], False)
            S.op("pool", lambda e, t=t: e.tensor_copy(out=MVA[:, t, :, 0:128], in_=STG[t][:, 0:512].rearrange("p (h d) -> p h d", h=4)), [STG[t]], [MVA])

    memkv()
    for bgi in range((NPRE + 3) // 4):
        big_group("pre", bgi)
    nbo = (NOWN + 3) // 4
    merge = SAM and (NOWN - (nbo - 1) * 4) == 4
    for bgi in range(nbo):
        big_group("own", bgi, with_sam=(merge and bgi == nbo - 1))
    if SAM and not merge:
        big_group("sam", 0)
    S.finish()
    return nc


def make_consts():
    c = np.zeros((128, 1536), np.float32)
    k = np.arange(128)
    le = k[:, None] <= k[None, :]
    same = (k[:, None] // 8) == (k[None, :] // 8)
    c[:, 0:128] = np.eye(128)
    c[:, 128:256] = le
    c[:, 256:384] = le & same
    c[:, 384:512] = 1.0
    c[:, 512:640] = np.where(le, 0.0, NEG)
    c[:, 640:768] = np.where(le & same, 0.0, NEG)
    c[:, 768:896] = (k[:, None] == 127)
    c[:, 896:1024] = (k[:, None] == (k[None, :] // 8) * 8 + 7)
    c[:, 1024:1040] = (k[:, None] // 8) == np.arange(16)[None, :]
    c[:, 1040] = EPS
    c[:, 1041] = 1.0
    c[:, 1042] = k
    c2 = np.zeros((128, 16, 128), np.float32)
    c2[:] = ((k[None, :] // 8) == np.arange(16)[:, None])[None]
    return c, c2.reshape(128, 2048)


def core_inputs(inp, c, cfg, cst, cst2):
    s, h = c // 2, c % 2
    xp = inp["x_prompt"]
    xs = inp["x_sample"]
    npre, nown = cfg["npre"] * 128, cfg["nown"] * 128
    f32 = lambda a: np.ascontiguousarray(np.asarray(a, np.float32))
    flg = np.zeros((128, 4), np.float32)
    flg[:, 0] = 1.0 if h == 1 else 0.0
    flg[:, 1] = 0.0 if h == 1 else NEG
    m = {"x_own": f32(xp[s, h * nown:(h + 1) * nown]),
         "x_pre": f32(xp[s, 0:npre]) if h == 1 else np.zeros((npre, D), np.float32),
         "x_sam": f32(xs[c * 16:(c + 1) * 16]).reshape(128, D),
         "mem_in": f32(inp["mem_prompt"][s]),
         "cst": cst, "cst2": cst2, "flg": flg,
         "cwb": np.ascontiguousarray(np.concatenate([np.asarray(inp["conv_w"], np.float32)[0], np.asarray(inp["conv_b"], np.float32)], axis=0)),
         "pool_kv": inp["_pool_kv"], "pool_lf": inp["_pool_lf"],
         "ptab": np.ascontiguousarray(np.asarray(inp["page_table"], np.int32)[c * 16:(c + 1) * 16]).reshape(-1),
         "cmem_k": f32(inp["cache_mem_k"][0, c * 16:(c + 1) * 16]).reshape(16, 256, 512),
         "cmem_v": f32(inp["cache_mem_v"][0, c * 16:(c + 1) * 16]).reshape(16, 256, 512),
         "st_ssm": f32(inp["state_ssm"][0, c * 16:(c + 1) * 16]).reshape(16, 1024, 128),
         "st_conv": f32(inp["state_conv"][0, c * 16:(c + 1) * 16]).reshape(48, 1536)}
    for n in _WNAMES:
        m[n] = f32(inp[n][0])
    return m


def kernel(**inp):
    inp = dict(inp)
    npool = inp["cache_fox_k"].shape[1]
    inp["_pool_kv"] = np.concatenate([np.asarray(inp["cache_fox_k"], np.float32)[0].reshape(npool * 128, 1024),
                                      np.asarray(inp["cache_fox_v"], np.float32)[0].reshape(npool * 128, 1024)], axis=1)
    inp["_pool_lf"] = np.ascontiguousarray(np.asarray(inp["cache_fox_logf"], np.float32)[0]).reshape(npool * 128, 8)
    cfg = {"npre": 8, "nown": 8, "npages": inp["page_table"].shape[1], "pool_rows": npool * 128}
    nc = build(cfg)
    cst, cst2 = make_consts()
    in_maps = [core_inputs(inp, c, cfg, cst, cst2) for c in range(NCORES)]
    res = run_bass_kernel_spmd(nc, in_maps, core_ids=list(range(NCORES))).results
    B, L = 4, 2048
    z = lambda *s: np.zeros(s, np.float32)
    y_p, pk, pv, plf = z(B, L, D), z(1, B, L, 8, 128), z(1, B, L, 8, 128), z(1, B, L, 8)
    pssm, pconv, pmk, pmv = z(1, B, 16, 64, 128), z(1, B, 3, 1536), z(1, B, 256, 4, 128), z(1, B, 256, 4, 128)
    y_s, sk, sv, slf = z(128, 8, D), z(1, 128, 8, 8, 128), z(1, 128, 8, 8, 128), z(1, 128, 8, 8)
    sssm, sconv = z(1, 128, 16, 64, 128), z(1, 128, 3, 1536)
    for c in range(NCORES):
        s, h = c // 2, c % 2
        r = res[c]
        sl = slice(h * 1024, (h + 1) * 1024)
        y_p[s, sl] = r["y_own"]
        pk[0, s, sl] = r["k_own"].reshape(1024, 8, 128)
        pv[0, s, sl] = r["v_own"].reshape(1024, 8, 128)
        plf[0, s, sl] = r["lf_own"]
        if h == 0:
            pmk[0, s] = r["memk_o"].reshape(256, 4, 128)
            pmv[0, s] = r["memv_o"].reshape(256, 4, 128)
        else:
            pssm[0, s] = r["ssm_p"].reshape(16, 64, 128)
            pconv[0, s] = r["conv_p"]
        cs = slice(c * 16, (c + 1) * 16)
        y_s[cs] = r["y_sam"].reshape(16, 8, D)
        sk[0, cs] = r["k_sam"].reshape(16, 8, 8, 128)
        sv[0, cs] = r["v_sam"].reshape(16, 8, 8, 128)
        slf[0, cs] = r["lf_sam"].reshape(16, 8, 8)
        sssm[0, cs] = r["ssm_s"].reshape(16, 16, 64, 128)
        sconv[0, cs] = r["conv_s"].reshape(16, 3, 1536)
    return (y_p, y_s, pk, pv, plf, pssm, pconv, pmk, pmv, sk, sv, slf, sssm, sconv)
```

```python
import numpy as np
import concourse.bass as bass
import concourse.mybir as mybir
from concourse.bass_utils import run_bass_kernel_spmd

F32, BF16, I32 = mybir.dt.float32, mybir.dt.bfloat16, mybir.dt.int32
AF = mybir.ActivationFunctionType
ALU = mybir.AluOpType
AX = mybir.AxisListType

D = 2048
DFF = 5632
NF = DFF // 128
EPS = 1e-6
NCORES = 8
INW = 5656
NEG = -30000.0


class Buf:
    def __init__(self, t, psum=False):
        self.t = t
        self.psum = psum
        self.lw = None
        self.rd = {}
        self.sem = None
        self.dcnt = 0
        self.dwcnt = 0

    def __getitem__(self, k):
        return self.t[k]


class Sched:
    ENG = ("pe", "act", "dve", "pool", "sp")

    def __init__(self, nc):
        self.nc = nc
        self.e = {"pe": nc.tensor, "act": nc.scalar, "dve": nc.vector, "pool": nc.gpsimd, "sp": nc.sync}
        self.cnt = {e: 0 for e in self.ENG}
        self.sem = {e: nc.alloc_semaphore(name="sem_" + e) for e in self.ENG}
        self.known = {e: {f: 0 for f in self.ENG} for e in self.ENG}
        self.kdma = {e: {} for e in self.ENG}
        self.dbufs = []
        self.nsb = 0
        self.ddep = {}

    def sb(self, shape, dt=F32, name=None):
        self.nsb += 1
        return Buf(self.nc.alloc_sbuf_tensor(name or f"sb{self.nsb}", list(shape), dt))

    def _wait(self, e, f, idx, raw=False, force=False):
        if f == e and not force:
            if e in ("pe", "sp") or not raw:
                return
        if self.known[e][f] >= idx:
            return
        self.known[e][f] = idx
        self.e[e].wait_ge(self.sem[f], idx)

    def _wait_dma(self, e, b, cnt):
        if cnt == 0 or self.kdma[e].get(id(b), 0) >= cnt:
            return
        self.kdma[e][id(b)] = cnt
        self.e[e].wait_ge(b.sem, cnt)

    def op(self, e, fn, reads=(), writes=()):
        for b in reads:
            if b.lw:
                self._wait(e, b.lw[0], b.lw[1], raw=True)
            self._wait_dma(e, b, b.dwcnt)
            if b.psum:
                for f, i in b.rd.items():
                    self._wait(e, f, i)
        for b in writes:
            if b.lw:
                self._wait(e, b.lw[0], b.lw[1], raw=b.psum)
            for f, i in b.rd.items():
                self._wait(e, f, i)
            self._wait_dma(e, b, b.dcnt)
        self.cnt[e] += 1
        idx = self.cnt[e]
        fn(self.e[e]).then_inc(self.sem[e], 1)
        for b in reads:
            b.rd[e] = idx
        for b in writes:
            b.lw = (e, idx)
            b.rd = {}

    def dma(self, q, out, in_, b, load, after=None, mark=None, indirect=None, reads=(), disjoint=False):
        if b.sem is None:
            b.sem = self.nc.alloc_semaphore(name=f"dsem{len(self.dbufs)}")
            self.dbufs.append(b)
        if load:
            if b.lw:
                self._wait(q, b.lw[0], b.lw[1], force=True)
            for f, i in b.rd.items():
                self._wait(q, f, i, force=True)
            if not disjoint:
                self._wait_dma(q, b, b.dcnt)
        else:
            if b.lw:
                self._wait(q, b.lw[0], b.lw[1], force=True)
            self._wait_dma(q, b, b.dwcnt)
        for rb in reads:
            if rb.lw:
                self._wait(q, rb.lw[0], rb.lw[1], force=True)
            self._wait_dma(q, rb, rb.dwcnt)
        if after is not None and after in self.ddep:
            db, dc = self.ddep[after]
            self._wait_dma(q, db, dc)
        b.dcnt += 16
        if load:
            b.dwcnt = b.dcnt
            b.lw = None
            b.rd = {}
        if mark is not None:
            self.ddep[mark] = (b, b.dcnt)
        if indirect is not None:
            self.e[q].indirect_dma_start(out=out, out_offset=None, in_=in_, in_offset=indirect).then_inc(b.sem, 16)
        else:
            self.e[q].dma_start(out=out, in_=in_).then_inc(b.sem, 16)

    def finish(self):
        for b in self.dbufs:
            self._wait_dma("sp", b, b.dcnt)
        for f in self.ENG:
            if f != "sp" and self.cnt[f] > 0:
                self._wait("sp", f, self.cnt[f], force=True)


_WSHAPES = [("ffn1_w_gate", [D, DFF]), ("ffn1_w_up", [D, DFF]), ("ffn1_w_down", [DFF, D]),
            ("ffn2_w_gate", [D, DFF]), ("ffn2_w_up", [D, DFF]), ("ffn2_w_down", [DFF, D]),
            ("w_in", [D, INW]), ("w_out", [D, D]),
            ("ffn1_norm", [D]), ("mix_norm", [D]), ("xattn_norm", [D]), ("ffn2_norm", [D]),
            ("fox_q_norm", [128]), ("fox_k_norm", [128]), ("fox_b_f", [8]),
            ("mem_norm", [D]), ("xattn_w_kv", [D, 1024]), ("xattn_w_q", [D, 512]), ("xattn_w_o", [512, D]),
            ("xattn_q_norm", [128]), ("xattn_k_norm", [128]),
            ("ssd_dt_bias", [16]), ("ssd_A_log", [16]), ("ssd_D", [16]), ("ssd_out_norm", [1024])]
_WNAMES = [n for n, _ in _WSHAPES]


def build(cfg):
    nc = bass.Bass("TRN2", target_bir_lowering=False)
    S = Sched(nc)

    def din(name, shape, dt=F32):
        return nc.dram_tensor(name, list(shape), dt, kind="ExternalInput").ap()

    def dout(name, shape):
        return nc.dram_tensor(name, list(shape), F32, kind="ExternalOutput").ap()

    NPRE = cfg["npre"]
    NOWN = cfg["nown"]
    NPG = cfg["npages"]
    PR = cfg["pool_rows"]
    SAM = cfg.get("sam", True)
    NK = NPRE + NOWN
    x_pre = din("x_pre", [NPRE * 128, D])
    x_own = din("x_own", [NOWN * 128, D])
    x_sam = din("x_sam", [128, D])
    cst = din("cst", [128, 1536])
    cst2 = din("cst2", [128, 2048])
    flg = din("flg", [128, 4])
    cwb = din("cwb", [5, 1536])
    mem_in = din("mem_in", [256, D])
    pool_kv = din("pool_kv", [PR, 2048])
    pool_lf = din("pool_lf", [PR, 8])
    ptab = din("ptab", [16 * NPG], I32)
    cmem_k = din("cmem_k", [16, 256, 512])
    cmem_v = din("cmem_v", [16, 256, 512])
    st_ssm = din("st_ssm", [16, 1024, 128])
    st_conv = din("st_conv", [48, 1536])
    w = {n: din(n, shp) for n, shp in _WSHAPES}
    y_own = dout("y_own", [NOWN * 128, D])
    y_sam = dout("y_sam", [128, D])
    k_own = dout("k_own", [NOWN * 128, 1024])
    v_own = dout("v_own", [NOWN * 128, 1024])
    lf_own = dout("lf_own", [NOWN * 128, 8])
    k_sam = dout("k_sam", [128, 1024])
    v_sam = dout("v_sam", [128, 1024])
    lf_sam = dout("lf_sam", [128, 8])
    memk_o = dout("memk_o", [256, 512])
    memv_o = dout("memv_o", [256, 512])
    ssm_p = dout("ssm_p", [1024, 128])
    conv_p = dout("conv_p", [3, 1536])
    ssm_s = dout("ssm_s", [16, 1024, 128])
    conv_s = dout("conv_s", [48, 1536])
    kts = nc.dram_tensor("kts", [NK, 128, 1024], BF16, kind="Internal").ap()
    vas = nc.dram_tensor("vas", [NK, 128, 1040], BF16, kind="Internal").ap()

    CST = S.sb([128, 1536], F32, "CST")
    S.dma("sp", CST[:], cst[:, :], CST, True)
    IDF = CST[:, 0:128]
    TRI = CST[:, 128:256]
    TRIBD = CST[:, 256:384]
    ONES = CST[:, 384:512]
    SELP = CST[:, 768:896]
    SELS = CST[:, 896:1024]
    BDSEL = CST[:, 1024:1040]
    EPSC = CST[:, 1040:1041]
    ONEC = CST[:, 1041:1042]
    PIDX = CST[:, 1042:1043]
    IDB = S.sb([128, 128], BF16, "IDB")
    S.op("dve", lambda e: e.tensor_copy(out=IDB[:], in_=IDF), [CST], [IDB])
    MNEG = S.sb([128, 2, 128], BF16, "MNEG")
    S.op("dve", lambda e: e.tensor_copy(out=MNEG[:].rearrange("p a b -> p (a b)"), in_=CST[:, 512:768]), [CST], [MNEG])
    FLG = S.sb([128, 4], F32, "FLG")
    S.dma("sp", FLG[:], flg[:, :], FLG, True)

    PS = [Buf(nc.alloc_psum_tensor(f"ps{i}", [128, 512], F32), psum=True) for i in range(8)]
    psi = [0]

    def ps():
        b = PS[psi[0] % 4]
        psi[0] += 1
        return b

    TG = 256
    XG = [[S.sb([128, 512], F32, f"XG{t}_{c}") for c in range(4)] for t in range(5)]
    UT = S.sb([128, 16, TG], BF16, "UT")
    GB = S.sb([128, D], F32, "GB")
    JUNK = S.sb([128, 512], BF16, "JUNK")
    UN = S.sb([128, D], BF16, "UN")
    SS = [S.sb([128, 4], F32, f"SS{t}") for t in range(2)]
    WA = [S.sb([128, 16, 256], BF16, f"WA{i}") for i in range(4)]
    SCR = [S.sb([128, 1024], F32, f"SCR{i}") for i in range(13)]
    STG = SCR[0:2]
    STG2 = SCR[2]
    ZS = SCR[3:5]
    XS = SCR[5:7]
    KTG, QT = SCR[7], SCR[8]
    MIXT = SCR[9:11]
    XSD, YT = SCR[11], SCR[12]
    DEC = YT
    UT2 = [SCR[9], SCR[10]]
    UT2v = [u[:].bitcast(BF16).rearrange("p (k c) -> p k c", k=8) for u in UT2]
    HT = [SCR[7], SCR[8]]
    HTv = [h_[:].bitcast(BF16).rearrange("p (f c) -> p f c", f=4) for h_ in HT]
    SG = [SCR[11], SCR[12]]
    WBS = [SCR[i] for i in range(5)]
    UT3 = SCR[5]
    UT3v = UT3[:].bitcast(BF16).rearrange("p (k c) -> p k c", k=16)
    HTB = SCR[6]
    HTBv = HTB[:].bitcast(BF16)[:, 0:512].rearrange("p (f c) -> p f c", f=4)
    psi8 = [0]

    def ps8():
        b = PS[psi8[0] % 8]
        psi8[0] += 1
        return b
    WBv = [b_[:].bitcast(BF16) for b_ in WBS]
    wbi = [0]
    cur_tb = [0]
    KTGv = KTG[:].bitcast(BF16).rearrange("p (h c) -> p h c", h=8)
    QTv = QT[:].bitcast(BF16).rearrange("p (h c) -> p h c", h=8)
    MIXv = [m[:].bitcast(BF16) for m in MIXT]
    VAG = S.sb([128, 2, 8, 130], BF16, "VAG")
    KC = [S.sb([128, 8, 128], BF16, f"KC{i}") for i in range(2)]
    VC = [S.sb([128, 8, 130], BF16, f"VC{i}") for i in range(2)]
    PT = [S.sb([128, 4, 128], BF16, f"PT{i}") for i in range(4)]
    GQ = S.sb([128, 2, 128], F32, "GQ")
    GX = S.sb([128, 2, 128], F32, "GX")
    BFB = S.sb([128, 8], F32, "BFB")
    DTB = S.sb([128, 16], F32, "DTB")
    ANEG = S.sb([128, 16], F32, "ANEG")
    DBC = S.sb([128, 16], F32, "DBC")
    WFD = S.sb([128, 16, 24], BF16, "WFD")
    SM = [S.sb([128, 32], F32, f"SM{t}") for t in range(2)]
    SMF = [S.sb([128, 64], F32, f"SMF{t}") for t in range(2)]
    DTT = [S.sb([128, 16], F32, f"DTT{t}") for t in range(2)]
    CARRY = S.sb([128, 8], F32, "CARRY")
    CKT = S.sb([128, NK, 8], F32, "CKT")
    CREFS = S.sb([128, 2, 8], F32, "CREFS")
    BIAS = S.sb([128, NK, 8], F32, "BIAS")
    RD = S.sb([128, 8], F32, "RD")
    H = S.sb([128, 1024], F32, "H")
    HB = S.sb([128, 1024], BF16, "HB")
    HX = S.sb([128, 12, 3], F32, "HX")
    CW = S.sb([128, 12, 5], F32, "CW")
    CWL = GB
    XE = [S.sb([128, 16, 11], F32, f"XE{i}") for i in range(1)]
    XEW = S.sb([128, 3 + TG], F32, "XEW")
    ACW = S.sb([128, TG], F32, "ACW")
    XCW = S.sb([128, TG], F32, "XCW")
    AC = [S.sb([128, 128], F32, f"AC{i}") for i in range(1)]
    XCF = S.sb([128, 128], F32, "XCF")
    BCT = S.sb([128, 4, TG], BF16, "BCT")
    SA = S.sb([128, 128], F32, "SA")
    ALB = S.sb([128, 16], F32, "ALB")
    CD = S.sb([128, 16], F32, "CD")
    CBM = S.sb([128, 2, 128], F32, "CBM")
    MT = [S.sb([128, 4, 128], BF16, f"MT{i}") for i in range(2)]
    XDT = S.sb([128, 1024], BF16, "XDT")
    XDTE = S.sb([128, 1024], BF16, "XDTE")
    BTM = S.sb([128, 2, 128], BF16, "BTM")
    MKT = S.sb([128, 4, 256], BF16, "MKT")
    MVA = S.sb([128, 2, 4, 130], BF16, "MVA")
    wai = [0, 0]

    def bc8(ap128, H8=8):
        return ap128.unsqueeze(1).to_broadcast([128, H8, 128])

    S.dma("sp", GQ[:, 0, :], w["fox_q_norm"].partition_broadcast(128), GQ, True)
    S.dma("sp", GQ[:, 1, :], w["fox_k_norm"].partition_broadcast(128), GQ, True)
    S.dma("sp", GX[:, 0, :], w["xattn_q_norm"].partition_broadcast(128), GX, True)
    S.dma("sp", GX[:, 1, :], w["xattn_k_norm"].partition_broadcast(128), GX, True)
    S.dma("sp", BFB[:], w["fox_b_f"].partition_broadcast(128), BFB, True)
    S.dma("sp", DTB[:], w["ssd_dt_bias"].partition_broadcast(128), DTB, True)
    S.dma("sp", ANEG[:], w["ssd_A_log"].partition_broadcast(128), ANEG, True)
    S.dma("sp", DBC[:], w["ssd_D"].partition_broadcast(128), DBC, True)
    S.op("act", lambda e: e.activation(out=ANEG[:], in_=ANEG[:], func=AF.Exp), [ANEG], [ANEG])
    S.op("dve", lambda e: e.tensor_scalar(out=ANEG[:], in0=ANEG[:], scalar1=-1.0, scalar2=None, op0=ALU.mult), [ANEG], [ANEG])
    win_v = w["w_in"].rearrange("(kc p) f -> p kc f", p=128)
    wkv_v = w["xattn_w_kv"].rearrange("(kc p) f -> p kc f", p=128)
    wq_v = w["xattn_w_q"].rearrange("(kc p) f -> p kc f", p=128)
    wout_v = w["w_out"].rearrange("(kc p) f -> p kc f", p=128)
    wo_v = w["xattn_w_o"].rearrange("(kc p) f -> p kc f", p=128)
    S.dma("pool", WFD[:, :, 0:8], win_v[:, :, 3072:3080], WFD, True)
    S.dma("pool", WFD[:, :, 8:24], win_v[:, :, 5640:5656], WFD, True)
    S.dma("sp", CWL[0:5, 0:1536], cwb[:, :], CWL, True)
    pcw = ps()
    for cc in range(12):
        S.op("pe", lambda e, cc=cc: e.transpose(out=pcw[:, cc * 5:cc * 5 + 5], in_=CWL[0:5, cc * 128:(cc + 1) * 128], identity=IDF[0:5, 0:5]),
             [CWL, CST], [pcw])
    S.op("dve", lambda e: e.tensor_copy(out=CW[:].rearrange("p a b -> p (a b)"), in_=pcw[:, 0:60]), [pcw], [CW])
    for b_, v_ in ((CARRY, 0.0), (H, 0.0), (HX, 0.0)):
        S.op("dve", lambda e, b_=b_, v_=v_: e.memset(b_[:], v_), [], [b_])
    S.op("dve", lambda e: e.memset(HB[:], 0.0), [], [HB])
    for vb in [VAG] + VC + [MVA]:
        S.op("pool", lambda e, vb=vb: e.memset(vb[:], 1.0), [], [vb])

    def wload(slot, src):
        S.dma("pool", slot[:, 0:src.shape[1], 0:src.shape[2]], src, slot, True)

    def norm_T(nt, gain, tiles=None, pos0=0):
        if tiles is None:
            tiles = [cur_tb[0] + t for t in range(nt)]
        S.dma("sp", GB[:], gain.partition_broadcast(128), GB, True)
        for i, xt in enumerate(tiles):
            pos = pos0 + i
            ss = SS[i % 2]
            for c in range(4):
                S.op("act", lambda e, xt=xt, c=c: e.activation(out=JUNK[:], in_=XG[xt][c][:], func=AF.Square,
                                                                accum_out=ss[:, c:c + 1]), [XG[xt][c]], [JUNK, ss])
            S.op("dve", lambda e: e.tensor_reduce(out=ss[:, 0:1], in_=ss[:, 0:4], axis=AX.X, op=ALU.add), [ss], [ss])
            S.op("act", lambda e: e.activation(out=ss[:, 1:2], in_=ss[:, 0:1], func=AF.Sqrt, bias=EPSC, scale=1.0 / D), [ss, CST], [ss])
            S.op("dve", lambda e: e.reciprocal(out=ss[:, 2:3], in_=ss[:, 1:2]), [ss], [ss])
            for c in range(4):
                S.op("dve", lambda e, xt=xt, c=c: e.scalar_tensor_tensor(out=UN[:, c * 512:(c + 1) * 512], in0=XG[xt][c][:], scalar=ss[:, 2:3],
                                                                          in1=GB[:, c * 512:(c + 1) * 512], op0=ALU.mult, op1=ALU.mult),
                     [XG[xt][c], ss, GB], [UN])

            def evac(kc0, n, pv3, pos=pos):
                if pos < 2:
                    S.op("act", lambda e: e.activation(out=UT[:, kc0:kc0 + n, pos * 128:(pos + 1) * 128], in_=pv3, func=AF.Copy), [curp[0]], [UT])
                elif pos == 4:
                    S.op("act", lambda e: e.activation(out=UT3v[:, kc0:kc0 + n, :], in_=pv3, func=AF.Copy), [curp[0]], [UT3])
                else:
                    u2 = UT2[kc0 // 8]
                    S.op("act", lambda e: e.activation(out=UT2v[kc0 // 8][:, 0:n, (pos - 2) * 128:(pos - 1) * 128], in_=pv3, func=AF.Copy), [curp[0]], [u2])
            tr_bf(UN[:], 16, evac, [UN])

    curp = [None]

    def tr_bf(src2d, nblk, evac, rbufs):
        for b0 in range(0, nblk, 8):
            n = min(8, nblk - b0)
            p = ps()
            curp[0] = p
            pv = p[:].bitcast(BF16)
            for j in range(n):
                S.op("pe", lambda e, j=j, b0=b0: e.transpose(out=pv[:, j * 128:(j + 1) * 128], in_=src2d[:, (b0 + j) * 128:(b0 + j + 1) * 128],
                                                              identity=IDB[:]), rbufs + [IDB], [p])
            evac(b0, n, pv[:, 0:n * 128].rearrange("p (j c) -> p j c", j=n))

    def tr_f32(src_fn, nblk, evac, rbufs, rows=128):
        for b0 in range(0, nblk, 4):
            n = min(4, nblk - b0)
            p = ps()
            curp[0] = p
            for j in range(n):
                S.op("pe", lambda e, j=j, b0=b0: e.transpose(out=p[:, j * rows:(j + 1) * rows], in_=src_fn(b0 + j), identity=IDF[0:rows, 0:rows]),
                     rbufs + [CST], [p])
            evac(b0, n, p[:, 0:n * rows])

    def ffn(tiles, wg, wu, wd):
        ntl = len(tiles)
        T = min(ntl, 4) * 128
        T0 = min(T, 256)
        five = ntl == 5
        wgv = wg.rearrange("(kc p) f -> p kc f", p=128)
        wuv = wu.rearrange("(kc p) f -> p kc f", p=128)
        wdv = wd.rearrange("(f p) c -> p f c", p=128)

        def gu(pb, pb2, slot, j):
            for kc in range(16):
                S.op("pe", lambda e, kc=kc: e.matmul(pb[:, 0:T0], lhsT=slot[:, kc, j * 128:(j + 1) * 128], rhs=UT[:, kc, 0:T0],
                                                       start=(kc == 0), stop=(kc == 15)), [slot, UT], [pb])
            if T > 256:
                for kc in range(16):
                    S.op("pe", lambda e, kc=kc: e.matmul(pb[:, 256:T], lhsT=slot[:, kc, j * 128:(j + 1) * 128], rhs=UT2v[kc // 8][:, kc % 8, 0:T - 256],
                                                           start=(kc == 0), stop=(kc == 15)), [slot, UT2[kc // 8]], [pb])
            if five:
                for kc in range(16):
                    S.op("pe", lambda e, kc=kc: e.matmul(pb2[:, 0:128], lhsT=slot[:, kc, j * 128:(j + 1) * 128], rhs=UT3v[:, kc, :],
                                                           start=(kc == 0), stop=(kc == 15)), [slot, UT3], [pb2])

        for sl in range(NF // 4):
            ht, htv = HT[sl % 2], HTv[sl % 2]
            for bb in range(2):
                blk = sl * 2 + bb
                g_s = WA[wai[0] % 2]
                u_s = WA[2 + wai[0] % 2]
                wai[0] += 1
                wload(g_s, wgv[:, :, blk * 256:(blk + 1) * 256])
                wload(u_s, wuv[:, :, blk * 256:(blk + 1) * 256])
                for j in range(2):
                    fi = bb * 2 + j
                    pg, pu = ps8(), ps8()
                    pg2, pu2 = (ps8(), ps8()) if five else (None, None)
                    gu(pg, pg2, g_s, j)
                    gu(pu, pu2, u_s, j)
                    sg = SG[fi % 2]
                    S.op("act", lambda e: e.activation(out=sg[:, 0:T], in_=pg[:, 0:T], func=AF.Silu), [pg], [sg])
                    S.op("dve", lambda e, fi=fi: e.tensor_tensor(out=htv[:, fi, 0:T], in0=sg[:, 0:T], in1=pu[:, 0:T], op=ALU.mult), [sg, pu], [ht])
                    if five:
                        S.op("act", lambda e: e.activation(out=sg[:, 512:640], in_=pg2[:, 0:128], func=AF.Silu), [pg2], [sg])
                        S.op("dve", lambda e, fi=fi: e.tensor_tensor(out=HTBv[:, fi, :], in0=sg[:, 512:640], in1=pu2[:, 0:128], op=ALU.mult), [sg, pu2], [HTB])
            wds = []
            for fi in range(4):
                k = wbi[0] % len(WBS)
                wbi[0] += 1
                S.dma("pool", WBv[k], wdv[:, sl * 4 + fi, :], WBS[k], True)
                wds.append(k)
            for i, xt in enumerate(tiles):
                for c in range(4):
                    po = ps8()
                    for fi in range(4):
                        k = wds[fi]
                        lh = htv[:, fi, i * 128:(i + 1) * 128] if i < 4 else HTBv[:, fi, :]
                        hb = ht if i < 4 else HTB
                        S.op("pe", lambda e, fi=fi, c=c, k=k, lh=lh: e.matmul(po[:, :], lhsT=lh, rhs=WBv[k][:, c * 512:(c + 1) * 512],
                                                                                start=(fi == 0), stop=(fi == 3)), [hb, WBS[k]], [po])
                    S.op("dve", lambda e, xt=xt, c=c: e.scalar_tensor_tensor(out=XG[xt][c][:], in0=po[:, :], scalar=0.5, in1=XG[xt][c][:],
                                                                              op0=ALU.mult, op1=ALU.add), [po, XG[xt][c]], [XG[xt][c]])

    def proj_tm(nt, wv, col0, ncols, sink, nkc=16, src=None):
        src = UT if src is None else src
        for b0 in range(0, ncols, 256):
            nb = min(256, ncols - b0)
            slot = WA[wai[1] % 4]
            wai[1] += 1
            wload(slot, wv[:, :, col0 + b0:col0 + b0 + nb])
            for t in range(nt):
                p = ps()
                for kc in range(nkc):
                    S.op("pe", lambda e, kc=kc, t=t: e.matmul(p[:, 0:nb], lhsT=src[:, kc, t * 128:(t + 1) * 128], rhs=slot[:, kc, 0:nb],
                                                                start=(kc == 0), stop=(kc == nkc - 1)), [src, slot], [p])
                sink(t, b0, nb, p)

    def to_stg(t, b0, nb, p):
        S.op("act", lambda e: e.activation(out=STG[t][:, b0:b0 + nb], in_=p[:, 0:nb], func=AF.Copy), [p], [STG[t]])

    def add_resid(t, b0, nb, p):
        c = b0 // 512
        o = b0 % 512
        xt = cur_tb[0] + t
        S.op("dve", lambda e: e.tensor_tensor(out=XG[xt][c][:, o:o + nb], in0=p[:, 0:nb], in1=XG[xt][c][:, o:o + nb], op=ALU.add),
             [p, XG[xt][c]], [XG[xt][c]])

    def headnorm(t, which, H8=8, G=None, xb=None):
        G = GQ if G is None else G
        xb = STG[t] if xb is None else xb
        W = H8 * 128
        x3 = xb[:, 0:W].rearrange("p (h d) -> p h d", h=H8)
        s3 = STG2[:, 0:W].rearrange("p (h d) -> p h d", h=H8)
        sm = SM[t]
        S.op("dve", lambda e: e.tensor_tensor(out=STG2[:, 0:W], in0=xb[:, 0:W], in1=xb[:, 0:W], op=ALU.mult), [xb], [STG2])
        S.op("dve", lambda e: e.tensor_reduce(out=sm[:, 0:H8], in_=s3, axis=AX.X, op=ALU.add), [STG2], [sm])
        S.op("act", lambda e: e.activation(out=sm[:, 8:8 + H8], in_=sm[:, 0:H8], func=AF.Sqrt, bias=EPSC, scale=1.0 / 128), [sm, CST], [sm])
        S.op("dve", lambda e: e.reciprocal(out=sm[:, 16:16 + H8], in_=sm[:, 8:8 + H8]), [sm], [sm])
        S.op("dve", lambda e: e.tensor_tensor(out=x3, in0=x3, in1=sm[:, 16:16 + H8].unsqueeze(2).to_broadcast([128, H8, 128]), op=ALU.mult),
             [xb, sm], [xb])
        S.op("dve", lambda e: e.tensor_tensor(out=x3, in0=x3, in1=bc8(G[:, which, :], H8), op=ALU.mult), [xb, G], [xb])

    def stg_T(t, nh, dstv, dbuf, scale=1.0, xb=None):
        xb = STG[t] if xb is None else xb
        tr_f32(lambda j: xb[:, j * 128:(j + 1) * 128], nh,
               lambda b0, n, pv: S.op("act", lambda e: e.activation(out=dstv[:, b0:b0 + n, t * 128:(t + 1) * 128],
                                                                      in_=pv.rearrange("p (j c) -> p j c", j=n), func=AF.Copy, scale=scale),
                                      [curp[0]], [dbuf]), [xb])

    def to_buf(bufs):
        def sink(t, b0, nb, p):
            S.op("act", lambda e: e.activation(out=bufs[t][:, b0:b0 + nb], in_=p[:, 0:nb], func=AF.Copy), [p], [bufs[t]])
        return sink

    def ssd_tile(kind, t, want_y):
        sam = kind == "sam"
        tri = TRIBD if sam else TRI
        sel = SELS if sam else SELP
        c0 = t * 128
        dt = SA[:, 80:96]
        da, acol, altm, dte, eac = SA[:, 0:16], SA[:, 16:32], SA[:, 32:48], SA[:, 48:64], SA[:, 64:80]
        S.op("dve", lambda e: e.tensor_copy(out=dt, in_=DTT[t][:]), [DTT[t]], [SA])
        S.op("dve", lambda e: e.tensor_tensor(out=da, in0=dt, in1=ANEG[:], op=ALU.mult), [SA, ANEG], [SA])
        p1 = ps()
        S.op("pe", lambda e: e.matmul(p1[:, 0:16], lhsT=tri, rhs=da, start=True, stop=True), [CST, SA], [p1])
        S.op("dve", lambda e: e.tensor_copy(out=acol, in_=p1[:, 0:16]), [p1], [SA])
        S.op("pe", lambda e: e.matmul(p1[:, 16:32], lhsT=sel, rhs=acol, start=True, stop=True), [CST, SA], [p1])
        S.op("dve", lambda e: e.tensor_copy(out=altm, in_=p1[:, 16:32]), [p1], [SA])
        S.op("dve", lambda e: e.tensor_tensor(out=dte, in0=altm, in1=acol, op=ALU.subtract), [SA], [SA])
        S.op("act", lambda e: e.activation(out=dte, in_=dte, func=AF.Exp), [SA], [SA])
        S.op("act", lambda e: e.activation(out=eac, in_=acol, func=AF.Exp), [SA], [SA])
        xs3 = XS[t][:].rearrange("p (h d) -> p h d", h=16)
        S.op("dve", lambda e: e.tensor_tensor(out=XDT[:].rearrange("p (h d) -> p h d", h=16), in0=xs3,
                                              in1=dt.unsqueeze(2).to_broadcast([128, 16, 64]), op=ALU.mult), [XS[t], SA], [XDT])
        S.op("dve", lambda e: e.tensor_tensor(out=XDTE[:].rearrange("p (h d) -> p h d", h=16), in0=XDT[:].rearrange("p (h d) -> p h d", h=16),
                                              in1=dte.unsqueeze(2).to_broadcast([128, 16, 64]), op=ALU.mult), [XDT, SA], [XDTE])
        if want_y:
            S.op("dve", lambda e: e.tensor_tensor(out=XSD[:].rearrange("p (h d) -> p h d", h=16), in0=xs3,
                                                  in1=DBC[:].unsqueeze(2).to_broadcast([128, 16, 64]), op=ALU.mult), [XS[t], DBC], [XSD])
            pc = ps()
            for g in range(2):
                S.op("pe", lambda e, g=g: e.matmul(pc[:, g * 128:(g + 1) * 128], lhsT=BCT[:, g, c0:c0 + 128], rhs=BCT[:, 2 + g, c0:c0 + 128],
                                                    start=True, stop=True), [BCT], [pc])
            S.op("dve", lambda e: e.tensor_tensor(out=CBM[:], in0=pc[:, 0:256].rearrange("p (g c) -> p g c", g=2),
                                                  in1=tri.unsqueeze(1).to_broadcast([128, 2, 128]), op=ALU.mult), [pc, CST], [CBM])
            for hq in range(4):
                pa = ps()
                for j in range(4):
                    h = hq * 4 + j
                    S.op("pe", lambda e, j=j, h=h: e.matmul(pa[:, j * 128:(j + 1) * 128], lhsT=da[:, h:h + 1].to_broadcast([128, 128]), rhs=tri,
                                                              start=True, stop=True), [SA, CST], [pa])
                dec3 = DEC[:, 0:512].rearrange("p (j c) -> p j c", j=4)
                for j in range(4):
                    h = hq * 4 + j
                    S.op("dve", lambda e, j=j, h=h: e.tensor_scalar(out=dec3[:, j, :], in0=pa[:, j * 128:(j + 1) * 128], scalar1=acol[:, h:h + 1],
                                                                      scalar2=0.0, op0=ALU.subtract, op1=ALU.min), [pa, SA], [DEC])
                S.op("act", lambda e: e.activation(out=DEC[:, 0:512], in_=DEC[:, 0:512], func=AF.Exp), [DEC], [DEC])
                mt = MT[hq % 2]
                g = hq // 2
                S.op("dve", lambda e, g=g: e.tensor_tensor(out=mt[:], in0=dec3, in1=CBM[:, g, :].unsqueeze(1).to_broadcast([128, 4, 128]), op=ALU.mult),
                     [DEC, CBM], [mt])
                for j in range(4):
                    h = hq * 4 + j
                    pb = PS[4 + h // 8]
                    S.op("pe", lambda e, j=j, h=h, pb=pb: e.matmul(pb[:, (h % 8) * 64:(h % 8 + 1) * 64], lhsT=mt[:, j, :], rhs=XDT[:, h * 64:(h + 1) * 64],
                                                                     start=True, stop=True), [mt, XDT], [pb])
            if not sam:
                for g in range(2):
                    S.op("pe", lambda e, g=g: e.matmul(PS[6 + g][:, :], lhsT=BCT[:, 2 + g, c0:c0 + 128], rhs=HB[:, g * 512:(g + 1) * 512],
                                                        start=True, stop=True), [BCT, HB], [PS[6 + g]])
        pbt = ps()
        pbv = pbt[:].bitcast(BF16)
        for g in range(2):
            S.op("pe", lambda e, g=g: e.transpose(out=pbv[:, g * 128:(g + 1) * 128], in_=BCT[:, g, c0:c0 + 128], identity=IDB[:]), [BCT, IDB], [pbt])
        S.op("act", lambda e: e.activation(out=BTM[:].rearrange("p a b -> p (a b)"), in_=pbv[:, 0:256], func=AF.Copy), [pbt], [BTM])
        return da, acol, altm, dte, eac

    def ssd_finish_y(t, eac):
        y3 = YT[:].rearrange("p (h d) -> p h d", h=16)
        for g in range(2):
            S.op("dve", lambda e, g=g: e.tensor_tensor(out=y3[:, g * 8:(g + 1) * 8, :], in0=PS[6 + g][:, :].rearrange("p (h d) -> p h d", h=8),
                                                        in1=eac[:, g * 8:(g + 1) * 8].unsqueeze(2).to_broadcast([128, 8, 64]), op=ALU.mult),
                 [PS[6 + g], SA], [YT])
        S.op("dve", lambda e: e.tensor_tensor(out=YT[:], in0=YT[:], in1=XSD[:], op=ALU.add), [YT, XSD], [YT])
        for g in range(2):
            S.op("dve", lambda e, g=g: e.tensor_tensor(out=YT[:, g * 512:(g + 1) * 512], in0=PS[4 + g][:, :], in1=YT[:, g * 512:(g + 1) * 512], op=ALU.add),
                 [PS[4 + g], YT], [YT])
        S.op("dve", lambda e: e.tensor_tensor(out=YT[:], in0=YT[:], in1=ZS[t][:], op=ALU.mult), [YT, ZS[t]], [YT])
        ss = SS[t]
        S.dma("sp", GB[:, 0:1024], w["ssd_out_norm"].partition_broadcast(128), GB, True)
        S.op("act", lambda e: e.activation(out=STG2[:], in_=YT[:], func=AF.Square, accum_out=ss[:, 0:1]), [YT], [STG2, ss])
        S.op("act", lambda e: e.activation(out=ss[:, 1:2], in_=ss[:, 0:1], func=AF.Sqrt, bias=EPSC, scale=1.0 / 1024), [ss, CST], [ss])
        S.op("dve", lambda e: e.reciprocal(out=ss[:, 2:3], in_=ss[:, 1:2]), [ss], [ss])
        S.op("dve", lambda e: e.scalar_tensor_tensor(out=MIXv[t][:, 1024:2048], in0=YT[:], scalar=ss[:, 2:3], in1=GB[:, 0:1024], op0=ALU.mult, op1=ALU.mult),
             [YT, ss, GB], [MIXT[t]])

    def state_out(dst_view, hsrc, hbuf):
        ho = STG2
        tr_f32(lambda j: hsrc[:, j * 128:(j + 1) * 128], 8,
               lambda b0, n, pv: S.op("act", lambda e: e.activation(out=ho[:, b0 * 128:(b0 + n) * 128], in_=pv, func=AF.Copy), [curp[0]], [ho]), [hbuf])
        S.dma("sp", dst_view, ho[:].rearrange("p (c n) -> p c n", c=8), ho, False)

    def big_group(kind, bgi, with_sam=False):
        sam = kind == "sam"
        full = kind != "pre"
        ntot = 1 if sam else (NPRE if kind == "pre" else NOWN)
        ntl = 1 if sam else min(4, ntot - bgi * 4)
        xsrc = {"pre": x_pre, "own": x_own, "sam": x_sam}[kind]
        rb = bgi * 512
        tiles = list(range(ntl))
        for t in tiles:
            for c in range(4):
                S.dma("sp", XG[t][c][:], xsrc[rb + t * 128:rb + (t + 1) * 128, c * 512:(c + 1) * 512], XG[t][c], True)
        if with_sam:
            for c in range(4):
                S.dma("sp", XG[4][c][:], x_sam[0:128, c * 512:(c + 1) * 512], XG[4][c], True)
            tiles = tiles + [4]
        poss = list(range(ntl)) + ([4] if with_sam else [])
        for xt, pos in zip(tiles, poss):
            norm_T(1, w["ffn1_norm"], [xt], pos)
        ffn(tiles, w["ffn1_w_gate"], w["ffn1_w_up"], w["ffn1_w_down"])
        for sub in range((ntl + 1) // 2):
            cur_tb[0] = sub * 2
            mixer(kind, bgi * 2 + sub, min(2, ntl - sub * 2))
        if with_sam:
            cur_tb[0] = 4
            mixer("sam", 0, 1)
        cur_tb[0] = 0
        if not full:
            return
        for xt, pos in zip(tiles, poss):
            norm_T(1, w["ffn2_norm"], [xt], pos)
        ffn(tiles, w["ffn2_w_gate"], w["ffn2_w_up"], w["ffn2_w_down"])
        ydst = {"own": y_own, "sam": y_sam}[kind]
        for t in range(ntl):
            for c in range(4):
                S.dma("sp", ydst[rb + t * 128:rb + (t + 1) * 128, c * 512:(c + 1) * 512], XG[t][c][:], XG[t][c], False)
        if with_sam:
            for c in range(4):
                S.dma("sp", y_sam[0:128, c * 512:(c + 1) * 512], XG[4][c][:], XG[4][c], False)

    def mixer(kind, gi, nt):
        sam = kind == "sam"
        T = nt * 128
        full = kind != "pre"
        r0 = gi * 256
        norm_T(nt, w["mix_norm"])
        kdst = {"own": k_own, "sam": k_sam}.get(kind)
        vdst = {"own": v_own, "sam": v_sam}.get(kind)
        ldst = {"own": lf_own, "sam": lf_sam}.get(kind)
        kt0 = (0 if kind == "pre" else NPRE) + gi * 2
        proj_tm(nt, win_v, 1024, 1024, to_stg)
        proj_tm(nt, win_v, 2048, 1024, to_buf(XS))
        if full:
            proj_tm(nt, win_v, 0, 1024, to_buf(MIXT))
            proj_tm(nt, win_v, 3080, 1024,
                    lambda t, b0, nb, p: S.op("act", lambda e: e.activation(out=ZS[t][:, b0:b0 + nb], in_=p[:, 0:nb], func=AF.Silu), [p], [ZS[t]]))
        for t in range(nt):
            headnorm(t, 1)
            if kdst is not None:
                S.dma("sp", kdst[r0 + t * 128:r0 + (t + 1) * 128, :], STG[t][:], STG[t], False)
            stg_T(t, 8, KTGv, KTG)
        if not sam:
            for t in range(nt):
                S.dma("sp", kts[kt0 + t].rearrange("p (h c) -> p h c", h=8), KTGv[:, :, t * 128:(t + 1) * 128], KTG, False, mark=("k", kt0 + t))
        for t in range(nt):
            if vdst is not None:
                S.dma("sp", vdst[r0 + t * 128:r0 + (t + 1) * 128, :], XS[t][:], XS[t], False)
            S.op("pool", lambda e, t=t: e.tensor_copy(out=VAG[:, t, :, 0:128], in_=XS[t][:].rearrange("p (h d) -> p h d", h=8)), [XS[t]], [VAG])
            if not sam:
                S.dma("sp", vas[kt0 + t], VAG[:, t, :, :].rearrange("p h c -> p (h c)"), VAG, False, mark=("v", kt0 + t))
        if full:
            for t in range(nt):
                headnorm(t, 0, xb=MIXT[t])
                stg_T(t, 8, QTv, QT, scale=128 ** -0.5, xb=MIXT[t])
        for t in range(nt):
            p = ps()
            sm = SMF[t]
            for kc in range(16):
                S.op("pe", lambda e, kc=kc, t=t: e.matmul(p[:, 0:24], lhsT=UT[:, kc, t * 128:(t + 1) * 128], rhs=WFD[:, kc, :],
                                                            start=(kc == 0), stop=(kc == 15)), [UT, WFD], [p])
            S.op("dve", lambda e: e.tensor_tensor(out=sm[:, 0:8], in0=p[:, 0:8], in1=BFB[:], op=ALU.add), [p, BFB], [sm])
            S.op("dve", lambda e: e.tensor_tensor(out=sm[:, 32:48], in0=p[:, 8:24], in1=DTB[:], op=ALU.add), [p, DTB], [sm])
            S.op("act", lambda e: e.activation(out=sm[:, 8:16], in_=sm[:, 0:8], func=AF.Exp, scale=-1.0), [sm], [sm])
            S.op("act", lambda e: e.activation(out=sm[:, 48:64], in_=sm[:, 32:48], func=AF.Exp), [sm], [sm])
            S.op("act", lambda e: e.activation(out=sm[:, 16:24], in_=sm[:, 8:16], func=AF.Ln, bias=ONEC, scale=1.0), [sm, CST], [sm])
            S.op("act", lambda e, t=t: e.activation(out=DTT[t][:], in_=sm[:, 48:64], func=AF.Ln, bias=ONEC, scale=1.0), [sm, CST], [DTT[t]])
            S.op("dve", lambda e: e.tensor_scalar(out=sm[:, 24:32], in0=sm[:, 16:24], scalar1=-1.0, scalar2=None, op0=ALU.mult), [sm], [sm])
            if ldst is not None:
                S.dma("sp", ldst[r0 + t * 128:r0 + (t + 1) * 128, :], sm[:, 24:32], sm, False)
        if not sam:
            for t in range(nt):
                kt = kt0 + t
                p = ps()
                lf = SMF[t][:, 24:32]
                S.op("pe", lambda e: e.matmul(p[:, 0:8], lhsT=TRI, rhs=lf, start=True, stop=True), [CST, SMF[t]], [p])
                S.op("pe", lambda e: e.matmul(p[:, 8:16], lhsT=ONES, rhs=lf, start=True, stop=True), [CST, SMF[t]], [p])
                S.op("dve", lambda e, kt=kt: e.tensor_tensor(out=CKT[:, kt, :], in0=p[:, 0:8], in1=CARRY[:], op=ALU.add), [p, CARRY], [CKT])
                S.op("dve", lambda e: e.tensor_tensor(out=CARRY[:], in0=p[:, 8:16], in1=CARRY[:], op=ALU.add), [p, CARRY], [CARRY])
                S.op("dve", lambda e, t=t: e.tensor_copy(out=CREFS[:, t, :], in_=CARRY[:]), [CARRY], [CREFS])
        if sam:
            S.dma("sp", STG2[0:48, :], st_conv[:, 0:1024], STG2, True)
            S.dma("sp", STG[0][0:48, 0:512], st_conv[:, 1024:1536], STG[0], True)
        def conv_post(cc, p):
            if (not sam) and nt == 2:
                S.op("dve", lambda e: e.tensor_copy(out=XEW[:, 0:3], in_=HX[:, cc, :]), [HX], [XEW])
                S.op("act", lambda e: e.activation(out=XEW[:, 3:3 + T], in_=p[:, 0:T], func=AF.Copy), [p], [XEW])
                S.op("dve", lambda e: e.tensor_copy(out=HX[:, cc, :], in_=XEW[:, T:T + 3]), [XEW], [HX])
                S.op("dve", lambda e: e.tensor_scalar(out=ACW[:, 0:T], in0=XEW[:, 0:T], scalar1=CW[:, cc, 0:1], scalar2=CW[:, cc, 4:5], op0=ALU.mult, op1=ALU.add),
                     [XEW, CW], [ACW])
                for j2 in range(1, 4):
                    S.op("dve", lambda e, j2=j2: e.scalar_tensor_tensor(out=ACW[:, 0:T], in0=XEW[:, j2:j2 + T], scalar=CW[:, cc, j2:j2 + 1], in1=ACW[:, 0:T],
                                                                         op0=ALU.mult, op1=ALU.add), [XEW, CW, ACW], [ACW])
                if cc < 8:
                    S.op("act", lambda e: e.activation(out=XCW[:, 0:T], in_=ACW[:, 0:T], func=AF.Silu), [ACW], [XCW])
                    pt_ = ps()
                    for t in range(nt):
                        S.op("pe", lambda e, t=t: e.transpose(out=pt_[:, t * 128:(t + 1) * 128], in_=XCW[:, t * 128:(t + 1) * 128], identity=IDF), [XCW, CST], [pt_])
                    for t in range(nt):
                        S.op("dve", lambda e, t=t: e.tensor_copy(out=XS[t][:, cc * 128:(cc + 1) * 128], in_=pt_[:, t * 128:(t + 1) * 128]), [pt_], [XS[t]])
                else:
                    S.op("act", lambda e: e.activation(out=BCT[:, cc - 8, 0:T], in_=ACW[:, 0:T], func=AF.Silu), [ACW], [BCT])
                if kind == "own" and gi == NOWN // 2 - 1:
                    pt2 = ps()
                    S.op("act", lambda e: e.activation(out=XCF[:], in_=XEW[:, 3 + 128:3 + 256], func=AF.Copy), [XEW], [XCF])
                    S.op("pe", lambda e: e.transpose(out=pt2[:, 0:128], in_=XCF[:], identity=IDF), [XCF, CST], [pt2])
                    cvb = DEC if cc < 8 else XSD
                    S.op("dve", lambda e: e.tensor_copy(out=cvb[:, (cc % 8) * 128:(cc % 8 + 1) * 128], in_=pt2[:, 0:128]), [pt2], [cvb])
                return
            assert nt == 1
            for t in range(nt):
                xe = XE[t]
                xef = xe[:].rearrange("p a b -> p (a b)")
                ac = AC[t]
                if sam:
                    ph = ps()
                    hsrc = STG2[0:48, cc * 128:(cc + 1) * 128] if cc < 8 else STG[0][0:48, (cc - 8) * 128:(cc - 7) * 128]
                    hb = STG2 if cc < 8 else STG[0]
                    S.op("pe", lambda e: e.transpose(out=ph[:, 0:48], in_=hsrc, identity=IDF[0:48, 0:48]), [hb, CST], [ph])
                    S.op("dve", lambda e: e.tensor_copy(out=xe[:, :, 0:3], in_=ph[:, 0:48].rearrange("p (b j) -> p b j", j=3)), [ph], [xe])
                    S.op("act", lambda e: e.activation(out=xe[:, :, 3:11], in_=p[:, 0:128].rearrange("p (b j) -> p b j", j=8), func=AF.Copy), [p], [xe])
                    xin = [xe[:, :, j2:j2 + 8] for j2 in range(4)]
                    aco = ac[:].rearrange("p (b j) -> p b j", j=8)
                    pre_cols = xe[:, :, 8:11]
                else:
                    if t == 0:
                        S.op("dve", lambda e, cc=cc: e.tensor_copy(out=xef[:, 0:3], in_=HX[:, cc, :]), [HX], [xe])
                    else:
                        xp = XE[0][:].rearrange("p a b -> p (a b)")
                        S.op("dve", lambda e: e.tensor_copy(out=xef[:, 0:3], in_=xp[:, 128:131]), [XE[0]], [xe])
                    S.op("act", lambda e, t=t: e.activation(out=xef[:, 3:131], in_=p[:, t * 128:(t + 1) * 128], func=AF.Copy), [p], [xe])
                    if t == nt - 1:
                        S.op("dve", lambda e, cc=cc: e.tensor_copy(out=HX[:, cc, :], in_=xef[:, 128:131]), [xe], [HX])
                    xin = [xef[:, j2:j2 + 128] for j2 in range(4)]
                    aco = ac[:]
                S.op("dve", lambda e, cc=cc: e.tensor_scalar(out=aco, in0=xin[0], scalar1=CW[:, cc, 0:1], scalar2=CW[:, cc, 4:5], op0=ALU.mult, op1=ALU.add),
                     [xe, CW], [ac])
                for j2 in range(1, 4):
                    S.op("dve", lambda e, cc=cc, j2=j2: e.scalar_tensor_tensor(out=aco, in0=xin[j2], scalar=CW[:, cc, j2:j2 + 1], in1=aco, op0=ALU.mult, op1=ALU.add),
                         [xe, CW, ac], [ac])
                if cc < 8:
                    S.op("act", lambda e: e.activation(out=XCF[:], in_=ac[:], func=AF.Silu), [ac], [XCF])
                    pt_ = ps()
                    S.op("pe", lambda e: e.transpose(out=pt_[:, 0:128], in_=XCF[:], identity=IDF), [XCF, CST], [pt_])
                    S.op("dve", lambda e, cc=cc, t=t: e.tensor_copy(out=XS[t][:, cc * 128:(cc + 1) * 128], in_=pt_[:, 0:128]), [pt_], [XS[t]])
                else:
                    S.op("act", lambda e, cc=cc, t=t: e.activation(out=BCT[:, cc - 8, t * 128:(t + 1) * 128], in_=ac[:], func=AF.Silu), [ac], [BCT])
                last_prompt = (kind == "own" and gi == NOWN // 2 - 1 and t == nt - 1)
                if last_prompt or sam:
                    pt2 = ps()
                    if sam:
                        S.op("act", lambda e: e.activation(out=XCF[:].rearrange("p (b j) -> p b j", j=8), in_=xe[:, :, 3:11], func=AF.Copy), [xe], [XCF])
                    else:
                        S.op("act", lambda e: e.activation(out=XCF[:], in_=xef[:, 3:131], func=AF.Copy), [xe], [XCF])
                    S.op("pe", lambda e: e.transpose(out=pt2[:, 0:128], in_=XCF[:], identity=IDF), [XCF, CST], [pt2])
                    cvb = DEC if cc < 8 else XSD
                    S.op("dve", lambda e, cc=cc: e.tensor_copy(out=cvb[:, (cc % 8) * 128:(cc % 8 + 1) * 128], in_=pt2[:, 0:128]), [pt2], [cvb])
        pend = None
        for cc in range(13):
            cur = None
            if cc < 12:
                if cc % 2 == 0:
                    slot = WA[wai[1] % 4]
                    wai[1] += 1
                    wload(slot, win_v[:, :, 4104 + cc * 128:4104 + cc * 128 + 256])
                j = cc % 2
                p = PS[4 + cc % 2]
                for kc in range(16):
                    S.op("pe", lambda e, kc=kc, j=j, slot=slot, p=p: e.matmul(p[:, 0:T], lhsT=slot[:, kc, j * 128:(j + 1) * 128], rhs=UT[:, kc, 0:T],
                                                                                start=(kc == 0), stop=(kc == 15)), [slot, UT], [p])
                cur = (cc, p)
            if pend is not None:
                conv_post(*pend)
            pend = cur
        if kind == "own" and gi == NOWN // 2 - 1:
            S.dma("sp", conv_p[:, 0:1024], DEC[125:128, :], DEC, False)
            S.dma("sp", conv_p[:, 1024:1536], XSD[125:128, 0:512], XSD, False)
        if sam:
            for b in range(16):
                S.dma("sp", conv_s[b * 3:b * 3 + 3, 0:1024], DEC[b * 8 + 5:b * 8 + 8, :], DEC, False)
                S.dma("sp", conv_s[b * 3:b * 3 + 3, 1024:1536], XSD[b * 8 + 5:b * 8 + 8, 0:512], XSD, False)
        for t in range(nt):
            da, acol, altm, dte, eac = ssd_tile(kind, t, full)
            if not sam:
                if full:
                    ssd_finish_y(t, eac)
                p1 = ps()
                S.op("pe", lambda e: e.matmul(p1[:, 16:32], lhsT=ONES, rhs=da, start=True, stop=True), [CST, SA], [p1])
                S.op("act", lambda e: e.activation(out=CD[:], in_=p1[:, 16:32], func=AF.Exp), [p1], [CD])
                for g in range(2):
                    S.op("pe", lambda e, g=g: e.matmul(PS[4 + g][:, :], lhsT=BTM[:, g, :], rhs=XDTE[:, g * 512:(g + 1) * 512], start=True, stop=True),
                         [BTM, XDTE], [PS[4 + g]])
                h3 = H[:].rearrange("p (h d) -> p h d", h=16)
                S.op("dve", lambda e: e.tensor_tensor(out=h3, in0=h3, in1=CD[:].unsqueeze(2).to_broadcast([128, 16, 64]), op=ALU.mult), [H, CD], [H])
                for g in range(2):
                    S.op("dve", lambda e, g=g: e.tensor_tensor(out=H[:, g * 512:(g + 1) * 512], in0=PS[4 + g][:, :], in1=H[:, g * 512:(g + 1) * 512], op=ALU.add),
                         [PS[4 + g], H], [H])
                S.op("act", lambda e: e.activation(out=HB[:], in_=H[:], func=AF.Copy), [H], [HB])
            else:
                sample_ssd(acol, eac)
        if kind == "pre" and gi == NPRE // 2 - 1:
            S.op("dve", lambda e: e.tensor_scalar(out=H[:], in0=H[:], scalar1=FLG[:, 0:1], scalar2=None, op0=ALU.mult), [H, FLG], [H])
            S.op("act", lambda e: e.activation(out=HB[:], in_=H[:], func=AF.Copy), [H], [HB])
        if kind == "own" and gi == NOWN // 2 - 1:
            state_out(ssm_p.rearrange("(c p) n -> p c n", p=128), H, H)
        if not full:
            return
        for t in range(nt):
            if sam:
                sample_attn()
            else:
                prompt_attn(gi, t)
        for t in range(nt):
            tr_bf(MIXv[t], 16, lambda kc0, n, pv3, t=t: S.op("act", lambda e: e.activation(
                out=UT[:, kc0:kc0 + n, t * 128:(t + 1) * 128], in_=pv3, func=AF.Copy), [curp[0]], [UT]), [MIXT[t]])
        proj_tm(nt, wout_v, 0, D, add_resid)
        norm_T(nt, w["xattn_norm"])
        proj_tm(nt, wq_v, 0, 512, to_stg)
        for t in range(nt):
            headnorm(t, 0, 4, GX)
            stg_T(t, 4, QTv, QT, scale=128 ** -0.5)
        for t in range(nt):
            xattn(sam, t)
        for t in range(nt):
            tr_bf(MIXv[t][:, 0:512], 4, lambda kc0, n, pv3, t=t: S.op("act", lambda e: e.activation(
                out=UT[:, kc0:kc0 + n, t * 128:(t + 1) * 128], in_=pv3, func=AF.Copy), [curp[0]], [UT]), [MIXT[t]])
        proj_tm(nt, wo_v, 0, D, add_resid, nkc=4)

    OB = [PS[4], PS[5], PS[6]]

    def o_region(h, n=129):
        return OB[h // 3][:, (h % 3) * 129:(h % 3) * 129 + n]

    def attn_finish(t, nheads, width=128):
        for h in range(nheads):
            S.op("dve", lambda e, h=h: e.reciprocal(out=RD[:, h:h + 1], in_=o_region(h)[:, 128:129]), [OB[h // 3]], [RD])
        for h in range(nheads):
            S.op("act", lambda e, h=h: e.activation(out=MIXv[t][:, h * 128:(h + 1) * 128], in_=o_region(h, 128), func=AF.Copy, scale=RD[:, h:h + 1]),
                 [OB[h // 3], RD], [MIXT[t]])

    kci = [0]

    def prompt_attn(gi, t):
        oi = gi * 2 + t
        nk = NPRE + oi + 1
        S.op("dve", lambda e: e.tensor_tensor(out=BIAS[:, 0:nk, :], in0=CREFS[:, t, :].unsqueeze(1).to_broadcast([128, nk, 8]), in1=CKT[:, 0:nk, :],
                                              op=ALU.subtract), [CREFS, CKT], [BIAS])
        if NPRE > 0:
            S.op("dve", lambda e: e.tensor_scalar(out=BIAS[:, 0:NPRE, :], in0=BIAS[:, 0:NPRE, :], scalar1=FLG[:, 1:2], scalar2=None, op0=ALU.add),
                 [BIAS, FLG], [BIAS])
        for ob in OB:
            S.op("dve", lambda e, ob=ob: e.memset(ob[:, :], 0.0), [], [ob])
        st = {}

        def stage_a(kt):
            kc_, vc_ = KC[kt % 2], VC[kt % 2]
            S.dma("sp", kc_[:], kts[kt].rearrange("p (h c) -> p h c", h=8), kc_, True, after=("k", kt))
            S.dma("sp", vc_[:].rearrange("p h c -> p (h c)"), vas[kt], vc_, True, after=("v", kt))
            diag = kt == nk - 1
            banks = []
            for hq in range(2):
                p = ps()
                banks.append(p)
                for j in range(4):
                    h = hq * 4 + j
                    S.op("pe", lambda e, j=j, h=h, p=p: e.matmul(p[:, j * 128:(j + 1) * 128], lhsT=kc_[:, h, :], rhs=QTv[:, h, t * 128:(t + 1) * 128],
                                                                   start=True, stop=not diag), [kc_, QT], [p])
                    if diag:
                        S.op("pe", lambda e, j=j, p=p: e.matmul(p[:, j * 128:(j + 1) * 128], lhsT=IDB[:], rhs=MNEG[:, 0, :], start=False, stop=True),
                             [IDB, MNEG], [p])
            st[kt] = (banks, vc_)

        def stage_b(kt):
            banks, vc_ = st.pop(kt)
            for hq in range(2):
                p = banks[hq]
                pt = PT[(kt * 2 + hq) % 4]
                for j in range(4):
                    h = hq * 4 + j
                    S.op("act", lambda e, j=j, h=h, p=p, pt=pt: e.activation(out=pt[:, j, :], in_=p[:, j * 128:(j + 1) * 128], func=AF.Exp,
                                                                               bias=BIAS[:, kt, h:h + 1], scale=1.0), [p, BIAS], [pt])
                for j in range(4):
                    h = hq * 4 + j
                    S.op("pe", lambda e, j=j, h=h, pt=pt: e.matmul(o_region(h), lhsT=pt[:, j, :], rhs=vc_[:, h, 0:129], start=False, stop=(kt == nk - 1),
                                                                     skip_group_check=True), [pt, vc_], [OB[h // 3]])

        stage_a(0)
        for kt in range(nk):
            if kt + 1 < nk:
                stage_a(kt + 1)
            stage_b(kt)
        attn_finish(t, 8)

    def xattn(sam, t):
        for ob in OB[0:2]:
            S.op("dve", lambda e, ob=ob: e.memset(ob[:, :], 0.0), [], [ob])
        if not sam:
            for mt_ in range(2):
                p = ps()
                pt = PT[mt_ % 4]
                for h in range(4):
                    S.op("pe", lambda e, h=h: e.matmul(p[:, h * 128:(h + 1) * 128], lhsT=MKT[:, h, mt_ * 128:(mt_ + 1) * 128],
                                                        rhs=QTv[:, h, t * 128:(t + 1) * 128], start=True, stop=True), [MKT, QT], [p])
                S.op("act", lambda e: e.activation(out=pt[:].rearrange("p a b -> p (a b)"), in_=p[:, :], func=AF.Exp), [p], [pt])
                for h in range(4):
                    S.op("pe", lambda e, h=h: e.matmul(o_region(h), lhsT=pt[:, h, :], rhs=MVA[:, mt_, h, 0:129], start=False, stop=(mt_ == 1),
                                                        skip_group_check=True), [pt, MVA], [OB[h // 3]])
        else:
            sample_xattn()
        attn_finish(t, 4)

    IDX = S.sb([128, 16 * NPG], I32, "IDX")
    IDXF = STG2
    if SAM:
        S.dma("sp", IDX[:], ptab.partition_broadcast(128), IDX, True)
        S.op("dve", lambda e: e.tensor_copy(out=IDXF[:, 0:16 * NPG], in_=IDX[:]), [IDX], [IDXF])
        S.op("dve", lambda e: e.tensor_scalar(out=IDXF[:, 0:16 * NPG], in0=IDXF[:, 0:16 * NPG], scalar1=128.0, scalar2=PIDX, op0=ALU.mult, op1=ALU.add), [IDXF, CST], [IDXF])
        S.op("dve", lambda e: e.tensor_copy(out=IDX[:], in_=IDXF[:, 0:16 * NPG]), [IDXF], [IDX])

    U32 = mybir.dt.uint32
    TMPS = [S.sb([128, 64], F32, f"TMPS{i}") for i in range(2)]
    LFP = [S.sb([128, 16, 8], F32, f"LFP{i}") for i in range(2)]
    CPX = S.sb([128, 17, 8], F32, "CPX")
    TOTP = S.sb([128, 16, 8], F32, "TOTP")
    BIASPS = [S.sb([128, 16, 8], F32, f"BIASP{i}") for i in range(2)]
    NEWTOT = S.sb([128, 16, 8], F32, "NEWTOT")
    LFB = S.sb([128, 16, 16], F32, "LFB")
    CDS = S.sb([128, 16, 16], F32, "CDS")
    BIASN = S.sb([128, 24], F32, "BIASN")
    CTPB = [S.sb([128, 2, 128], BF16, f"CTPB{i}") for i in range(2)]
    BTMB = [S.sb([128, 2, 128], BF16, f"BTMB{i}") for i in range(2)]

    def sample_ssd(acol, eac):
        da = SA[:, 0:16]
        S.op("dve", lambda e: e.tensor_tensor(out=LFB[:], in0=da.unsqueeze(1).to_broadcast([128, 16, 16]),
                                              in1=BDSEL.unsqueeze(2).to_broadcast([128, 16, 16]), op=ALU.mult), [SA, CST], [LFB])
        pc = ps()
        S.op("pe", lambda e: e.matmul(pc[:, 0:256], lhsT=ONES, rhs=LFB[:].rearrange("p a b -> p (a b)"), start=True, stop=True), [CST, LFB], [pc])
        S.op("act", lambda e: e.activation(out=CDS[:].rearrange("p a b -> p (a b)"), in_=pc[:, 0:256], func=AF.Exp), [pc], [CDS])
        H0L = [SCR[1], SCR[4]]
        H0T = [SCR[6], SCR[10]]
        HBS = [(HB, HB[:]), (H, H[:].bitcast(BF16)[:, 0:1024])]
        for b in range(16):
            h0l, h0t = H0L[b % 2], H0T[b % 2]
            hbb, hbv = HBS[b % 2]
            S.dma("sp", h0l[:].rearrange("p (c n) -> p c n", c=8), st_ssm[b].rearrange("(c p) n -> p c n", p=128), h0l, True)
            tr_f32(lambda j: h0l[:, j * 128:(j + 1) * 128], 8,
                   lambda b0, n, pv: S.op("act", lambda e: e.activation(out=h0t[:, b0 * 128:(b0 + n) * 128], in_=pv, func=AF.Copy), [curp[0]], [h0t]), [h0l])
            S.op("dve", lambda e: e.tensor_copy(out=hbv, in_=h0t[:]), [h0t], [hbb])
            ctp, btm = CTPB[b % 2], BTMB[b % 2]
            S.op("pool", lambda e: e.memset(ctp[:], 0.0), [], [ctp])
            S.op("pool", lambda e, b=b: e.tensor_copy(out=ctp[:, :, b * 8:(b + 1) * 8], in_=BCT[:, 2:4, b * 8:(b + 1) * 8]), [BCT], [ctp])
            S.op("dve", lambda e, b=b: e.tensor_scalar(out=btm[:].rearrange("p a b -> p (a b)"), in0=BTM[:].rearrange("p a b -> p (a b)"),
                                                         scalar1=BDSEL[:, b:b + 1], scalar2=None, op0=ALU.mult), [BTM, CST], [btm])
            for g in range(2):
                S.op("pe", lambda e, g=g, b=b: e.matmul(PS[6 + g][:, :], lhsT=ctp[:, g, :], rhs=hbv[:, g * 512:(g + 1) * 512], start=(b == 0), stop=(b == 15)),
                     [ctp, hbb], [PS[6 + g]])
            pS = [ps(), ps()]
            for g in range(2):
                S.op("pe", lambda e, g=g: e.matmul(pS[g][:, :], lhsT=btm[:, g, :], rhs=XDTE[:, g * 512:(g + 1) * 512], start=True, stop=True), [btm, XDTE], [pS[g]])
            h3 = h0t[:].rearrange("p (h d) -> p h d", h=16)
            S.op("dve", lambda e, b=b: e.tensor_tensor(out=h3, in0=h3, in1=CDS[:, b, :].unsqueeze(2).to_broadcast([128, 16, 64]), op=ALU.mult), [h0t, CDS], [h0t])
            for g in range(2):
                S.op("dve", lambda e, g=g: e.tensor_tensor(out=h0t[:, g * 512:(g + 1) * 512], in0=pS[g][:, :], in1=h0t[:, g * 512:(g + 1) * 512], op=ALU.add),
                     [pS[g], h0t], [h0t])
            state_out(ssm_s[b].rearrange("(c p) n -> p c n", p=128), h0t, h0t)
        ssd_finish_y(0, eac)

    def sample_attn():
        lf = SMF[0][:, 24:32]
        for ob in OB:
            S.op("dve", lambda e, ob=ob: e.memset(ob[:, :], 0.0), [], [ob])
        S.op("dve", lambda e: e.tensor_tensor(out=LFB[:, :, 0:8], in0=lf.unsqueeze(1).to_broadcast([128, 16, 8]),
                                              in1=BDSEL.unsqueeze(2).to_broadcast([128, 16, 8]), op=ALU.mult), [SMF[0], CST], [LFB])
        p = ps()
        S.op("pe", lambda e: e.matmul(p[:, 0:128].rearrange("p (a b) -> p a b", a=16), lhsT=ONES, rhs=LFB[:, :, 0:8], start=True, stop=True), [CST, LFB], [p])
        S.op("pe", lambda e: e.matmul(p[:, 128:136], lhsT=TRIBD, rhs=lf, start=True, stop=True), [CST, SMF[0]], [p])
        S.op("dve", lambda e: e.tensor_copy(out=NEWTOT[:].rearrange("p a b -> p (a b)"), in_=p[:, 0:128]), [p], [NEWTOT])
        S.op("dve", lambda e: e.tensor_copy(out=BIASN[:, 0:8], in_=p[:, 128:136]), [p], [BIASN])
        S.op("pe", lambda e: e.matmul(p[:, 136:144], lhsT=SELS, rhs=BIASN[:, 0:8], start=True, stop=True), [CST, BIASN], [p])
        S.op("dve", lambda e: e.tensor_tensor(out=BIASN[:, 8:16], in0=p[:, 136:144], in1=BIASN[:, 0:8], op=ALU.subtract), [p, BIASN], [BIASN])
        for hq in range(2):
            p = ps()
            pt = PT[hq]
            for j in range(4):
                h = hq * 4 + j
                S.op("pe", lambda e, j=j, h=h: e.matmul(p[:, j * 128:(j + 1) * 128], lhsT=KTGv[:, h, 0:128], rhs=QTv[:, h, 0:128], start=True, stop=False), [KTG, QT], [p])
                S.op("pe", lambda e, j=j: e.matmul(p[:, j * 128:(j + 1) * 128], lhsT=IDB[:], rhs=MNEG[:, 1, :], start=False, stop=True), [IDB, MNEG], [p])
            for j in range(4):
                h = hq * 4 + j
                S.op("act", lambda e, j=j, h=h: e.activation(out=pt[:, j, :], in_=p[:, j * 128:(j + 1) * 128], func=AF.Exp, bias=BIASN[:, 8 + h:9 + h], scale=1.0),
                     [p, BIASN], [pt])
            for j in range(4):
                h = hq * 4 + j
                S.op("pe", lambda e, j=j, h=h: e.matmul(o_region(h), lhsT=pt[:, j, :], rhs=VAG[:, 0, h, 0:129], start=False, stop=False, skip_group_check=True),
                     [pt, VAG], [OB[h // 3]])
        NS = 4
        KVS = WA
        KVv = [k_[:].bitcast(F32).rearrange("p a b -> p (a b)") for k_ in KVS]
        KTP = [XS[0], XS[1], SCR[4], SCR[0]]
        VPB = [(VC[0], VC[0][:]), (VC[1], VC[1][:]), (MVA, MVA[:].rearrange("p a b c -> p (a b) c")),
               (SCR[1], SCR[1][:].bitcast(BF16)[:, 0:1040].rearrange("p (h c) -> p h c", h=8))]
        S.op("dve", lambda e: e.memset(SCR[1][:].bitcast(BF16)[:, 0:1040], 1.0), [], [SCR[1]])
        PTZ = [(PT[0], PT[1]), (PT[2], PT[3])]
        pages = [(b, j) for b in range(16) for j in range(NPG)]
        NP_ = len(pages)
        lastb = [None, None]
        ktvs = {}

        def seq_setup(b):
            lfp = LFP[b % 2]
            for j in range(NPG):
                col = b * NPG + j
                S.dma("pool", lfp[:, j, :], pool_lf[:, :], lfp, True, indirect=bass.IndirectOffsetOnAxis(ap=IDX[:, col:col + 1].bitcast(U32), axis=0), reads=[IDX],
                      disjoint=(j > 0))
            p = ps()
            lfp2 = lfp[:, 0:NPG, :]
            S.op("pe", lambda e: e.matmul(p[:, 0:NPG * 8].rearrange("p (a b) -> p a b", a=NPG), lhsT=ONES, rhs=lfp2, start=True, stop=True), [CST, lfp], [p])
            S.op("pe", lambda e: e.matmul(p[:, 128:128 + NPG * 8].rearrange("p (a b) -> p a b", a=NPG), lhsT=TRI, rhs=lfp2, start=True, stop=True), [CST, lfp], [p])
            S.op("dve", lambda e: e.tensor_copy(out=TOTP[:, 0:NPG, :].rearrange("p a b -> p (a b)"), in_=p[:, 0:NPG * 8]), [p], [TOTP])
            S.op("dve", lambda e: e.memset(CPX[:, 0, :], 0.0), [], [CPX])
            for j in range(NPG):
                S.op("dve", lambda e, j=j: e.tensor_tensor(out=CPX[:, j + 1, :], in0=CPX[:, j, :], in1=TOTP[:, j, :], op=ALU.add), [CPX, TOTP], [CPX])
            bp = BIASPS[b % 2]
            S.op("dve", lambda e, b=b: e.tensor_tensor(out=BIASN[:, 16:24], in0=CPX[:, NPG, :], in1=NEWTOT[:, b, :], op=ALU.add), [CPX, NEWTOT], [BIASN])
            S.op("dve", lambda e: e.tensor_tensor(out=bp[:, 0:NPG, :], in0=BIASN[:, 16:24].unsqueeze(1).to_broadcast([128, NPG, 8]), in1=CPX[:, 0:NPG, :],
                                                  op=ALU.subtract), [BIASN, CPX], [bp])
            S.op("dve", lambda e: e.tensor_tensor(out=bp[:, 0:NPG, :], in0=bp[:, 0:NPG, :], in1=p[:, 128:128 + NPG * 8].rearrange("p (a b) -> p a b", a=NPG),
                                                  op=ALU.subtract), [bp, p], [bp])

        def stage_T(i):
            b, j = pages[i]
            if j == 0:
                seq_setup(b)
            col = b * NPG + j
            kv, kvv, ktp = KVS[i % NS], KVv[i % NS], KTP[i % NS]
            vb, vv = VPB[i % NS]
            off = bass.IndirectOffsetOnAxis(ap=IDX[:, col:col + 1].bitcast(U32), axis=0)
            S.dma("pool", kvv, pool_kv[:, :], kv, True, indirect=off, reads=[IDX])
            ktv = ktp[:].bitcast(BF16)[:, 0:1024].rearrange("p (h c) -> p h c", h=8)
            ktvs[i] = ktv
            tr_f32(lambda jj: kvv[:, jj * 128:(jj + 1) * 128], 8,
                   lambda b0, n, pv: S.op("act", lambda e: e.activation(out=ktv[:, b0:b0 + n, :], in_=pv.rearrange("p (j c) -> p j c", j=n), func=AF.Copy),
                                          [curp[0]], [ktp]), [kv])
            S.op("dve", lambda e: e.tensor_copy(out=vv[:, :, 0:128], in_=kvv[:, 1024:2048].rearrange("p (h d) -> p h d", h=8)), [kv], [vb])

        def stage_Q(i):
            b, j = pages[i]
            ktp, ktv = KTP[i % NS], ktvs[i]
            ptz = PTZ[i % 2]
            tmp = TMPS[i % 2]
            if lastb[i % 2] is not None and lastb[i % 2] != b:
                ob = lastb[i % 2]
                for z_ in ptz:
                    S.op("dve", lambda e, z_=z_, ob=ob: e.memset(z_[:, :, ob * 8:(ob + 1) * 8], 0.0), [], [z_])
            lastb[i % 2] = b
            p = ps()
            for h in range(8):
                S.op("pe", lambda e, h=h, b=b: e.matmul(p[:, h * 8:(h + 1) * 8], lhsT=ktv[:, h, :], rhs=QTv[:, h, b * 8:(b + 1) * 8], start=True, stop=True),
                     [ktp, QT], [p])
            S.op("dve", lambda e, j=j, b=b: e.tensor_tensor(out=tmp[:].rearrange("p (h q) -> p h q", h=8), in0=p[:, 0:64].rearrange("p (h q) -> p h q", h=8),
                                                             in1=BIASPS[b % 2][:, j, :].unsqueeze(2).to_broadcast([128, 8, 8]), op=ALU.add), [p, BIASPS[b % 2]], [tmp])
            for hh in range(2):
                S.op("act", lambda e, hh=hh, b=b: e.activation(out=ptz[hh][:, :, b * 8:(b + 1) * 8], in_=tmp[:, hh * 32:(hh + 1) * 32].rearrange("p (h q) -> p h q", h=4),
                                                                 func=AF.Exp), [tmp], [ptz[hh]])

        def stage_P(i):
            ptz = PTZ[i % 2]
            vb, vv = VPB[i % NS]
            for h in range(8):
                S.op("pe", lambda e, h=h: e.matmul(o_region(h), lhsT=ptz[h // 4][:, h % 4, :], rhs=vv[:, h, 0:129], start=False, stop=False, skip_group_check=True),
                     [ptz[h // 4], vb], [OB[h // 3]])

        for pt in PT:
            S.op("dve", lambda e, pt=pt: e.memset(pt[:], 0.0), [], [pt])
        for i in range(NP_ + 3):
            if i < NP_:
                stage_T(i)
            if 0 <= i - 2 < NP_:
                stage_Q(i - 2)
            if 0 <= i - 3 < NP_:
                stage_P(i - 3)
        attn_finish(0, 8)

    def sample_xattn():
        MKL = [SCR[0], SCR[1]]
        MVL = [SCR[3], SCR[4]]
        MKTB = [XS[0], XS[1]]
        MVAB = [(MVA, MVA[:]), (VC[0], VC[0][:].rearrange("p (a b) c -> p a b c", a=2))]
        for b in range(16):
            mkl, mvl, mktb = MKL[b % 2], MVL[b % 2], MKTB[b % 2]
            mvb, mvv = MVAB[b % 2]
            ptz = (PT[0], PT[1]) if b % 2 == 0 else (PT[2], PT[3])
            S.dma("sp", mkl[:].rearrange("p (t c) -> p t c", t=2), cmem_k[b].rearrange("(t p) c -> p t c", p=128), mkl, True)
            S.dma("sp", mvl[:].rearrange("p (t c) -> p t c", t=2), cmem_v[b].rearrange("(t p) c -> p t c", p=128), mvl, True)
            mkv = mktb[:].bitcast(BF16)[:, 0:1024].rearrange("p (h c) -> p h c", h=4)
            for mt_ in range(2):
                tr_f32(lambda jj, mt_=mt_: mkl[:, mt_ * 512 + jj * 128:mt_ * 512 + (jj + 1) * 128], 4,
                       lambda b0, n, pv, mt_=mt_: S.op("act", lambda e: e.activation(out=mkv[:, b0:b0 + n, mt_ * 128:(mt_ + 1) * 128],
                                                                                      in_=pv.rearrange("p (j c) -> p j c", j=n), func=AF.Copy), [curp[0]], [mktb]), [mkl])
            S.op("pool", lambda e: e.tensor_copy(out=mvv[:, :, :, 0:128], in_=mvl[:].rearrange("p (t h d) -> p t h d", t=2, h=4)), [mvl], [mvb])
            for z_ in ptz:
                S.op("pool", lambda e, z_=z_: e.memset(z_[:], 0.0), [], [z_])
            p = ps()
            for mt_ in range(2):
                for h in range(4):
                    S.op("pe", lambda e, h=h, mt_=mt_, b=b: e.matmul(p[:, mt_ * 32 + h * 8:mt_ * 32 + (h + 1) * 8], lhsT=mkv[:, h, mt_ * 128:(mt_ + 1) * 128],
                                                                       rhs=QTv[:, h, b * 8:(b + 1) * 8], start=True, stop=True), [mktb, QT], [p])
            for mt_ in range(2):
                S.op("act", lambda e, mt_=mt_, b=b: e.activation(out=ptz[mt_][:, :, b * 8:(b + 1) * 8], in_=p[:, mt_ * 32:(mt_ + 1) * 32].rearrange("p (h q) -> p h q", h=4),
                                                                   func=AF.Exp), [p], [ptz[mt_]])
            for mt_ in range(2):
                for h in range(4):
                    S.op("pe", lambda e, h=h, mt_=mt_: e.matmul(o_region(h), lhsT=ptz[mt_][:, h, :], rhs=mvv[:, mt_, h, 0:129], start=False, stop=False,
                                                                  skip_group_check=True), [ptz[mt_], mvb], [OB[h // 3]])

    def memkv():
        for t in range(2):
            for c in range(4):
                S.dma("sp", XG[t][c][:], mem_in[t * 128:(t + 1) * 128, c * 512:(c + 1) * 512], XG[t][c], True)
        norm_T(2, w["mem_norm"])
        proj_tm(2, wkv_v, 0, 512, to_stg)
        for t in range(2):
            headnorm(t, 1, 4, GX)
            S.dma("sp", memk_o[t * 128:(t + 1) * 128, :], STG[t][:, 0:512], STG[t], False)
            stg_T(t, 4, MKT, MKT)
        proj_tm(2, wkv_v, 512, 512, to_stg)
        for t in range(2):
            S.dma("sp", memv_o[t * 128:(t + 1) * 128, :], STG[t][:, 0:512], STG[t], False)
            S.op("pool", lambda e, t=t: e.tensor_copy(out=MVA[:, t, :, 0:128], in_=STG[t][:, 0:512].rearrange("p (h d) -> p h d", h=4)), [STG[t]], [MVA])

    memkv()
    for bgi in range((NPRE + 3) // 4):
        big_group("pre", bgi)
    nbo = (NOWN + 3) // 4
    merge = SAM and (NOWN - (nbo - 1) * 4) == 4
    for bgi in range(nbo):
        big_group("own", bgi, with_sam=(merge and bgi == nbo - 1))
    if SAM and not merge:
        big_group("sam", 0)
    S.finish()
    return nc


def make_consts():
    c = np.zeros((128, 1536), np.float32)
    k = np.arange(128)
    le = k[:, None] <= k[None, :]
    same = (k[:, None] // 8) == (k[None, :] // 8)
    c[:, 0:128] = np.eye(128)
    c[:, 128:256] = le
    c[:, 256:384] = le & same
    c[:, 384:512] = 1.0
    c[:, 512:640] = np.where(le, 0.0, NEG)
    c[:, 640:768] = np.where(le & same, 0.0, NEG)
    c[:, 768:896] = (k[:, None] == 127)
    c[:, 896:1024] = (k[:, None] == (k[None, :] // 8) * 8 + 7)
    c[:, 1024:1040] = (k[:, None] // 8) == np.arange(16)[None, :]
    c[:, 1040] = EPS
    c[:, 1041] = 1.0
    c[:, 1042] = k
    c2 = np.zeros((128, 16, 128), np.float32)
    c2[:] = ((k[None, :] // 8) == np.arange(16)[:, None])[None]
    return c, c2.reshape(128, 2048)


def core_inputs(inp, c, cfg, cst, cst2):
    s, h = c // 2, c % 2
    xp = inp["x_prompt"]
    xs = inp["x_sample"]
    npre, nown = cfg["npre"] * 128, cfg["nown"] * 128
    f32 = lambda a: np.ascontiguousarray(np.asarray(a, np.float32))
    flg = np.zeros((128, 4), np.float32)
    flg[:, 0] = 1.0 if h == 1 else 0.0
    flg[:, 1] = 0.0 if h == 1 else NEG
    m = {"x_own": f32(xp[s, h * nown:(h + 1) * nown]),
         "x_pre": f32(xp[s, 0:npre]) if h == 1 else np.zeros((npre, D), np.float32),
         "x_sam": f32(xs[c * 16:(c + 1) * 16]).reshape(128, D),
         "mem_in": f32(inp["mem_prompt"][s]),
         "cst": cst, "cst2": cst2, "flg": flg,
         "cwb": np.ascontiguousarray(np.concatenate([np.asarray(inp["conv_w"], np.float32)[0], np.asarray(inp["conv_b"], np.float32)], axis=0)),
         "pool_kv": inp["_pool_kv"], "pool_lf": inp["_pool_lf"],
         "ptab": np.ascontiguousarray(np.asarray(inp["page_table"], np.int32)[c * 16:(c + 1) * 16]).reshape(-1),
         "cmem_k": f32(inp["cache_mem_k"][0, c * 16:(c + 1) * 16]).reshape(16, 256, 512),
         "cmem_v": f32(inp["cache_mem_v"][0, c * 16:(c + 1) * 16]).reshape(16, 256, 512),
         "st_ssm": f32(inp["state_ssm"][0, c * 16:(c + 1) * 16]).reshape(16, 1024, 128),
         "st_conv": f32(inp["state_conv"][0, c * 16:(c + 1) * 16]).reshape(48, 1536)}
    for n in _WNAMES:
        m[n] = f32(inp[n][0])
    return m


def kernel(**inp):
    inp = dict(inp)
    npool = inp["cache_fox_k"].shape[1]
    inp["_pool_kv"] = np.concatenate([np.asarray(inp["cache_fox_k"], np.float32)[0].reshape(npool * 128, 1024),
                                      np.asarray(inp["cache_fox_v"], np.float32)[0].reshape(npool * 128, 1024)], axis=1)
    inp["_pool_lf"] = np.ascontiguousarray(np.asarray(inp["cache_fox_logf"], np.float32)[0]).reshape(npool * 128, 8)
    cfg = {"npre": 8, "nown": 8, "npages": inp["page_table"].shape[1], "pool_rows": npool * 128}
    nc = build(cfg)
    cst, cst2 = make_consts()
    in_maps = [core_inputs(inp, c, cfg, cst, cst2) for c in range(NCORES)]
    res = run_bass_kernel_spmd(nc, in_maps, core_ids=list(range(NCORES))).results
    B, L = 4, 2048
    z = lambda *s: np.zeros(s, np.float32)
    y_p, pk, pv, plf = z(B, L, D), z(1, B, L, 8, 128), z(1, B, L, 8, 128), z(1, B, L, 8)
    pssm, pconv, pmk, pmv = z(1, B, 16, 64, 128), z(1, B, 3, 1536), z(1, B, 256, 4, 128), z(1, B, 256, 4, 128)
    y_s, sk, sv, slf = z(128, 8, D), z(1, 128, 8, 8, 128), z(1, 128, 8, 8, 128), z(1, 128, 8, 8)
    sssm, sconv = z(1, 128, 16, 64, 128), z(1, 128, 3, 1536)
    for c in range(NCORES):
        s, h = c // 2, c % 2
        r = res[c]
        sl = slice(h * 1024, (h + 1) * 1024)
        y_p[s, sl] = r["y_own"]
        pk[0, s, sl] = r["k_own"].reshape(1024, 8, 128)
        pv[0, s, sl] = r["v_own"].reshape(1024, 8, 128)
        plf[0, s, sl] = r["lf_own"]
        if h == 0:
            pmk[0, s] = r["memk_o"].reshape(256, 4, 128)
            pmv[0, s] = r["memv_o"].reshape(256, 4, 128)
        else:
            pssm[0, s] = r["ssm_p"].reshape(16, 64, 128)
            pconv[0, s] = r["conv_p"]
        cs = slice(c * 16, (c + 1) * 16)
        y_s[cs] = r["y_sam"].reshape(16, 8, D)
        sk[0, cs] = r["k_sam"].reshape(16, 8, 8, 128)
        sv[0, cs] = r["v_sam"].reshape(16, 8, 8, 128)
        slf[0, cs] = r["lf_sam"].reshape(16, 8, 8)
        sssm[0, cs] = r["ssm_s"].reshape(16, 16, 64, 128)
        sconv[0, cs] = r["conv_s"].reshape(16, 3, 1536)
    return (y_p, y_s, pk, pv, plf, pssm, pconv, pmk, pmv, sk, sv, slf, sssm, sconv)
```

```python
import numpy as np
import concourse.bass as bass
import concourse.mybir as mybir
from concourse.bass_utils import run_bass_kernel_spmd

F32, BF16, I32 = mybir.dt.float32, mybir.dt.bfloat16, mybir.dt.int32
AF = mybir.ActivationFunctionType
ALU = mybir.AluOpType
AX = mybir.AxisListType

D = 2048
DFF = 5632
NF = DFF // 128
EPS = 1e-6
NCORES = 8
INW = 5656
NEG = -30000.0


class Buf:
    def __init__(self, t, psum=False):
        self.t = t
        self.psum = psum
        self.lw = None
        self.rd = {}
        self.sem = None
        self.dcnt = 0
        self.dwcnt = 0

    def __getitem__(self, k):
        return self.t[k]


class Sched:
    ENG = ("pe", "act", "dve", "pool", "sp")

    def __init__(self, nc):
        self.nc = nc
        self.e = {"pe": nc.tensor, "act": nc.scalar, "dve": nc.vector, "pool": nc.gpsimd, "sp": nc.sync}
        self.cnt = {e: 0 for e in self.ENG}
        self.sem = {e: nc.alloc_semaphore(name="sem_" + e) for e in self.ENG}
        self.known = {e: {f: 0 for f in self.ENG} for e in self.ENG}
        self.kdma = {e: {} for e in self.ENG}
        self.dbufs = []
        self.nsb = 0
        self.ddep = {}

    def sb(self, shape, dt=F32, name=None):
        self.nsb += 1
        return Buf(self.nc.alloc_sbuf_tensor(name or f"sb{self.nsb}", list(shape), dt))

    def _wait(self, e, f, idx, raw=False, force=False):
        if f == e and not force:
            if e in ("pe", "sp") or not raw:
                return
        if self.known[e][f] >= idx:
            return
        self.known[e][f] = idx
        self.e[e].wait_ge(self.sem[f], idx)

    def _wait_dma(self, e, b, cnt):
        if cnt == 0 or self.kdma[e].get(id(b), 0) >= cnt:
            return
        self.kdma[e][id(b)] = cnt
        self.e[e].wait_ge(b.sem, cnt)

    def op(self, e, fn, reads=(), writes=()):
        for b in reads:
            if b.lw:
                self._wait(e, b.lw[0], b.lw[1], raw=True)
            self._wait_dma(e, b, b.dwcnt)
            if b.psum:
                for f, i in b.rd.items():
                    self._wait(e, f, i)
        for b in writes:
            if b.lw:
                self._wait(e, b.lw[0], b.lw[1], raw=b.psum)
            for f, i in b.rd.items():
                self._wait(e, f, i)
            self._wait_dma(e, b, b.dcnt)
        self.cnt[e] += 1
        idx = self.cnt[e]
        fn(self.e[e]).then_inc(self.sem[e], 1)
        for b in reads:
            b.rd[e] = idx
        for b in writes:
            b.lw = (e, idx)
            b.rd = {}

    def dma(self, q, out, in_, b, load, after=None, mark=None, indirect=None, reads=(), disjoint=False):
        if b.sem is None:
            b.sem = self.nc.alloc_semaphore(name=f"dsem{len(self.dbufs)}")
            self.dbufs.append(b)
        if load:
            if b.lw:
                self._wait(q, b.lw[0], b.lw[1], force=True)
            for f, i in b.rd.items():
                self._wait(q, f, i, force=True)
            if not disjoint:
                self._wait_dma(q, b, b.dcnt)
        else:
            if b.lw:
                self._wait(q, b.lw[0], b.lw[1], force=True)
            self._wait_dma(q, b, b.dwcnt)
        for rb in reads:
            if rb.lw:
                self._wait(q, rb.lw[0], rb.lw[1], force=True)
            self._wait_dma(q, rb, rb.dwcnt)
        if after is not None and after in self.ddep:
            db, dc = self.ddep[after]
            self._wait_dma(q, db, dc)
        b.dcnt += 16
        if load:
            b.dwcnt = b.dcnt
            b.lw = None
            b.rd = {}
        if mark is not None:
            self.ddep[mark] = (b, b.dcnt)
        if indirect is not None:
            self.e[q].indirect_dma_start(out=out, out_offset=None, in_=in_, in_offset=indirect).then_inc(b.sem, 16)
        else:
            self.e[q].dma_start(out=out, in_=in_).then_inc(b.sem, 16)

    def finish(self):
        for b in self.dbufs:
            self._wait_dma("sp", b, b.dcnt)
        for f in self.ENG:
            if f != "sp" and self.cnt[f] > 0:
                self._wait("sp", f, self.cnt[f], force=True)


_WSHAPES = [("ffn1_w_gate", [D, DFF]), ("ffn1_w_up", [D, DFF]), ("ffn1_w_down", [DFF, D]),
            ("ffn2_w_gate", [D, DFF]), ("ffn2_w_up", [D, DFF]), ("ffn2_w_down", [DFF, D]),
            ("w_in", [D, INW]), ("w_out", [D, D]),
            ("ffn1_norm", [D]), ("mix_norm", [D]), ("xattn_norm", [D]), ("ffn2_norm", [D]),
            ("fox_q_norm", [128]), ("fox_k_norm", [128]), ("fox_b_f", [8]),
            ("mem_norm", [D]), ("xattn_w_kv", [D, 1024]), ("xattn_w_q", [D, 512]), ("xattn_w_o", [512, D]),
            ("xattn_q_norm", [128]), ("xattn_k_norm", [128]),
            ("ssd_dt_bias", [16]), ("ssd_A_log", [16]), ("ssd_D", [16]), ("ssd_out_norm", [1024])]
_WNAMES = [n for n, _ in _WSHAPES]


def build(cfg):
    nc = bass.Bass("TRN2", target_bir_lowering=False)
    S = Sched(nc)

    def din(name, shape, dt=F32):
        return nc.dram_tensor(name, list(shape), dt, kind="ExternalInput").ap()

    def dout(name, shape):
        return nc.dram_tensor(name, list(shape), F32, kind="ExternalOutput").ap()

    NPRE = cfg["npre"]
    NOWN = cfg["nown"]
    NPG = cfg["npages"]
    PR = cfg["pool_rows"]
    SAM = cfg.get("sam", True)
    NK = NPRE + NOWN
    x_pre = din("x_pre", [NPRE * 128, D])
    x_own = din("x_own", [NOWN * 128, D])
    x_sam = din("x_sam", [128, D])
    cst = din("cst", [128, 1536])
    cst2 = din("cst2", [128, 2048])
    flg = din("flg", [128, 4])
    cwb = din("cwb", [5, 1536])
    mem_in = din("mem_in", [256, D])
    pool_kv = din("pool_kv", [PR, 2048])
    pool_lf = din("pool_lf", [PR, 8])
    ptab = din("ptab", [16 * NPG], I32)
    cmem_k = din("cmem_k", [16, 256, 512])
    cmem_v = din("cmem_v", [16, 256, 512])
    st_ssm = din("st_ssm", [16, 1024, 128])
    st_conv = din("st_conv", [48, 1536])
    w = {n: din(n, shp) for n, shp in _WSHAPES}
    y_own = dout("y_own", [NOWN * 128, D])
    y_sam = dout("y_sam", [128, D])
    k_own = dout("k_own", [NOWN * 128, 1024])
    v_own = dout("v_own", [NOWN * 128, 1024])
    lf_own = dout("lf_own", [NOWN * 128, 8])
    k_sam = dout("k_sam", [128, 1024])
    v_sam = dout("v_sam", [128, 1024])
    lf_sam = dout("lf_sam", [128, 8])
    memk_o = dout("memk_o", [256, 512])
    memv_o = dout("memv_o", [256, 512])
    ssm_p = dout("ssm_p", [1024, 128])
    conv_p = dout("conv_p", [3, 1536])
    ssm_s = dout("ssm_s", [16, 1024, 128])
    conv_s = dout("conv_s", [48, 1536])
    kts = nc.dram_tensor("kts", [NK, 128, 1024], BF16, kind="Internal").ap()
    vas = nc.dram_tensor("vas", [NK, 128, 1040], BF16, kind="Internal").ap()

    CST = S.sb([128, 1536], F32, "CST")
    S.dma("sp", CST[:], cst[:, :], CST, True)
    IDF = CST[:, 0:128]
    TRI = CST[:, 128:256]
    TRIBD = CST[:, 256:384]
    ONES = CST[:, 384:512]
    SELP = CST[:, 768:896]
    SELS = CST[:, 896:1024]
    BDSEL = CST[:, 1024:1040]
    EPSC = CST[:, 1040:1041]
    ONEC = CST[:, 1041:1042]
    PIDX = CST[:, 1042:1043]
    IDB = S.sb([128, 128], BF16, "IDB")
    S.op("dve", lambda e: e.tensor_copy(out=IDB[:], in_=IDF), [CST], [IDB])
    MNEG = S.sb([128, 2, 128], BF16, "MNEG")
    S.op("dve", lambda e: e.tensor_copy(out=MNEG[:].rearrange("p a b -> p (a b)"), in_=CST[:, 512:768]), [CST], [MNEG])
    FLG = S.sb([128, 4], F32, "FLG")
    S.dma("sp", FLG[:], flg[:, :], FLG, True)

    PS = [Buf(nc.alloc_psum_tensor(f"ps{i}", [128, 512], F32), psum=True) for i in range(8)]
    psi = [0]

    def ps():
        b = PS[psi[0] % 4]
        psi[0] += 1
        return b

    TG = 256
    XG = [[S.sb([128, 512], F32, f"XG{t}_{c}") for c in range(4)] for t in range(5)]
    UT = S.sb([128, 16, TG], BF16, "UT")
    GB = S.sb([128, D], F32, "GB")
    JUNK = S.sb([128, 512], BF16, "JUNK")
    UN = S.sb([128, D], BF16, "UN")
    SS = [S.sb([128, 4], F32, f"SS{t}") for t in range(2)]
    WA = [S.sb([128, 16, 256], BF16, f"WA{i}") for i in range(4)]
    SCR = [S.sb([128, 1024], F32, f"SCR{i}") for i in range(13)]
    STG = SCR[0:2]
    STG2 = SCR[2]
    ZS = SCR[3:5]
    XS = SCR[5:7]
    KTG, QT = SCR[7], SCR[8]
    MIXT = SCR[9:11]
    XSD, YT = SCR[11], SCR[12]
    DEC = YT
    UT2 = [SCR[9], SCR[10]]
    UT2v = [u[:].bitcast(BF16).rearrange("p (k c) -> p k c", k=8) for u in UT2]
    HT = [SCR[7], SCR[8]]
    HTv = [h_[:].bitcast(BF16).rearrange("p (f c) -> p f c", f=4) for h_ in HT]
    SG = [SCR[11], SCR[12]]
    WBS = [SCR[i] for i in range(5)]
    UT3 = SCR[5]
    UT3v = UT3[:].bitcast(BF16).rearrange("p (k c) -> p k c", k=16)
    HTB = SCR[6]
    HTBv = HTB[:].bitcast(BF16)[:, 0:512].rearrange("p (f c) -> p f c", f=4)
    psi8 = [0]

    def ps8():
        b = PS[psi8[0] % 8]
        psi8[0] += 1
        return b
    WBv = [b_[:].bitcast(BF16) for b_ in WBS]
    wbi = [0]
    cur_tb = [0]
    KTGv = KTG[:].bitcast(BF16).rearrange("p (h c) -> p h c", h=8)
    QTv = QT[:].bitcast(BF16).rearrange("p (h c) -> p h c", h=8)
    MIXv = [m[:].bitcast(BF16) for m in MIXT]
    VAG = S.sb([128, 2, 8, 130], BF16, "VAG")
    KC = [S.sb([128, 8, 128], BF16, f"KC{i}") for i in range(2)]
    VC = [S.sb([128, 8, 130], BF16, f"VC{i}") for i in range(2)]
    PT = [S.sb([128, 4, 128], BF16, f"PT{i}") for i in range(4)]
    GQ = S.sb([128, 2, 128], F32, "GQ")
    GX = S.sb([128, 2, 128], F32, "GX")
    BFB = S.sb([128, 8], F32, "BFB")
    DTB = S.sb([128, 16], F32, "DTB")
    ANEG = S.sb([128, 16], F32, "ANEG")
    DBC = S.sb([128, 16], F32, "DBC")
    WFD = S.sb([128, 16, 24], BF16, "WFD")
    SM = [S.sb([128, 32], F32, f"SM{t}") for t in range(2)]
    SMF = [S.sb([128, 64], F32, f"SMF{t}") for t in range(2)]
    DTT = [S.sb([128, 16], F32, f"DTT{t}") for t in range(2)]
    CARRY = S.sb([128, 8], F32, "CARRY")
    CKT = S.sb([128, NK, 8], F32, "CKT")
    CREFS = S.sb([128, 2, 8], F32, "CREFS")
    BIAS = S.sb([128, NK, 8], F32, "BIAS")
    RD = S.sb([128, 8], F32, "RD")
    H = S.sb([128, 1024], F32, "H")
    HB = S.sb([128, 1024], BF16, "HB")
    HX = S.sb([128, 12, 3], F32, "HX")
    CW = S.sb([128, 12, 5], F32, "CW")
    CWL = GB
    XE = [S.sb([128, 16, 11], F32, f"XE{i}") for i in range(1)]
    XEW = S.sb([128, 3 + TG], F32, "XEW")
    ACW = S.sb([128, TG], F32, "ACW")
    XCW = S.sb([128, TG], F32, "XCW")
    AC = [S.sb([128, 128], F32, f"AC{i}") for i in range(1)]
    XCF = S.sb([128, 128], F32, "XCF")
    BCT = S.sb([128, 4, TG], BF16, "BCT")
    SA = S.sb([128, 128], F32, "SA")
    ALB = S.sb([128, 16], F32, "ALB")
    CD = S.sb([128, 16], F32, "CD")
    CBM = S.sb([128, 2, 128], F32, "CBM")
    MT = [S.sb([128, 4, 128], BF16, f"MT{i}") for i in range(2)]
    XDT = S.sb([128, 1024], BF16, "XDT")
    XDTE = S.sb([128, 1024], BF16, "XDTE")
    BTM = S.sb([128, 2, 128], BF16, "BTM")
    MKT = S.sb([128, 4, 256], BF16, "MKT")
    MVA = S.sb([128, 2, 4, 130], BF16, "MVA")
    wai = [0, 0]

    def bc8(ap128, H8=8):
        return ap128.unsqueeze(1).to_broadcast([128, H8, 128])

    S.dma("sp", GQ[:, 0, :], w["fox_q_norm"].partition_broadcast(128), GQ, True)
    S.dma("sp", GQ[:, 1, :], w["fox_k_norm"].partition_broadcast(128), GQ, True)
    S.dma("sp", GX[:, 0, :], w["xattn_q_norm"].partition_broadcast(128), GX, True)
    S.dma("sp", GX[:, 1, :], w["xattn_k_norm"].partition_broadcast(128), GX, True)
    S.dma("sp", BFB[:], w["fox_b_f"].partition_broadcast(128), BFB, True)
    S.dma("sp", DTB[:], w["ssd_dt_bias"].partition_broadcast(128), DTB, True)
    S.dma("sp", ANEG[:], w["ssd_A_log"].partition_broadcast(128), ANEG, True)
    S.dma("sp", DBC[:], w["ssd_D"].partition_broadcast(128), DBC, True)
    S.op("act", lambda e: e.activation(out=ANEG[:], in_=ANEG[:], func=AF.Exp), [ANEG], [ANEG])
    S.op("dve", lambda e: e.tensor_scalar(out=ANEG[:], in0=ANEG[:], scalar1=-1.0, scalar2=None, op0=ALU.mult), [ANEG], [ANEG])
    win_v = w["w_in"].rearrange("(kc p) f -> p kc f", p=128)
    wkv_v = w["xattn_w_kv"].rearrange("(kc p) f -> p kc f", p=128)
    wq_v = w["xattn_w_q"].rearrange("(kc p) f -> p kc f", p=128)
    wout_v = w["w_out"].rearrange("(kc p) f -> p kc f", p=128)
    wo_v = w["xattn_w_o"].rearrange("(kc p) f -> p kc f", p=128)
    S.dma("pool", WFD[:, :, 0:8], win_v[:, :, 3072:3080], WFD, True)
    S.dma("pool", WFD[:, :, 8:24], win_v[:, :, 5640:5656], WFD, True)
    S.dma("sp", CWL[0:5, 0:1536], cwb[:, :], CWL, True)
    pcw = ps()
    for cc in range(12):
        S.op("pe", lambda e, cc=cc: e.transpose(out=pcw[:, cc * 5:cc * 5 + 5], in_=CWL[0:5, cc * 128:(cc + 1) * 128], identity=IDF[0:5, 0:5]),
             [CWL, CST], [pcw])
    S.op("dve", lambda e: e.tensor_copy(out=CW[:].rearrange("p a b -> p (a b)"), in_=pcw[:, 0:60]), [pcw], [CW])
    for b_, v_ in ((CARRY, 0.0), (H, 0.0), (HX, 0.0)):
        S.op("dve", lambda e, b_=b_, v_=v_: e.memset(b_[:], v_), [], [b_])
    S.op("dve", lambda e: e.memset(HB[:], 0.0), [], [HB])
    for vb in [VAG] + VC + [MVA]:
        S.op("pool", lambda e, vb=vb: e.memset(vb[:], 1.0), [], [vb])

    class WT:
        def __init__(self, name, rearr):
            self.name = name
            self.f = w[name].rearrange(rearr, p=128)
            self.b = nc.dram_tensor(name + "_bf", list(w[name].shape), BF16, kind="Internal").ap().rearrange(rearr, p=128)
            self.done = set()

    WTS = {}

    def wload2(slot_buf, slot_ap, name, idx, key):
        if name not in WTS:
            WTS[name] = WT(name, "(kc p) f -> p kc f")
        wt = WTS[name]
        if key in wt.done:
            S.dma("pool", slot_ap, wt.b[idx], slot_buf, True, after=(name, key))
        else:
            S.dma("pool", slot_ap, wt.f[idx], slot_buf, True)
            S.dma("sp", wt.b[idx], slot_ap, slot_buf, False, mark=(name, key))
            wt.done.add(key)

    def wload(slot, src):
        S.dma("pool", slot[:, 0:src.shape[1], 0:src.shape[2]], src, slot, True)

    def norm_T(nt, gain, tiles=None, pos0=0):
        if tiles is None:
            tiles = [cur_tb[0] + t for t in range(nt)]
        S.dma("sp", GB[:], gain.partition_broadcast(128), GB, True)
        for i, xt in enumerate(tiles):
            pos = pos0 + i
            ss = SS[i % 2]
            for c in range(4):
                S.op("act", lambda e, xt=xt, c=c: e.activation(out=JUNK[:], in_=XG[xt][c][:], func=AF.Square,
                                                                accum_out=ss[:, c:c + 1]), [XG[xt][c]], [JUNK, ss])
            S.op("dve", lambda e: e.tensor_reduce(out=ss[:, 0:1], in_=ss[:, 0:4], axis=AX.X, op=ALU.add), [ss], [ss])
            S.op("act", lambda e: e.activation(out=ss[:, 1:2], in_=ss[:, 0:1], func=AF.Sqrt, bias=EPSC, scale=1.0 / D), [ss, CST], [ss])
            S.op("dve", lambda e: e.reciprocal(out=ss[:, 2:3], in_=ss[:, 1:2]), [ss], [ss])
            for c in range(4):
                S.op("dve", lambda e, xt=xt, c=c: e.scalar_tensor_tensor(out=UN[:, c * 512:(c + 1) * 512], in0=XG[xt][c][:], scalar=ss[:, 2:3],
                                                                          in1=GB[:, c * 512:(c + 1) * 512], op0=ALU.mult, op1=ALU.mult),
                     [XG[xt][c], ss, GB], [UN])

            def evac(kc0, n, pv3, pos=pos):
                if pos < 2:
                    S.op("act", lambda e: e.activation(out=UT[:, kc0:kc0 + n, pos * 128:(pos + 1) * 128], in_=pv3, func=AF.Copy), [curp[0]], [UT])
                elif pos == 4:
                    S.op("act", lambda e: e.activation(out=UT3v[:, kc0:kc0 + n, :], in_=pv3, func=AF.Copy), [curp[0]], [UT3])
                else:
                    u2 = UT2[kc0 // 8]
                    S.op("act", lambda e: e.activation(out=UT2v[kc0 // 8][:, 0:n, (pos - 2) * 128:(pos - 1) * 128], in_=pv3, func=AF.Copy), [curp[0]], [u2])
            tr_bf(UN[:], 16, evac, [UN])

    curp = [None]

    def tr_bf(src2d, nblk, evac, rbufs):
        for b0 in range(0, nblk, 8):
            n = min(8, nblk - b0)
            p = ps()
            curp[0] = p
            pv = p[:].bitcast(BF16)
            for j in range(n):
                S.op("pe", lambda e, j=j, b0=b0: e.transpose(out=pv[:, j * 128:(j + 1) * 128], in_=src2d[:, (b0 + j) * 128:(b0 + j + 1) * 128],
                                                              identity=IDB[:]), rbufs + [IDB], [p])
            evac(b0, n, pv[:, 0:n * 128].rearrange("p (j c) -> p j c", j=n))

    def tr_f32(src_fn, nblk, evac, rbufs, rows=128):
        for b0 in range(0, nblk, 4):
            n = min(4, nblk - b0)
            p = ps()
            curp[0] = p
            for j in range(n):
                S.op("pe", lambda e, j=j, b0=b0: e.transpose(out=p[:, j * rows:(j + 1) * rows], in_=src_fn(b0 + j), identity=IDF[0:rows, 0:rows]),
                     rbufs + [CST], [p])
            evac(b0, n, p[:, 0:n * rows])

    def ffn(tiles, wg, wu, wd):
        ntl = len(tiles)
        T = min(ntl, 4) * 128
        T0 = min(T, 256)
        five = ntl == 5
        wgv = wg.rearrange("(kc p) f -> p kc f", p=128)
        wuv = wu.rearrange("(kc p) f -> p kc f", p=128)
        wdv = wd.rearrange("(f p) c -> p f c", p=128)

        def gu(pb, pb2, slot, j):
            for kc in range(16):
                S.op("pe", lambda e, kc=kc: e.matmul(pb[:, 0:T0], lhsT=slot[:, kc, j * 128:(j + 1) * 128], rhs=UT[:, kc, 0:T0],
                                                       start=(kc == 0), stop=(kc == 15)), [slot, UT], [pb])
            if T > 256:
                for kc in range(16):
                    S.op("pe", lambda e, kc=kc: e.matmul(pb[:, 256:T], lhsT=slot[:, kc, j * 128:(j + 1) * 128], rhs=UT2v[kc // 8][:, kc % 8, 0:T - 256],
                                                           start=(kc == 0), stop=(kc == 15)), [slot, UT2[kc // 8]], [pb])
            if five:
                for kc in range(16):
                    S.op("pe", lambda e, kc=kc: e.matmul(pb2[:, 0:128], lhsT=slot[:, kc, j * 128:(j + 1) * 128], rhs=UT3v[:, kc, :],
                                                           start=(kc == 0), stop=(kc == 15)), [slot, UT3], [pb2])

        for sl in range(NF // 4):
            ht, htv = HT[sl % 2], HTv[sl % 2]
            for bb in range(2):
                blk = sl * 2 + bb
                g_s = WA[wai[0] % 2]
                u_s = WA[2 + wai[0] % 2]
                wai[0] += 1
                wload(g_s, wgv[:, :, blk * 256:(blk + 1) * 256])
                wload(u_s, wuv[:, :, blk * 256:(blk + 1) * 256])
                for j in range(2):
                    fi = bb * 2 + j
                    pg, pu = ps8(), ps8()
                    pg2, pu2 = (ps8(), ps8()) if five else (None, None)
                    gu(pg, pg2, g_s, j)
                    gu(pu, pu2, u_s, j)
                    sg = SG[fi % 2]
                    S.op("act", lambda e: e.activation(out=sg[:, 0:T], in_=pg[:, 0:T], func=AF.Silu), [pg], [sg])
                    S.op("dve", lambda e, fi=fi: e.tensor_tensor(out=htv[:, fi, 0:T], in0=sg[:, 0:T], in1=pu[:, 0:T], op=ALU.mult), [sg, pu], [ht])
                    if five:
                        S.op("act", lambda e: e.activation(out=sg[:, 512:640], in_=pg2[:, 0:128], func=AF.Silu), [pg2], [sg])
                        S.op("dve", lambda e, fi=fi: e.tensor_tensor(out=HTBv[:, fi, :], in0=sg[:, 512:640], in1=pu2[:, 0:128], op=ALU.mult), [sg, pu2], [HTB])
            wds = []
            for fi in range(4):
                k = wbi[0] % len(WBS)
                wbi[0] += 1
                S.dma("pool", WBv[k], wdv[:, sl * 4 + fi, :], WBS[k], True)
                wds.append(k)
            for i, xt in enumerate(tiles):
                for c in range(4):
                    po = ps8()
                    for fi in range(4):
                        k = wds[fi]
                        lh = htv[:, fi, i * 128:(i + 1) * 128] if i < 4 else HTBv[:, fi, :]
                        hb = ht if i < 4 else HTB
                        S.op("pe", lambda e, fi=fi, c=c, k=k, lh=lh: e.matmul(po[:, :], lhsT=lh, rhs=WBv[k][:, c * 512:(c + 1) * 512],
                                                                                start=(fi == 0), stop=(fi == 3)), [hb, WBS[k]], [po])
                    S.op("dve", lambda e, xt=xt, c=c: e.scalar_tensor_tensor(out=XG[xt][c][:], in0=po[:, :], scalar=0.5, in1=XG[xt][c][:],
                                                                              op0=ALU.mult, op1=ALU.add), [po, XG[xt][c]], [XG[xt][c]])

    def proj_tm(nt, wv, col0, ncols, sink, nkc=16, src=None):
        src = UT if src is None else src
        for b0 in range(0, ncols, 256):
            nb = min(256, ncols - b0)
            slot = WA[wai[1] % 4]
            wai[1] += 1
            if isinstance(wv, str):
                wload2(slot, slot[:, 0:nkc, 0:nb], wv, (slice(None), slice(None), slice(col0 + b0, col0 + b0 + nb)), col0 + b0)
            else:
                wload(slot, wv[:, :, col0 + b0:col0 + b0 + nb])
            for t in range(nt):
                p = ps()
                for kc in range(nkc):
                    S.op("pe", lambda e, kc=kc, t=t: e.matmul(p[:, 0:nb], lhsT=src[:, kc, t * 128:(t + 1) * 128], rhs=slot[:, kc, 0:nb],
                                                                start=(kc == 0), stop=(kc == nkc - 1)), [src, slot], [p])
                sink(t, b0, nb, p)

    def to_stg(t, b0, nb, p):
        S.op("act", lambda e: e.activation(out=STG[t][:, b0:b0 + nb], in_=p[:, 0:nb], func=AF.Copy), [p], [STG[t]])

    def add_resid(t, b0, nb, p):
        c = b0 // 512
        o = b0 % 512
        xt = cur_tb[0] + t
        S.op("dve", lambda e: e.tensor_tensor(out=XG[xt][c][:, o:o + nb], in0=p[:, 0:nb], in1=XG[xt][c][:, o:o + nb], op=ALU.add),
             [p, XG[xt][c]], [XG[xt][c]])

    def headnorm(t, which, H8=8, G=None, xb=None):
        G = GQ if G is None else G
        xb = STG[t] if xb is None else xb
        W = H8 * 128
        x3 = xb[:, 0:W].rearrange("p (h d) -> p h d", h=H8)
        s3 = STG2[:, 0:W].rearrange("p (h d) -> p h d", h=H8)
        sm = SM[t]
        S.op("dve", lambda e: e.tensor_tensor(out=STG2[:, 0:W], in0=xb[:, 0:W], in1=xb[:, 0:W], op=ALU.mult), [xb], [STG2])
        S.op("dve", lambda e: e.tensor_reduce(out=sm[:, 0:H8], in_=s3, axis=AX.X, op=ALU.add), [STG2], [sm])
        S.op("act", lambda e: e.activation(out=sm[:, 8:8 + H8], in_=sm[:, 0:H8], func=AF.Sqrt, bias=EPSC, scale=1.0 / 128), [sm, CST], [sm])
        S.op("dve", lambda e: e.reciprocal(out=sm[:, 16:16 + H8], in_=sm[:, 8:8 + H8]), [sm], [sm])
        S.op("dve", lambda e: e.tensor_tensor(out=x3, in0=x3, in1=sm[:, 16:16 + H8].unsqueeze(2).to_broadcast([128, H8, 128]), op=ALU.mult),
             [xb, sm], [xb])
        S.op("dve", lambda e: e.tensor_tensor(out=x3, in0=x3, in1=bc8(G[:, which, :], H8), op=ALU.mult), [xb, G], [xb])

    def stg_T(t, nh, dstv, dbuf, scale=1.0, xb=None):
        xb = STG[t] if xb is None else xb
        tr_f32(lambda j: xb[:, j * 128:(j + 1) * 128], nh,
               lambda b0, n, pv: S.op("act", lambda e: e.activation(out=dstv[:, b0:b0 + n, t * 128:(t + 1) * 128],
                                                                      in_=pv.rearrange("p (j c) -> p j c", j=n), func=AF.Copy, scale=scale),
                                      [curp[0]], [dbuf]), [xb])

    def to_buf(bufs):
        def sink(t, b0, nb, p):
            S.op("act", lambda e: e.activation(out=bufs[t][:, b0:b0 + nb], in_=p[:, 0:nb], func=AF.Copy), [p], [bufs[t]])
        return sink

    def ssd_tile(kind, t, want_y):
        sam = kind == "sam"
        tri = TRIBD if sam else TRI
        sel = SELS if sam else SELP
        c0 = t * 128
        dt = SA[:, 80:96]
        da, acol, altm, dte, eac = SA[:, 0:16], SA[:, 16:32], SA[:, 32:48], SA[:, 48:64], SA[:, 64:80]
        S.op("dve", lambda e: e.tensor_copy(out=dt, in_=DTT[t][:]), [DTT[t]], [SA])
        S.op("dve", lambda e: e.tensor_tensor(out=da, in0=dt, in1=ANEG[:], op=ALU.mult), [SA, ANEG], [SA])
        p1 = ps()
        S.op("pe", lambda e: e.matmul(p1[:, 0:16], lhsT=tri, rhs=da, start=True, stop=True), [CST, SA], [p1])
        S.op("dve", lambda e: e.tensor_copy(out=acol, in_=p1[:, 0:16]), [p1], [SA])
        S.op("pe", lambda e: e.matmul(p1[:, 16:32], lhsT=sel, rhs=acol, start=True, stop=True), [CST, SA], [p1])
        S.op("dve", lambda e: e.tensor_copy(out=altm, in_=p1[:, 16:32]), [p1], [SA])
        S.op("dve", lambda e: e.tensor_tensor(out=dte, in0=altm, in1=acol, op=ALU.subtract), [SA], [SA])
        S.op("act", lambda e: e.activation(out=dte, in_=dte, func=AF.Exp), [SA], [SA])
        S.op("act", lambda e: e.activation(out=eac, in_=acol, func=AF.Exp), [SA], [SA])
        xs3 = XS[t][:].rearrange("p (h d) -> p h d", h=16)
        S.op("dve", lambda e: e.tensor_tensor(out=XDT[:].rearrange("p (h d) -> p h d", h=16), in0=xs3,
                                              in1=dt.unsqueeze(2).to_broadcast([128, 16, 64]), op=ALU.mult), [XS[t], SA], [XDT])
        S.op("dve", lambda e: e.tensor_tensor(out=XDTE[:].rearrange("p (h d) -> p h d", h=16), in0=XDT[:].rearrange("p (h d) -> p h d", h=16),
                                              in1=dte.unsqueeze(2).to_broadcast([128, 16, 64]), op=ALU.mult), [XDT, SA], [XDTE])
        if want_y:
            S.op("dve", lambda e: e.tensor_tensor(out=XSD[:].rearrange("p (h d) -> p h d", h=16), in0=xs3,
                                                  in1=DBC[:].unsqueeze(2).to_broadcast([128, 16, 64]), op=ALU.mult), [XS[t], DBC], [XSD])
            pc = ps()
            for g in range(2):
                S.op("pe", lambda e, g=g: e.matmul(pc[:, g * 128:(g + 1) * 128], lhsT=BCT[:, g, c0:c0 + 128], rhs=BCT[:, 2 + g, c0:c0 + 128],
                                                    start=True, stop=True), [BCT], [pc])
            S.op("dve", lambda e: e.tensor_tensor(out=CBM[:], in0=pc[:, 0:256].rearrange("p (g c) -> p g c", g=2),
                                                  in1=tri.unsqueeze(1).to_broadcast([128, 2, 128]), op=ALU.mult), [pc, CST], [CBM])
            for hq in range(4):
                pa = ps()
                for j in range(4):
                    h = hq * 4 + j
                    S.op("pe", lambda e, j=j, h=h: e.matmul(pa[:, j * 128:(j + 1) * 128], lhsT=da[:, h:h + 1].to_broadcast([128, 128]), rhs=tri,
                                                              start=True, stop=True), [SA, CST], [pa])
                dec3 = DEC[:, 0:512].rearrange("p (j c) -> p j c", j=4)
                for j in range(4):
                    h = hq * 4 + j
                    S.op("dve", lambda e, j=j, h=h: e.tensor_scalar(out=dec3[:, j, :], in0=pa[:, j * 128:(j + 1) * 128], scalar1=acol[:, h:h + 1],
                                                                      scalar2=0.0, op0=ALU.subtract, op1=ALU.min), [pa, SA], [DEC])
                S.op("act", lambda e: e.activation(out=DEC[:, 0:512], in_=DEC[:, 0:512], func=AF.Exp), [DEC], [DEC])
                mt = MT[hq % 2]
                g = hq // 2
                S.op("dve", lambda e, g=g: e.tensor_tensor(out=mt[:], in0=dec3, in1=CBM[:, g, :].unsqueeze(1).to_broadcast([128, 4, 128]), op=ALU.mult),
                     [DEC, CBM], [mt])
                for j in range(4):
                    h = hq * 4 + j
                    pb = PS[4 + h // 8]
                    S.op("pe", lambda e, j=j, h=h, pb=pb: e.matmul(pb[:, (h % 8) * 64:(h % 8 + 1) * 64], lhsT=mt[:, j, :], rhs=XDT[:, h * 64:(h + 1) * 64],
                                                                     start=True, stop=True), [mt, XDT], [pb])
            if not sam:
                for g in range(2):
                    S.op("pe", lambda e, g=g: e.matmul(PS[6 + g][:, :], lhsT=BCT[:, 2 + g, c0:c0 + 128], rhs=HB[:, g * 512:(g + 1) * 512],
                                                        start=True, stop=True), [BCT, HB], [PS[6 + g]])
        pbt = ps()
        pbv = pbt[:].bitcast(BF16)
        for g in range(2):
            S.op("pe", lambda e, g=g: e.transpose(out=pbv[:, g * 128:(g + 1) * 128], in_=BCT[:, g, c0:c0 + 128], identity=IDB[:]), [BCT, IDB], [pbt])
        S.op("act", lambda e: e.activation(out=BTM[:].rearrange("p a b -> p (a b)"), in_=pbv[:, 0:256], func=AF.Copy), [pbt], [BTM])
        return da, acol, altm, dte, eac

    def ssd_finish_y(t, eac):
        y3 = YT[:].rearrange("p (h d) -> p h d", h=16)
        for g in range(2):
            S.op("dve", lambda e, g=g: e.tensor_tensor(out=y3[:, g * 8:(g + 1) * 8, :], in0=PS[6 + g][:, :].rearrange("p (h d) -> p h d", h=8),
                                                        in1=eac[:, g * 8:(g + 1) * 8].unsqueeze(2).to_broadcast([128, 8, 64]), op=ALU.mult),
                 [PS[6 + g], SA], [YT])
        S.op("dve", lambda e: e.tensor_tensor(out=YT[:], in0=YT[:], in1=XSD[:], op=ALU.add), [YT, XSD], [YT])
        for g in range(2):
            S.op("dve", lambda e, g=g: e.tensor_tensor(out=YT[:, g * 512:(g + 1) * 512], in0=PS[4 + g][:, :], in1=YT[:, g * 512:(g + 1) * 512], op=ALU.add),
                 [PS[4 + g], YT], [YT])
        S.op("dve", lambda e: e.tensor_tensor(out=YT[:], in0=YT[:], in1=ZS[t][:], op=ALU.mult), [YT, ZS[t]], [YT])
        ss = SS[t]
        S.dma("sp", GB[:, 0:1024], w["ssd_out_norm"].partition_broadcast(128), GB, True)
        S.op("act", lambda e: e.activation(out=STG2[:], in_=YT[:], func=AF.Square, accum_out=ss[:, 0:1]), [YT], [STG2, ss])
        S.op("act", lambda e: e.activation(out=ss[:, 1:2], in_=ss[:, 0:1], func=AF.Sqrt, bias=EPSC, scale=1.0 / 1024), [ss, CST], [ss])
        S.op("dve", lambda e: e.reciprocal(out=ss[:, 2:3], in_=ss[:, 1:2]), [ss], [ss])
        S.op("dve", lambda e: e.scalar_tensor_tensor(out=MIXv[t][:, 1024:2048], in0=YT[:], scalar=ss[:, 2:3], in1=GB[:, 0:1024], op0=ALU.mult, op1=ALU.mult),
             [YT, ss, GB], [MIXT[t]])

    def state_out(dst_view, hsrc, hbuf):
        ho = STG2
        tr_f32(lambda j: hsrc[:, j * 128:(j + 1) * 128], 8,
               lambda b0, n, pv: S.op("act", lambda e: e.activation(out=ho[:, b0 * 128:(b0 + n) * 128], in_=pv, func=AF.Copy), [curp[0]], [ho]), [hbuf])
        S.dma("sp", dst_view, ho[:].rearrange("p (c n) -> p c n", c=8), ho, False)

    def big_group(kind, bgi, with_sam=False):
        sam = kind == "sam"
        full = kind != "pre"
        ntot = 1 if sam else (NPRE if kind == "pre" else NOWN)
        ntl = 1 if sam else min(4, ntot - bgi * 4)
        xsrc = {"pre": x_pre, "own": x_own, "sam": x_sam}[kind]
        rb = bgi * 512
        tiles = list(range(ntl))
        for t in tiles:
            for c in range(4):
                S.dma("sp", XG[t][c][:], xsrc[rb + t * 128:rb + (t + 1) * 128, c * 512:(c + 1) * 512], XG[t][c], True)
        if with_sam:
            for c in range(4):
                S.dma("sp", XG[4][c][:], x_sam[0:128, c * 512:(c + 1) * 512], XG[4][c], True)
            tiles = tiles + [4]
        poss = list(range(ntl)) + ([4] if with_sam else [])
        for xt, pos in zip(tiles, poss):
            norm_T(1, w["ffn1_norm"], [xt], pos)
        ffn(tiles, w["ffn1_w_gate"], w["ffn1_w_up"], w["ffn1_w_down"])
        for sub in range((ntl + 1) // 2):
            cur_tb[0] = sub * 2
            mixer(kind, bgi * 2 + sub, min(2, ntl - sub * 2))
        if with_sam:
            cur_tb[0] = 4
            mixer("sam", 0, 1)
        cur_tb[0] = 0
        if not full:
            return
        for xt, pos in zip(tiles, poss):
            norm_T(1, w["ffn2_norm"], [xt], pos)
        ffn(tiles, w["ffn2_w_gate"], w["ffn2_w_up"], w["ffn2_w_down"])
        ydst = {"own": y_own, "sam": y_sam}[kind]
        for t in range(ntl):
            for c in range(4):
                S.dma("sp", ydst[rb + t * 128:rb + (t + 1) * 128, c * 512:(c + 1) * 512], XG[t][c][:], XG[t][c], False)
        if with_sam:
            for c in range(4):
                S.dma("sp", y_sam[0:128, c * 512:(c + 1) * 512], XG[4][c][:], XG[4][c], False)

    def mixer(kind, gi, nt):
        sam = kind == "sam"
        T = nt * 128
        full = kind != "pre"
        r0 = gi * 256
        norm_T(nt, w["mix_norm"])
        kdst = {"own": k_own, "sam": k_sam}.get(kind)
        vdst = {"own": v_own, "sam": v_sam}.get(kind)
        ldst = {"own": lf_own, "sam": lf_sam}.get(kind)
        kt0 = (0 if kind == "pre" else NPRE) + gi * 2
        proj_tm(nt, "w_in", 1024, 1024, to_stg)
        proj_tm(nt, "w_in", 2048, 1024, to_buf(XS))
        if full:
            proj_tm(nt, "w_in", 0, 1024, to_buf(MIXT))
            proj_tm(nt, "w_in", 3080, 1024,
                    lambda t, b0, nb, p: S.op("act", lambda e: e.activation(out=ZS[t][:, b0:b0 + nb], in_=p[:, 0:nb], func=AF.Silu), [p], [ZS[t]]))
        for t in range(nt):
            headnorm(t, 1)
            if kdst is not None:
                S.dma("sp", kdst[r0 + t * 128:r0 + (t + 1) * 128, :], STG[t][:], STG[t], False)
            stg_T(t, 8, KTGv, KTG)
        if not sam:
            for t in range(nt):
                S.dma("sp", kts[kt0 + t].rearrange("p (h c) -> p h c", h=8), KTGv[:, :, t * 128:(t + 1) * 128], KTG, False, mark=("k", kt0 + t))
        for t in range(nt):
            if vdst is not None:
                S.dma("sp", vdst[r0 + t * 128:r0 + (t + 1) * 128, :], XS[t][:], XS[t], False)
            S.op("pool", lambda e, t=t: e.tensor_copy(out=VAG[:, t, :, 0:128], in_=XS[t][:].rearrange("p (h d) -> p h d", h=8)), [XS[t]], [VAG])
            if not sam:
                S.dma("sp", vas[kt0 + t], VAG[:, t, :, :].rearrange("p h c -> p (h c)"), VAG, False, mark=("v", kt0 + t))
        if full:
            for t in range(nt):
                headnorm(t, 0, xb=MIXT[t])
                stg_T(t, 8, QTv, QT, scale=128 ** -0.5, xb=MIXT[t])
        for t in range(nt):
            p = ps()
            sm = SMF[t]
            for kc in range(16):
                S.op("pe", lambda e, kc=kc, t=t: e.matmul(p[:, 0:24], lhsT=UT[:, kc, t * 128:(t + 1) * 128], rhs=WFD[:, kc, :],
                                                            start=(kc == 0), stop=(kc == 15)), [UT, WFD], [p])
            S.op("dve", lambda e: e.tensor_tensor(out=sm[:, 0:8], in0=p[:, 0:8], in1=BFB[:], op=ALU.add), [p, BFB], [sm])
            S.op("dve", lambda e: e.tensor_tensor(out=sm[:, 32:48], in0=p[:, 8:24], in1=DTB[:], op=ALU.add), [p, DTB], [sm])
            S.op("act", lambda e: e.activation(out=sm[:, 8:16], in_=sm[:, 0:8], func=AF.Exp, scale=-1.0), [sm], [sm])
            S.op("act", lambda e: e.activation(out=sm[:, 48:64], in_=sm[:, 32:48], func=AF.Exp), [sm], [sm])
            S.op("act", lambda e: e.activation(out=sm[:, 16:24], in_=sm[:, 8:16], func=AF.Ln, bias=ONEC, scale=1.0), [sm, CST], [sm])
            S.op("act", lambda e, t=t: e.activation(out=DTT[t][:], in_=sm[:, 48:64], func=AF.Ln, bias=ONEC, scale=1.0), [sm, CST], [DTT[t]])
            S.op("dve", lambda e: e.tensor_scalar(out=sm[:, 24:32], in0=sm[:, 16:24], scalar1=-1.0, scalar2=None, op0=ALU.mult), [sm], [sm])
            if ldst is not None:
                S.dma("sp", ldst[r0 + t * 128:r0 + (t + 1) * 128, :], sm[:, 24:32], sm, False)
        if not sam:
            for t in range(nt):
                kt = kt0 + t
                p = ps()
                lf = SMF[t][:, 24:32]
                S.op("pe", lambda e: e.matmul(p[:, 0:8], lhsT=TRI, rhs=lf, start=True, stop=True), [CST, SMF[t]], [p])
                S.op("pe", lambda e: e.matmul(p[:, 8:16], lhsT=ONES, rhs=lf, start=True, stop=True), [CST, SMF[t]], [p])
                S.op("dve", lambda e, kt=kt: e.tensor_tensor(out=CKT[:, kt, :], in0=p[:, 0:8], in1=CARRY[:], op=ALU.add), [p, CARRY], [CKT])
                S.op("dve", lambda e: e.tensor_tensor(out=CARRY[:], in0=p[:, 8:16], in1=CARRY[:], op=ALU.add), [p, CARRY], [CARRY])
                S.op("dve", lambda e, t=t: e.tensor_copy(out=CREFS[:, t, :], in_=CARRY[:]), [CARRY], [CREFS])
        if sam:
            S.dma("sp", STG2[0:48, :], st_conv[:, 0:1024], STG2, True)
            S.dma("sp", STG[0][0:48, 0:512], st_conv[:, 1024:1536], STG[0], True)
        def conv_post(cc, p):
            if (not sam) and nt == 2:
                S.op("dve", lambda e: e.tensor_copy(out=XEW[:, 0:3], in_=HX[:, cc, :]), [HX], [XEW])
                S.op("act", lambda e: e.activation(out=XEW[:, 3:3 + T], in_=p[:, 0:T], func=AF.Copy), [p], [XEW])
                S.op("dve", lambda e: e.tensor_copy(out=HX[:, cc, :], in_=XEW[:, T:T + 3]), [XEW], [HX])
                S.op("dve", lambda e: e.tensor_scalar(out=ACW[:, 0:T], in0=XEW[:, 0:T], scalar1=CW[:, cc, 0:1], scalar2=CW[:, cc, 4:5], op0=ALU.mult, op1=ALU.add),
                     [XEW, CW], [ACW])
                for j2 in range(1, 4):
                    S.op("dve", lambda e, j2=j2: e.scalar_tensor_tensor(out=ACW[:, 0:T], in0=XEW[:, j2:j2 + T], scalar=CW[:, cc, j2:j2 + 1], in1=ACW[:, 0:T],
                                                                         op0=ALU.mult, op1=ALU.add), [XEW, CW, ACW], [ACW])
                if cc < 8:
                    S.op("act", lambda e: e.activation(out=XCW[:, 0:T], in_=ACW[:, 0:T], func=AF.Silu), [ACW], [XCW])
                    pt_ = ps()
                    for t in range(nt):
                        S.op("pe", lambda e, t=t: e.transpose(out=pt_[:, t * 128:(t + 1) * 128], in_=XCW[:, t * 128:(t + 1) * 128], identity=IDF), [XCW, CST], [pt_])
                    for t in range(nt):
                        S.op("dve", lambda e, t=t: e.tensor_copy(out=XS[t][:, cc * 128:(cc + 1) * 128], in_=pt_[:, t * 128:(t + 1) * 128]), [pt_], [XS[t]])
                else:
                    S.op("act", lambda e: e.activation(out=BCT[:, cc - 8, 0:T], in_=ACW[:, 0:T], func=AF.Silu), [ACW], [BCT])
                if kind == "own" and gi == NOWN // 2 - 1:
                    pt2 = ps()
                    S.op("act", lambda e: e.activation(out=XCF[:], in_=XEW[:, 3 + 128:3 + 256], func=AF.Copy), [XEW], [XCF])
                    S.op("pe", lambda e: e.transpose(out=pt2[:, 0:128], in_=XCF[:], identity=IDF), [XCF, CST], [pt2])
                    cvb = DEC if cc < 8 else XSD
                    S.op("dve", lambda e: e.tensor_copy(out=cvb[:, (cc % 8) * 128:(cc % 8 + 1) * 128], in_=pt2[:, 0:128]), [pt2], [cvb])
                return
            assert nt == 1
            for t in range(nt):
                xe = XE[t]
                xef = xe[:].rearrange("p a b -> p (a b)")
                ac = AC[t]
                if sam:
                    ph = ps()
                    hsrc = STG2[0:48, cc * 128:(cc + 1) * 128] if cc < 8 else STG[0][0:48, (cc - 8) * 128:(cc - 7) * 128]
                    hb = STG2 if cc < 8 else STG[0]
                    S.op("pe", lambda e: e.transpose(out=ph[:, 0:48], in_=hsrc, identity=IDF[0:48, 0:48]), [hb, CST], [ph])
                    S.op("dve", lambda e: e.tensor_copy(out=xe[:, :, 0:3], in_=ph[:, 0:48].rearrange("p (b j) -> p b j", j=3)), [ph], [xe])
                    S.op("act", lambda e: e.activation(out=xe[:, :, 3:11], in_=p[:, 0:128].rearrange("p (b j) -> p b j", j=8), func=AF.Copy), [p], [xe])
                    xin = [xe[:, :, j2:j2 + 8] for j2 in range(4)]
                    aco = ac[:].rearrange("p (b j) -> p b j", j=8)
                    pre_cols = xe[:, :, 8:11]
                else:
                    if t == 0:
                        S.op("dve", lambda e, cc=cc: e.tensor_copy(out=xef[:, 0:3], in_=HX[:, cc, :]), [HX], [xe])
                    else:
                        xp = XE[0][:].rearrange("p a b -> p (a b)")
                        S.op("dve", lambda e: e.tensor_copy(out=xef[:, 0:3], in_=xp[:, 128:131]), [XE[0]], [xe])
                    S.op("act", lambda e, t=t: e.activation(out=xef[:, 3:131], in_=p[:, t * 128:(t + 1) * 128], func=AF.Copy), [p], [xe])
                    if t == nt - 1:
                        S.op("dve", lambda e, cc=cc: e.tensor_copy(out=HX[:, cc, :], in_=xef[:, 128:131]), [xe], [HX])
                    xin = [xef[:, j2:j2 + 128] for j2 in range(4)]
                    aco = ac[:]
                S.op("dve", lambda e, cc=cc: e.tensor_scalar(out=aco, in0=xin[0], scalar1=CW[:, cc, 0:1], scalar2=CW[:, cc, 4:5], op0=ALU.mult, op1=ALU.add),
                     [xe, CW], [ac])
                for j2 in range(1, 4):
                    S.op("dve", lambda e, cc=cc, j2=j2: e.scalar_tensor_tensor(out=aco, in0=xin[j2], scalar=CW[:, cc, j2:j2 + 1], in1=aco, op0=ALU.mult, op1=ALU.add),
                         [xe, CW, ac], [ac])
                if cc < 8:
                    S.op("act", lambda e: e.activation(out=XCF[:], in_=ac[:], func=AF.Silu), [ac], [XCF])
                    pt_ = ps()
                    S.op("pe", lambda e: e.transpose(out=pt_[:, 0:128], in_=XCF[:], identity=IDF), [XCF, CST], [pt_])
                    S.op("dve", lambda e, cc=cc, t=t: e.tensor_copy(out=XS[t][:, cc * 128:(cc + 1) * 128], in_=pt_[:, 0:128]), [pt_], [XS[t]])
                else:
                    S.op("act", lambda e, cc=cc, t=t: e.activation(out=BCT[:, cc - 8, t * 128:(t + 1) * 128], in_=ac[:], func=AF.Silu), [ac], [BCT])
                last_prompt = (kind == "own" and gi == NOWN // 2 - 1 and t == nt - 1)
                if last_prompt or sam:
                    pt2 = ps()
                    if sam:
                        S.op("act", lambda e: e.activation(out=XCF[:].rearrange("p (b j) -> p b j", j=8), in_=xe[:, :, 3:11], func=AF.Copy), [xe], [XCF])
                    else:
                        S.op("act", lambda e: e.activation(out=XCF[:], in_=xef[:, 3:131], func=AF.Copy), [xe], [XCF])
                    S.op("pe", lambda e: e.transpose(out=pt2[:, 0:128], in_=XCF[:], identity=IDF), [XCF, CST], [pt2])
                    cvb = DEC if cc < 8 else XSD
                    S.op("dve", lambda e, cc=cc: e.tensor_copy(out=cvb[:, (cc % 8) * 128:(cc % 8 + 1) * 128], in_=pt2[:, 0:128]), [pt2], [cvb])
        pend = None
        for cc in range(13):
            cur = None
            if cc < 12:
                if cc % 2 == 0:
                    slot = WA[wai[1] % 4]
                    wai[1] += 1
                    wload2(slot, slot[:], "w_in", (slice(None), slice(None), slice(4104 + cc * 128, 4104 + cc * 128 + 256)), 4104 + cc * 128)
                j = cc % 2
                p = PS[4 + cc % 2]
                for kc in range(16):
                    S.op("pe", lambda e, kc=kc, j=j, slot=slot, p=p: e.matmul(p[:, 0:T], lhsT=slot[:, kc, j * 128:(j + 1) * 128], rhs=UT[:, kc, 0:T],
                                                                                start=(kc == 0), stop=(kc == 15)), [slot, UT], [p])
                cur = (cc, p)
            if pend is not None:
                conv_post(*pend)
            pend = cur
        if kind == "own" and gi == NOWN // 2 - 1:
            S.dma("sp", conv_p[:, 0:1024], DEC[125:128, :], DEC, False)
            S.dma("sp", conv_p[:, 1024:1536], XSD[125:128, 0:512], XSD, False)
        if sam:
            for b in range(16):
                S.dma("sp", conv_s[b * 3:b * 3 + 3, 0:1024], DEC[b * 8 + 5:b * 8 + 8, :], DEC, False)
                S.dma("sp", conv_s[b * 3:b * 3 + 3, 1024:1536], XSD[b * 8 + 5:b * 8 + 8, 0:512], XSD, False)
        for t in range(nt):
            da, acol, altm, dte, eac = ssd_tile(kind, t, full)
            if not sam:
                if full:
                    ssd_finish_y(t, eac)
                p1 = ps()
                S.op("pe", lambda e: e.matmul(p1[:, 16:32], lhsT=ONES, rhs=da, start=True, stop=True), [CST, SA], [p1])
                S.op("act", lambda e: e.activation(out=CD[:], in_=p1[:, 16:32], func=AF.Exp), [p1], [CD])
                for g in range(2):
                    S.op("pe", lambda e, g=g: e.matmul(PS[4 + g][:, :], lhsT=BTM[:, g, :], rhs=XDTE[:, g * 512:(g + 1) * 512], start=True, stop=True),
                         [BTM, XDTE], [PS[4 + g]])
                h3 = H[:].rearrange("p (h d) -> p h d", h=16)
                S.op("dve", lambda e: e.tensor_tensor(out=h3, in0=h3, in1=CD[:].unsqueeze(2).to_broadcast([128, 16, 64]), op=ALU.mult), [H, CD], [H])
                for g in range(2):
                    S.op("dve", lambda e, g=g: e.tensor_tensor(out=H[:, g * 512:(g + 1) * 512], in0=PS[4 + g][:, :], in1=H[:, g * 512:(g + 1) * 512], op=ALU.add),
                         [PS[4 + g], H], [H])
                S.op("act", lambda e: e.activation(out=HB[:], in_=H[:], func=AF.Copy), [H], [HB])
            else:
                sample_ssd(acol, eac)
        if kind == "pre" and gi == NPRE // 2 - 1:
            S.op("dve", lambda e: e.tensor_scalar(out=H[:], in0=H[:], scalar1=FLG[:, 0:1], scalar2=None, op0=ALU.mult), [H, FLG], [H])
            S.op("act", lambda e: e.activation(out=HB[:], in_=H[:], func=AF.Copy), [H], [HB])
        if kind == "own" and gi == NOWN // 2 - 1:
            state_out(ssm_p.rearrange("(c p) n -> p c n", p=128), H, H)
        if not full:
            return
        for t in range(nt):
            if sam:
                sample_attn()
            else:
                prompt_attn(gi, t)
        for t in range(nt):
            tr_bf(MIXv[t], 16, lambda kc0, n, pv3, t=t: S.op("act", lambda e: e.activation(
                out=UT[:, kc0:kc0 + n, t * 128:(t + 1) * 128], in_=pv3, func=AF.Copy), [curp[0]], [UT]), [MIXT[t]])
        proj_tm(nt, "w_out", 0, D, add_resid)
        norm_T(nt, w["xattn_norm"])
        proj_tm(nt, "xattn_w_q", 0, 512, to_stg)
        for t in range(nt):
            headnorm(t, 0, 4, GX)
            stg_T(t, 4, QTv, QT, scale=128 ** -0.5)
        for t in range(nt):
            xattn(sam, t)
        for t in range(nt):
            tr_bf(MIXv[t][:, 0:512], 4, lambda kc0, n, pv3, t=t: S.op("act", lambda e: e.activation(
                out=UT[:, kc0:kc0 + n, t * 128:(t + 1) * 128], in_=pv3, func=AF.Copy), [curp[0]], [UT]), [MIXT[t]])
        proj_tm(nt, "xattn_w_o", 0, D, add_resid, nkc=4)

    OB = [PS[4], PS[5], PS[6]]

    def o_region(h, n=129):
        return OB[h // 3][:, (h % 3) * 129:(h % 3) * 129 + n]

    def attn_finish(t, nheads, width=128):
        for h in range(nheads):
            S.op("dve", lambda e, h=h: e.reciprocal(out=RD[:, h:h + 1], in_=o_region(h)[:, 128:129]), [OB[h // 3]], [RD])
        for h in range(nheads):
            S.op("act", lambda e, h=h: e.activation(out=MIXv[t][:, h * 128:(h + 1) * 128], in_=o_region(h, 128), func=AF.Copy, scale=RD[:, h:h + 1]),
                 [OB[h // 3], RD], [MIXT[t]])

    kci = [0]

    def prompt_attn(gi, t):
        oi = gi * 2 + t
        nk = NPRE + oi + 1
        S.op("dve", lambda e: e.tensor_tensor(out=BIAS[:, 0:nk, :], in0=CREFS[:, t, :].unsqueeze(1).to_broadcast([128, nk, 8]), in1=CKT[:, 0:nk, :],
                                              op=ALU.subtract), [CREFS, CKT], [BIAS])
        if NPRE > 0:
            S.op("dve", lambda e: e.tensor_scalar(out=BIAS[:, 0:NPRE, :], in0=BIAS[:, 0:NPRE, :], scalar1=FLG[:, 1:2], scalar2=None, op0=ALU.add),
                 [BIAS, FLG], [BIAS])
        for ob in OB:
            S.op("dve", lambda e, ob=ob: e.memset(ob[:, :], 0.0), [], [ob])
        st = {}

        def stage_a(kt):
            kc_, vc_ = KC[kt % 2], VC[kt % 2]
            S.dma("sp", kc_[:], kts[kt].rearrange("p (h c) -> p h c", h=8), kc_, True, after=("k", kt))
            S.dma("sp", vc_[:].rearrange("p h c -> p (h c)"), vas[kt], vc_, True, after=("v", kt))
            diag = kt == nk - 1
            banks = []
            for hq in range(2):
                p = ps()
                banks.append(p)
                for j in range(4):
                    h = hq * 4 + j
                    S.op("pe", lambda e, j=j, h=h, p=p: e.matmul(p[:, j * 128:(j + 1) * 128], lhsT=kc_[:, h, :], rhs=QTv[:, h, t * 128:(t + 1) * 128],
                                                                   start=True, stop=not diag), [kc_, QT], [p])
                    if diag:
                        S.op("pe", lambda e, j=j, p=p: e.matmul(p[:, j * 128:(j + 1) * 128], lhsT=IDB[:], rhs=MNEG[:, 0, :], start=False, stop=True),
                             [IDB, MNEG], [p])
            st[kt] = (banks, vc_)

        def stage_b(kt):
            banks, vc_ = st.pop(kt)
            for hq in range(2):
                p = banks[hq]
                pt = PT[(kt * 2 + hq) % 4]
                for j in range(4):
                    h = hq * 4 + j
                    S.op("act", lambda e, j=j, h=h, p=p, pt=pt: e.activation(out=pt[:, j, :], in_=p[:, j * 128:(j + 1) * 128], func=AF.Exp,
                                                                               bias=BIAS[:, kt, h:h + 1], scale=1.0), [p, BIAS], [pt])
                for j in range(4):
                    h = hq * 4 + j
                    S.op("pe", lambda e, j=j, h=h, pt=pt: e.matmul(o_region(h), lhsT=pt[:, j, :], rhs=vc_[:, h, 0:129], start=False, stop=(kt == nk - 1),
                                                                     skip_group_check=True), [pt, vc_], [OB[h // 3]])

        stage_a(0)
        for kt in range(nk):
            if kt + 1 < nk:
                stage_a(kt + 1)
            stage_b(kt)
        attn_finish(t, 8)

    def xattn(sam, t):
        for ob in OB[0:2]:
            S.op("dve", lambda e, ob=ob: e.memset(ob[:, :], 0.0), [], [ob])
        if not sam:
            for mt_ in range(2):
                p = ps()
                pt = PT[mt_ % 4]
                for h in range(4):
                    S.op("pe", lambda e, h=h: e.matmul(p[:, h * 128:(h + 1) * 128], lhsT=MKT[:, h, mt_ * 128:(mt_ + 1) * 128],
                                                        rhs=QTv[:, h, t * 128:(t + 1) * 128], start=True, stop=True), [MKT, QT], [p])
                S.op("act", lambda e: e.activation(out=pt[:].rearrange("p a b -> p (a b)"), in_=p[:, :], func=AF.Exp), [p], [pt])
                for h in range(4):
                    S.op("pe", lambda e, h=h: e.matmul(o_region(h), lhsT=pt[:, h, :], rhs=MVA[:, mt_, h, 0:129], start=False, stop=(mt_ == 1),
                                                        skip_group_check=True), [pt, MVA], [OB[h // 3]])
        else:
            sample_xattn()
        attn_finish(t, 4)

    IDX = S.sb([128, 16 * NPG], I32, "IDX")
    IDXF = STG2
    if SAM:
        S.dma("sp", IDX[:], ptab.partition_broadcast(128), IDX, True)
        S.op("dve", lambda e: e.tensor_copy(out=IDXF[:, 0:16 * NPG], in_=IDX[:]), [IDX], [IDXF])
        S.op("dve", lambda e: e.tensor_scalar(out=IDXF[:, 0:16 * NPG], in0=IDXF[:, 0:16 * NPG], scalar1=128.0, scalar2=PIDX, op0=ALU.mult, op1=ALU.add), [IDXF, CST], [IDXF])
        S.op("dve", lambda e: e.tensor_copy(out=IDX[:], in_=IDXF[:, 0:16 * NPG]), [IDXF], [IDX])

    U32 = mybir.dt.uint32
    TMPS = [S.sb([128, 64], F32, f"TMPS{i}") for i in range(2)]
    LFP = [S.sb([128, 16, 8], F32, f"LFP{i}") for i in range(2)]
    CPX = S.sb([128, 17, 8], F32, "CPX")
    TOTP = S.sb([128, 16, 8], F32, "TOTP")
    BIASPS = [S.sb([128, 16, 8], F32, f"BIASP{i}") for i in range(2)]
    NEWTOT = S.sb([128, 16, 8], F32, "NEWTOT")
    LFB = S.sb([128, 16, 16], F32, "LFB")
    CDS = S.sb([128, 16, 16], F32, "CDS")
    BIASN = S.sb([128, 24], F32, "BIASN")
    CTPB = [S.sb([128, 2, 128], BF16, f"CTPB{i}") for i in range(2)]
    BTMB = [S.sb([128, 2, 128], BF16, f"BTMB{i}") for i in range(2)]

    def sample_ssd(acol, eac):
        da = SA[:, 0:16]
        S.op("dve", lambda e: e.tensor_tensor(out=LFB[:], in0=da.unsqueeze(1).to_broadcast([128, 16, 16]),
                                              in1=BDSEL.unsqueeze(2).to_broadcast([128, 16, 16]), op=ALU.mult), [SA, CST], [LFB])
        pc = ps()
        S.op("pe", lambda e: e.matmul(pc[:, 0:256], lhsT=ONES, rhs=LFB[:].rearrange("p a b -> p (a b)"), start=True, stop=True), [CST, LFB], [pc])
        S.op("act", lambda e: e.activation(out=CDS[:].rearrange("p a b -> p (a b)"), in_=pc[:, 0:256], func=AF.Exp), [pc], [CDS])
        H0L = [SCR[1], SCR[4]]
        H0T = [SCR[6], SCR[10]]
        HBS = [(HB, HB[:]), (H, H[:].bitcast(BF16)[:, 0:1024])]
        for b in range(16):
            h0l, h0t = H0L[b % 2], H0T[b % 2]
            hbb, hbv = HBS[b % 2]
            S.dma("sp", h0l[:].rearrange("p (c n) -> p c n", c=8), st_ssm[b].rearrange("(c p) n -> p c n", p=128), h0l, True)
            tr_f32(lambda j: h0l[:, j * 128:(j + 1) * 128], 8,
                   lambda b0, n, pv: S.op("act", lambda e: e.activation(out=h0t[:, b0 * 128:(b0 + n) * 128], in_=pv, func=AF.Copy), [curp[0]], [h0t]), [h0l])
            S.op("dve", lambda e: e.tensor_copy(out=hbv, in_=h0t[:]), [h0t], [hbb])
            ctp, btm = CTPB[b % 2], BTMB[b % 2]
            S.op("pool", lambda e: e.memset(ctp[:], 0.0), [], [ctp])
            S.op("pool", lambda e, b=b: e.tensor_copy(out=ctp[:, :, b * 8:(b + 1) * 8], in_=BCT[:, 2:4, b * 8:(b + 1) * 8]), [BCT], [ctp])
            S.op("dve", lambda e, b=b: e.tensor_scalar(out=btm[:].rearrange("p a b -> p (a b)"), in0=BTM[:].rearrange("p a b -> p (a b)"),
                                                         scalar1=BDSEL[:, b:b + 1], scalar2=None, op0=ALU.mult), [BTM, CST], [btm])
            for g in range(2):
                S.op("pe", lambda e, g=g, b=b: e.matmul(PS[6 + g][:, :], lhsT=ctp[:, g, :], rhs=hbv[:, g * 512:(g + 1) * 512], start=(b == 0), stop=(b == 15)),
                     [ctp, hbb], [PS[6 + g]])
            pS = [ps(), ps()]
            for g in range(2):
                S.op("pe", lambda e, g=g: e.matmul(pS[g][:, :], lhsT=btm[:, g, :], rhs=XDTE[:, g * 512:(g + 1) * 512], start=True, stop=True), [btm, XDTE], [pS[g]])
            h3 = h0t[:].rearrange("p (h d) -> p h d", h=16)
            S.op("dve", lambda e, b=b: e.tensor_tensor(out=h3, in0=h3, in1=CDS[:, b, :].unsqueeze(2).to_broadcast([128, 16, 64]), op=ALU.mult), [h0t, CDS], [h0t])
            for g in range(2):
                S.op("dve", lambda e, g=g: e.tensor_tensor(out=h0t[:, g * 512:(g + 1) * 512], in0=pS[g][:, :], in1=h0t[:, g * 512:(g + 1) * 512], op=ALU.add),
                     [pS[g], h0t], [h0t])
            state_out(ssm_s[b].rearrange("(c p) n -> p c n", p=128), h0t, h0t)
        ssd_finish_y(0, eac)

    def sample_attn():
        lf = SMF[0][:, 24:32]
        for ob in OB:
            S.op("dve", lambda e, ob=ob: e.memset(ob[:, :], 0.0), [], [ob])
        S.op("dve", lambda e: e.tensor_tensor(out=LFB[:, :, 0:8], in0=lf.unsqueeze(1).to_broadcast([128, 16, 8]),
                                              in1=BDSEL.unsqueeze(2).to_broadcast([128, 16, 8]), op=ALU.mult), [SMF[0], CST], [LFB])
        p = ps()
        S.op("pe", lambda e: e.matmul(p[:, 0:128].rearrange("p (a b) -> p a b", a=16), lhsT=ONES, rhs=LFB[:, :, 0:8], start=True, stop=True), [CST, LFB], [p])
        S.op("pe", lambda e: e.matmul(p[:, 128:136], lhsT=TRIBD, rhs=lf, start=True, stop=True), [CST, SMF[0]], [p])
        S.op("dve", lambda e: e.tensor_copy(out=NEWTOT[:].rearrange("p a b -> p (a b)"), in_=p[:, 0:128]), [p], [NEWTOT])
        S.op("dve", lambda e: e.tensor_copy(out=BIASN[:, 0:8], in_=p[:, 128:136]), [p], [BIASN])
        S.op("pe", lambda e: e.matmul(p[:, 136:144], lhsT=SELS, rhs=BIASN[:, 0:8], start=True, stop=True), [CST, BIASN], [p])
        S.op("dve", lambda e: e.tensor_tensor(out=BIASN[:, 8:16], in0=p[:, 136:144], in1=BIASN[:, 0:8], op=ALU.subtract), [p, BIASN], [BIASN])
        for hq in range(2):
            p = ps()
            pt = PT[hq]
            for j in range(4):
                h = hq * 4 + j
                S.op("pe", lambda e, j=j, h=h: e.matmul(p[:, j * 128:(j + 1) * 128], lhsT=KTGv[:, h, 0:128], rhs=QTv[:, h, 0:128], start=True, stop=False), [KTG, QT], [p])
                S.op("pe", lambda e, j=j: e.matmul(p[:, j * 128:(j + 1) * 128], lhsT=IDB[:], rhs=MNEG[:, 1, :], start=False, stop=True), [IDB, MNEG], [p])
            for j in range(4):
                h = hq * 4 + j
                S.op("act", lambda e, j=j, h=h: e.activation(out=pt[:, j, :], in_=p[:, j * 128:(j + 1) * 128], func=AF.Exp, bias=BIASN[:, 8 + h:9 + h], scale=1.0),
                     [p, BIASN], [pt])
            for j in range(4):
                h = hq * 4 + j
                S.op("pe", lambda e, j=j, h=h: e.matmul(o_region(h), lhsT=pt[:, j, :], rhs=VAG[:, 0, h, 0:129], start=False, stop=False, skip_group_check=True),
                     [pt, VAG], [OB[h // 3]])
        NS = 4
        KVS = WA
        KVv = [k_[:].bitcast(F32).rearrange("p a b -> p (a b)") for k_ in KVS]
        KTP = [XS[0], XS[1], SCR[4], SCR[0]]
        VPB = [(VC[0], VC[0][:]), (VC[1], VC[1][:]), (MVA, MVA[:].rearrange("p a b c -> p (a b) c")),
               (SCR[1], SCR[1][:].bitcast(BF16)[:, 0:1040].rearrange("p (h c) -> p h c", h=8))]
        S.op("dve", lambda e: e.memset(SCR[1][:].bitcast(BF16)[:, 0:1040], 1.0), [], [SCR[1]])
        PTZ = [(PT[0], PT[1]), (PT[2], PT[3])]
        pages = [(b, j) for b in range(16) for j in range(NPG)]
        NP_ = len(pages)
        lastb = [None, None]
        ktvs = {}

        def seq_setup(b):
            lfp = LFP[b % 2]
            for j in range(NPG):
                col = b * NPG + j
                S.dma("pool", lfp[:, j, :], pool_lf[:, :], lfp, True, indirect=bass.IndirectOffsetOnAxis(ap=IDX[:, col:col + 1].bitcast(U32), axis=0), reads=[IDX],
                      disjoint=(j > 0))
            p = ps()
            lfp2 = lfp[:, 0:NPG, :]
            S.op("pe", lambda e: e.matmul(p[:, 0:NPG * 8].rearrange("p (a b) -> p a b", a=NPG), lhsT=ONES, rhs=lfp2, start=True, stop=True), [CST, lfp], [p])
            S.op("pe", lambda e: e.matmul(p[:, 128:128 + NPG * 8].rearrange("p (a b) -> p a b", a=NPG), lhsT=TRI, rhs=lfp2, start=True, stop=True), [CST, lfp], [p])
            S.op("dve", lambda e: e.tensor_copy(out=TOTP[:, 0:NPG, :].rearrange("p a b -> p (a b)"), in_=p[:, 0:NPG * 8]), [p], [TOTP])
            S.op("dve", lambda e: e.memset(CPX[:, 0, :], 0.0), [], [CPX])
            for j in range(NPG):
                S.op("dve", lambda e, j=j: e.tensor_tensor(out=CPX[:, j + 1, :], in0=CPX[:, j, :], in1=TOTP[:, j, :], op=ALU.add), [CPX, TOTP], [CPX])
            bp = BIASPS[b % 2]
            S.op("dve", lambda e, b=b: e.tensor_tensor(out=BIASN[:, 16:24], in0=CPX[:, NPG, :], in1=NEWTOT[:, b, :], op=ALU.add), [CPX, NEWTOT], [BIASN])
            S.op("dve", lambda e: e.tensor_tensor(out=bp[:, 0:NPG, :], in0=BIASN[:, 16:24].unsqueeze(1).to_broadcast([128, NPG, 8]), in1=CPX[:, 0:NPG, :],
                                                  op=ALU.subtract), [BIASN, CPX], [bp])
            S.op("dve", lambda e: e.tensor_tensor(out=bp[:, 0:NPG, :], in0=bp[:, 0:NPG, :], in1=p[:, 128:128 + NPG * 8].rearrange("p (a b) -> p a b", a=NPG),
                                                  op=ALU.subtract), [bp, p], [bp])

        def stage_T(i):
            b, j = pages[i]
            if j == 0:
                seq_setup(b)
            col = b * NPG + j
            kv, kvv, ktp = KVS[i % NS], KVv[i % NS], KTP[i % NS]
            vb, vv = VPB[i % NS]
            off = bass.IndirectOffsetOnAxis(ap=IDX[:, col:col + 1].bitcast(U32), axis=0)
            S.dma("pool", kvv, pool_kv[:, :], kv, True, indirect=off, reads=[IDX])
            ktv = ktp[:].bitcast(BF16)[:, 0:1024].rearrange("p (h c) -> p h c", h=8)
            ktvs[i] = ktv
            tr_f32(lambda jj: kvv[:, jj * 128:(jj + 1) * 128], 8,
                   lambda b0, n, pv: S.op("act", lambda e: e.activation(out=ktv[:, b0:b0 + n, :], in_=pv.rearrange("p (j c) -> p j c", j=n), func=AF.Copy),
                                          [curp[0]], [ktp]), [kv])
            S.op("dve", lambda e: e.tensor_copy(out=vv[:, :, 0:128], in_=kvv[:, 1024:2048].rearrange("p (h d) -> p h d", h=8)), [kv], [vb])

        def stage_Q(i):
            b, j = pages[i]
            ktp, ktv = KTP[i % NS], ktvs[i]
            ptz = PTZ[i % 2]
            tmp = TMPS[i % 2]
            if lastb[i % 2] is not None and lastb[i % 2] != b:
                ob = lastb[i % 2]
                for z_ in ptz:
                    S.op("dve", lambda e, z_=z_, ob=ob: e.memset(z_[:, :, ob * 8:(ob + 1) * 8], 0.0), [], [z_])
            lastb[i % 2] = b
            p = ps()
            for h in range(8):
                S.op("pe", lambda e, h=h, b=b: e.matmul(p[:, h * 8:(h + 1) * 8], lhsT=ktv[:, h, :], rhs=QTv[:, h, b * 8:(b + 1) * 8], start=True, stop=True),
                     [ktp, QT], [p])
            S.op("dve", lambda e, j=j, b=b: e.tensor_tensor(out=tmp[:].rearrange("p (h q) -> p h q", h=8), in0=p[:, 0:64].rearrange("p (h q) -> p h q", h=8),
                                                             in1=BIASPS[b % 2][:, j, :].unsqueeze(2).to_broadcast([128, 8, 8]), op=ALU.add), [p, BIASPS[b % 2]], [tmp])
            for hh in range(2):
                S.op("act", lambda e, hh=hh, b=b: e.activation(out=ptz[hh][:, :, b * 8:(b + 1) * 8], in_=tmp[:, hh * 32:(hh + 1) * 32].rearrange("p (h q) -> p h q", h=4),
                                                                 func=AF.Exp), [tmp], [ptz[hh]])

        def stage_P(i):
            ptz = PTZ[i % 2]
            vb, vv = VPB[i % NS]
            for h in range(8):
                S.op("pe", lambda e, h=h: e.matmul(o_region(h), lhsT=ptz[h // 4][:, h % 4, :], rhs=vv[:, h, 0:129], start=False, stop=False, skip_group_check=True),
                     [ptz[h // 4], vb], [OB[h // 3]])

        for pt in PT:
            S.op("dve", lambda e, pt=pt: e.memset(pt[:], 0.0), [], [pt])
        for i in range(NP_ + 3):
            if i < NP_:
                stage_T(i)
            if 0 <= i - 2 < NP_:
                stage_Q(i - 2)
            if 0 <= i - 3 < NP_:
                stage_P(i - 3)
        attn_finish(0, 8)

    def sample_xattn():
        MKL = [SCR[0], SCR[1]]
        MVL = [SCR[3], SCR[4]]
        MKTB = [XS[0], XS[1]]
        MVAB = [(MVA, MVA[:]), (VC[0], VC[0][:].rearrange("p (a b) c -> p a b c", a=2))]
        for b in range(16):
            mkl, mvl, mktb = MKL[b % 2], MVL[b % 2], MKTB[b % 2]
            mvb, mvv = MVAB[b % 2]
            ptz = (PT[0], PT[1]) if b % 2 == 0 else (PT[2], PT[3])
            S.dma("sp", mkl[:].rearrange("p (t c) -> p t c", t=2), cmem_k[b].rearrange("(t p) c -> p t c", p=128), mkl, True)
            S.dma("sp", mvl[:].rearrange("p (t c) -> p t c", t=2), cmem_v[b].rearrange("(t p) c -> p t c", p=128), mvl, True)
            mkv = mktb[:].bitcast(BF16)[:, 0:1024].rearrange("p (h c) -> p h c", h=4)
            for mt_ in range(2):
                tr_f32(lambda jj, mt_=mt_: mkl[:, mt_ * 512 + jj * 128:mt_ * 512 + (jj + 1) * 128], 4,
                       lambda b0, n, pv, mt_=mt_: S.op("act", lambda e: e.activation(out=mkv[:, b0:b0 + n, mt_ * 128:(mt_ + 1) * 128],
                                                                                      in_=pv.rearrange("p (j c) -> p j c", j=n), func=AF.Copy), [curp[0]], [mktb]), [mkl])
            S.op("pool", lambda e: e.tensor_copy(out=mvv[:, :, :, 0:128], in_=mvl[:].rearrange("p (t h d) -> p t h d", t=2, h=4)), [mvl], [mvb])
            for z_ in ptz:
                S.op("pool", lambda e, z_=z_: e.memset(z_[:], 0.0), [], [z_])
            p = ps()
            for mt_ in range(2):
                for h in range(4):
                    S.op("pe", lambda e, h=h, mt_=mt_, b=b: e.matmul(p[:, mt_ * 32 + h * 8:mt_ * 32 + (h + 1) * 8], lhsT=mkv[:, h, mt_ * 128:(mt_ + 1) * 128],
                                                                       rhs=QTv[:, h, b * 8:(b + 1) * 8], start=True, stop=True), [mktb, QT], [p])
            for mt_ in range(2):
                S.op("act", lambda e, mt_=mt_, b=b: e.activation(out=ptz[mt_][:, :, b * 8:(b + 1) * 8], in_=p[:, mt_ * 32:(mt_ + 1) * 32].rearrange("p (h q) -> p h q", h=4),
                                                                   func=AF.Exp), [p], [ptz[mt_]])
            for mt_ in range(2):
                for h in range(4):
                    S.op("pe", lambda e, h=h, mt_=mt_: e.matmul(o_region(h), lhsT=ptz[mt_][:, h, :], rhs=mvv[:, mt_, h, 0:129], start=False, stop=False,
                                                                  skip_group_check=True), [ptz[mt_], mvb], [OB[h // 3]])

    def memkv():
        for t in range(2):
            for c in range(4):
                S.dma("sp", XG[t][c][:], mem_in[t * 128:(t + 1) * 128, c * 512:(c + 1) * 512], XG[t][c], True)
        norm_T(2, w["mem_norm"])
        proj_tm(2, wkv_v, 0, 512, to_stg)
        for t in range(2):
            headnorm(t, 1, 4, GX)
            S.dma("sp", memk_o[t * 128:(t + 1) * 128, :], STG[t][:, 0:512], STG[t], False)
            stg_T(t, 4, MKT, MKT)
        proj_tm(2, wkv_v, 512, 512, to_stg)
        for t in range(2):
            S.dma("sp", memv_o[t * 128:(t + 1) * 128, :], STG[t][:, 0:512], STG[t], False)
            S.op("pool", lambda e, t=t: e.tensor_copy(out=MVA[:, t, :, 0:128], in_=STG[t][:, 0:512].rearrange("p (h d) -> p h d", h=4)), [STG[t]], [MVA])

    memkv()
    for bgi in range((NPRE + 3) // 4):
        big_group("pre", bgi)
    nbo = (NOWN + 3) // 4
    merge = SAM and (NOWN - (nbo - 1) * 4) == 4
    for bgi in range(nbo):
        big_group("own", bgi, with_sam=(merge and bgi == nbo - 1))
    if SAM and not merge:
        big_group("sam", 0)
    S.finish()
    return nc


def make_consts():
    c = np.zeros((128, 1536), np.float32)
    k = np.arange(128)
    le = k[:, None] <= k[None, :]
    same = (k[:, None] // 8) == (k[None, :] // 8)
    c[:, 0:128] = np.eye(128)
    c[:, 128:256] = le
    c[:, 256:384] = le & same
    c[:, 384:512] = 1.0
    c[:, 512:640] = np.where(le, 0.0, NEG)
    c[:, 640:768] = np.where(le & same, 0.0, NEG)
    c[:, 768:896] = (k[:, None] == 127)
    c[:, 896:1024] = (k[:, None] == (k[None, :] // 8) * 8 + 7)
    c[:, 1024:1040] = (k[:, None] // 8) == np.arange(16)[None, :]
    c[:, 1040] = EPS
    c[:, 1041] = 1.0
    c[:, 1042] = k
    c2 = np.zeros((128, 16, 128), np.float32)
    c2[:] = ((k[None, :] // 8) == np.arange(16)[:, None])[None]
    return c, c2.reshape(128, 2048)


def core_inputs(inp, c, cfg, cst, cst2):
    s, h = c // 2, c % 2
    xp = inp["x_prompt"]
    xs = inp["x_sample"]
    npre, nown = cfg["npre"] * 128, cfg["nown"] * 128
    f32 = lambda a: np.ascontiguousarray(np.asarray(a, np.float32))
    flg = np.zeros((128, 4), np.float32)
    flg[:, 0] = 1.0 if h == 1 else 0.0
    flg[:, 1] = 0.0 if h == 1 else NEG
    m = {"x_own": f32(xp[s, h * nown:(h + 1) * nown]),
         "x_pre": f32(xp[s, 0:npre]) if h == 1 else np.zeros((npre, D), np.float32),
         "x_sam": f32(xs[c * 16:(c + 1) * 16]).reshape(128, D),
         "mem_in": f32(inp["mem_prompt"][s]),
         "cst": cst, "cst2": cst2, "flg": flg,
         "cwb": np.ascontiguousarray(np.concatenate([np.asarray(inp["conv_w"], np.float32)[0], np.asarray(inp["conv_b"], np.float32)], axis=0)),
         "pool_kv": inp["_pool_kv"], "pool_lf": inp["_pool_lf"],
         "ptab": np.ascontiguousarray(np.asarray(inp["page_table"], np.int32)[c * 16:(c + 1) * 16]).reshape(-1),
         "cmem_k": f32(inp["cache_mem_k"][0, c * 16:(c + 1) * 16]).reshape(16, 256, 512),
         "cmem_v": f32(inp["cache_mem_v"][0, c * 16:(c + 1) * 16]).reshape(16, 256, 512),
         "st_ssm": f32(inp["state_ssm"][0, c * 16:(c + 1) * 16]).reshape(16, 1024, 128),
         "st_conv": f32(inp["state_conv"][0, c * 16:(c + 1) * 16]).reshape(48, 1536)}
    for n in _WNAMES:
        m[n] = f32(inp[n][0])
    return m


def kernel(**inp):
    inp = dict(inp)
    npool = inp["cache_fox_k"].shape[1]
    inp["_pool_kv"] = np.concatenate([np.asarray(inp["cache_fox_k"], np.float32)[0].reshape(npool * 128, 1024),
                                      np.asarray(inp["cache_fox_v"], np.float32)[0].reshape(npool * 128, 1024)], axis=1)
    inp["_pool_lf"] = np.ascontiguousarray(np.asarray(inp["cache_fox_logf"], np.float32)[0]).reshape(npool * 128, 8)
    cfg = {"npre": 8, "nown": 8, "npages": inp["page_table"].shape[1], "pool_rows": npool * 128}
    nc = build(cfg)
    cst, cst2 = make_consts()
    in_maps = [core_inputs(inp, c, cfg, cst, cst2) for c in range(NCORES)]
    res = run_bass_kernel_spmd(nc, in_maps, core_ids=list(range(NCORES))).results
    B, L = 4, 2048
    z = lambda *s: np.zeros(s, np.float32)
    y_p, pk, pv, plf = z(B, L, D), z(1, B, L, 8, 128), z(1, B, L, 8, 128), z(1, B, L, 8)
    pssm, pconv, pmk, pmv = z(1, B, 16, 64, 128), z(1, B, 3, 1536), z(1, B, 256, 4, 128), z(1, B, 256, 4, 128)
    y_s, sk, sv, slf = z(128, 8, D), z(1, 128, 8, 8, 128), z(1, 128, 8, 8, 128), z(1, 128, 8, 8)
    sssm, sconv = z(1, 128, 16, 64, 128), z(1, 128, 3, 1536)
    for c in range(NCORES):
        s, h = c // 2, c % 2
        r = res[c]
        sl = slice(h * 1024, (h + 1) * 1024)
        y_p[s, sl] = r["y_own"]
        pk[0, s, sl] = r["k_own"].reshape(1024, 8, 128)
        pv[0, s, sl] = r["v_own"].reshape(1024, 8, 128)
        plf[0, s, sl] = r["lf_own"]
        if h == 0:
            pmk[0, s] = r["memk_o"].reshape(256, 4, 128)
            pmv[0, s] = r["memv_o"].reshape(256, 4, 128)
        else:
            pssm[0, s] = r["ssm_p"].reshape(16, 64, 128)
            pconv[0, s] = r["conv_p"]
        cs = slice(c * 16, (c + 1) * 16)
        y_s[cs] = r["y_sam"].reshape(16, 8, D)
        sk[0, cs] = r["k_sam"].reshape(16, 8, 8, 128)
        sv[0, cs] = r["v_sam"].reshape(16, 8, 8, 128)
        slf[0, cs] = r["lf_sam"].reshape(16, 8, 8)
        sssm[0, cs] = r["ssm_s"].reshape(16, 16, 64, 128)
        sconv[0, cs] = r["conv_s"].reshape(16, 3, 1536)
    return (y_p, y_s, pk, pv, plf, pssm, pconv, pmk, pmv, sk, sv, slf, sssm, sconv)
```
